# Optimizing a Trainium2 kernel written in Bass

```python
import math
import jax, jax.numpy as jnp
from jax import lax
import numpy as np

D_MODEL = 1024
BATCH = 8
SEQ = 4096
DEPTH = 1

MIX_WIDTH = D_MODEL
DA_WIDTH = MIX_WIDTH // 2
ML_WIDTH = MIX_WIDTH - DA_WIDTH
DA_HEADS = 4
DA_VDIM = DA_WIDTH // DA_HEADS
DA_QKDIM = DA_VDIM // 2
DA_QK_COLS = DA_HEADS * 2 * DA_QKDIM
ML_HEADS = 4
ML_DIM = ML_WIDTH // ML_HEADS
ML_CHUNK = 64
ML_CONV = 4
FFN_CONV = 3
D_FF = 2816
Q_BLOCK = 128
NORM_EPS = 1e-6
IN_SIZES = (DA_QK_COLS, DA_QK_COLS, DA_WIDTH, 2 * ML_WIDTH, ML_WIDTH, ML_WIDTH, ML_HEADS, ML_HEADS)
IN_COLS = DA_QK_COLS * 2 + DA_WIDTH + 4 * ML_WIDTH + 2 * ML_HEADS

kernel_name = "hybrid_diffattn_mlstm_convglu"


def rmsnorm(x, w):
    x32 = x.astype(jnp.float32)
    y = x32 * lax.rsqrt(jnp.mean(x32 * x32, axis=-1, keepdims=True) + NORM_EPS)
    return (y * w.astype(jnp.float32)).astype(x.dtype)


def causal_dwconv(x, w, b):
    width, ch = w.shape
    y = lax.conv_general_dilated(
        x, w[:, None, :].astype(x.dtype), window_strides=(1,),
        padding=[(width - 1, 0)], dimension_numbers=('NWC', 'WIO', 'NWC'),
        feature_group_count=ch)
    return y + b.astype(x.dtype)


def alibi_slopes(n_heads):
    return 2.0 ** (-8.0 * jnp.arange(1, n_heads + 1, dtype=jnp.float32) / n_heads)


def split_projection(proj):
    idx = np.cumsum(IN_SIZES)[:-1].tolist()
    return jnp.split(proj, idx, axis=-1)


def diff_attention(q, k, v, lam, slopes):
    B, S, H = q.shape[0], q.shape[1], q.shape[2]
    nb = S // Q_BLOCK
    scale = DA_QKDIM ** -0.5
    q_blocks = q.reshape(B, nb, Q_BLOCK, H, 2, DA_QKDIM).swapaxes(0, 1)
    kpos = jnp.arange(S)

    def one_block(args):
        qb, blk = args
        qpos = blk * Q_BLOCK + jnp.arange(Q_BLOCK)
        s = jnp.einsum('bqhcd,bshcd->bhcqs', qb, k, preferred_element_type=jnp.float32) * scale
        dist = (qpos[:, None] - kpos[None, :]).astype(jnp.float32)
        bias = -slopes[:, None, None, None] * dist
        s = jnp.where(dist >= 0, s + bias, -jnp.inf)
        p = jax.nn.softmax(s, axis=-1)
        a = p[:, :, 0] - lam * p[:, :, 1]
        return jnp.einsum('bhqs,bshd->bqhd', a.astype(v.dtype), v)

    out = lax.map(one_block, (q_blocks, jnp.arange(nb)))
    return out.swapaxes(0, 1).reshape(B, S, H, v.shape[-1])


def mlstm_chunkwise(q, k, v, i_pre, logf):
    B, S, H, d = q.shape
    L = ML_CHUNK
    nc = S // L

    def to_chunks(t):
        t = t.astype(jnp.float32).reshape((B, nc, L) + t.shape[2:])
        return jnp.moveaxis(t.swapaxes(0, 1), 3, 2)

    qc, kc, vc, ic, fc = (to_chunks(t) for t in (q, k, v, i_pre, logf))
    causal = jnp.tril(jnp.ones((L, L), dtype=bool))

    def step(carry, xs):
        C, n, m = carry
        qb, kb, vb, ib, fb = xs
        b = jnp.cumsum(fb, axis=-1)
        log_d = b[..., :, None] - b[..., None, :] + ib[..., None, :]
        log_d = jnp.where(causal, log_d, -jnp.inf)
        log_prev = b + m[..., None]
        m_t = jnp.maximum(log_prev, jnp.max(log_d, axis=-1))
        d_mat = jnp.exp(log_d - m_t[..., None])
        w_prev = jnp.exp(log_prev - m_t)
        sc = jnp.einsum('bhld,bhsd->bhls', qb, kb) * d_mat
        num = jnp.einsum('bhls,bhsd->bhld', sc, vb) + w_prev[..., None] * jnp.einsum('bhlk,bhkv->bhlv', qb, C)
        den = jnp.sum(sc, axis=-1) + w_prev * jnp.einsum('bhlk,bhk->bhl', qb, n)
        h = num / jnp.maximum(jnp.abs(den), jnp.exp(-m_t))[..., None]
        w_state = d_mat[..., -1, :]
        w_last = w_prev[..., -1]
        C_new = w_last[..., None, None] * C + jnp.einsum('bhs,bhsk,bhsv->bhkv', w_state, kb, vb)
        n_new = w_last[..., None] * n + jnp.einsum('bhs,bhsk->bhk', w_state, kb)
        return (C_new, n_new, m_t[..., -1]), h

    init = (jnp.zeros((B, H, d, d), jnp.float32), jnp.zeros((B, H, d), jnp.float32),
            jnp.zeros((B, H), jnp.float32))
    _, hs = lax.scan(step, init, (qc, kc, vc, ic, fc))
    hs = jnp.moveaxis(hs, 2, 3).swapaxes(0, 1).reshape(B, S, H, d)
    return hs.astype(q.dtype)


def setup_inputs(seed: int = 0) -> dict:
    key = jax.random.key(seed)
    ks = jax.random.split(key, 20)
    f32 = jnp.float32

    def nrm(k, shape, s):
        return jax.random.normal(k, shape, f32) * s

    return {
        "x": nrm(ks[0], (BATCH, SEQ, D_MODEL), 1.0),
        "attn_norm_w": 1.0 + nrm(ks[1], (DEPTH, D_MODEL), 0.02),
        "w_in": nrm(ks[2], (DEPTH, D_MODEL, IN_COLS), D_MODEL ** -0.5),
        "mlstm_conv_w": nrm(ks[3], (DEPTH, ML_CONV, 2 * ML_WIDTH), ML_CONV ** -0.5),
        "mlstm_conv_b": nrm(ks[4], (DEPTH, 2 * ML_WIDTH), 0.01),
        "mlstm_igate_b": nrm(ks[5], (DEPTH, ML_HEADS), 0.1),
        "mlstm_fgate_b": jnp.linspace(3.0, 6.0, ML_HEADS, dtype=f32)[None, :] + nrm(ks[6], (DEPTH, ML_HEADS), 0.01),
        "lambda_q1": nrm(ks[7], (DEPTH, DA_QKDIM), 0.1),
        "lambda_k1": nrm(ks[8], (DEPTH, DA_QKDIM), 0.1),
        "lambda_q2": nrm(ks[9], (DEPTH, DA_QKDIM), 0.1),
        "lambda_k2": nrm(ks[10], (DEPTH, DA_QKDIM), 0.1),
        "diff_norm_w": 1.0 + nrm(ks[11], (DEPTH, DA_VDIM), 0.02),
        "mlstm_norm_w": 1.0 + nrm(ks[12], (DEPTH, ML_WIDTH), 0.02),
        "w_out": nrm(ks[13], (DEPTH, MIX_WIDTH, D_MODEL), MIX_WIDTH ** -0.5),
        "ffn_norm_w": 1.0 + nrm(ks[14], (DEPTH, D_MODEL), 0.02),
        "w_up": nrm(ks[15], (DEPTH, D_MODEL, 2 * D_FF), D_MODEL ** -0.5),
        "ffn_conv_w": nrm(ks[16], (DEPTH, FFN_CONV, 2 * D_FF), FFN_CONV ** -0.5),
        "ffn_conv_b": nrm(ks[17], (DEPTH, 2 * D_FF), 0.01),
        "w_down": nrm(ks[18], (DEPTH, D_FF, D_MODEL), D_FF ** -0.5),
        "final_norm_w": 1.0 + nrm(ks[19], (D_MODEL,), 0.02),
    }


def reference(x, attn_norm_w, w_in, mlstm_conv_w, mlstm_conv_b, mlstm_igate_b, mlstm_fgate_b,
              lambda_q1, lambda_k1, lambda_q2, lambda_k2, diff_norm_w, mlstm_norm_w, w_out,
              ffn_norm_w, w_up, ffn_conv_w, ffn_conv_b, w_down, final_norm_w):
    B, S, _ = x.shape
    slopes = alibi_slopes(DA_HEADS)
    for l in range(DEPTH):
        lam_init = 0.8 - 0.6 * math.exp(-0.3 * l)
        hn = rmsnorm(x, attn_norm_w[l])
        proj = hn @ w_in[l]
        da_q, da_k, da_v, ml_qk, ml_v, ml_o, ml_i, ml_f = split_projection(proj)

        lam = (jnp.exp(jnp.sum(lambda_q1[l].astype(jnp.float32) * lambda_k1[l].astype(jnp.float32)))
               - jnp.exp(jnp.sum(lambda_q2[l].astype(jnp.float32) * lambda_k2[l].astype(jnp.float32)))
               + lam_init)
        da_out = diff_attention(da_q.reshape(B, S, DA_HEADS, 2, DA_QKDIM),
                                da_k.reshape(B, S, DA_HEADS, 2, DA_QKDIM),
                                da_v.reshape(B, S, DA_HEADS, DA_VDIM), lam, slopes)
        da_out = rmsnorm(da_out, diff_norm_w[l]) * (1.0 - lam_init)
        da_out = da_out.reshape(B, S, DA_WIDTH)

        qk = jax.nn.silu(causal_dwconv(ml_qk, mlstm_conv_w[l], mlstm_conv_b[l]))
        mq, mk = jnp.split(qk, 2, axis=-1)
        mq = mq.reshape(B, S, ML_HEADS, ML_DIM)
        mk = mk.reshape(B, S, ML_HEADS, ML_DIM) * (ML_DIM ** -0.5)
        mv = ml_v.reshape(B, S, ML_HEADS, ML_DIM)
        i_pre = ml_i.astype(jnp.float32) + mlstm_igate_b[l].astype(jnp.float32)
        logf = jax.nn.log_sigmoid(ml_f.astype(jnp.float32) + mlstm_fgate_b[l].astype(jnp.float32))
        h_tilde = mlstm_chunkwise(mq, mk, mv, i_pre, logf)
        h_tilde = rmsnorm(h_tilde, mlstm_norm_w[l].reshape(ML_HEADS, ML_DIM))
        ml_out = jax.nn.sigmoid(ml_o) * h_tilde.reshape(B, S, ML_WIDTH)

        mixed = jnp.concatenate([da_out, ml_out], axis=-1)
        x = x + mixed @ w_out[l]

        hn = rmsnorm(x, ffn_norm_w[l])
        u = causal_dwconv(hn @ w_up[l], ffn_conv_w[l], ffn_conv_b[l])
        gate, up = jnp.split(u, 2, axis=-1)
        x = x + (jax.nn.silu(gate) * up) @ w_down[l]
    return rmsnorm(x, final_norm_w)
```

```python
import math
from contextlib import ExitStack

import numpy as np
import concourse.bass as bass
import concourse.mybir as mybir
from concourse.bass_utils import run_bass_kernel_spmd

F32 = mybir.dt.float32
BF16 = mybir.dt.bfloat16
I32 = mybir.dt.int32
AF = mybir.ActivationFunctionType
ALU = mybir.AluOpType

S = 4096
D = 1024
NT = S // 128
BLK = 256
TPB = BLK // 128
NB = S // BLK
INC = 3592
DFF = 2816
NF = DFF // 128
EPS = 1e-6
LAM_INIT = 0.8 - 0.6 * math.exp(0.0)
SLOPES = [2.0 ** (-8.0 * (i + 1) / 4) for i in range(4)]
LN_SQRT_D = 0.5 * math.log(128.0)
CAP = 2000
NDMA = 24

DEBUG = {}


class Buf:
    __slots__ = ("name", "w", "r")

    def __init__(self, name):
        self.name = name
        self.w = None
        self.r = {}


class Op:
    __slots__ = ("fn", "deps", "sig", "dma", "sigidx")

    def __init__(self, fn, deps, dma):
        self.fn = fn
        self.deps = deps
        self.sig = False
        self.dma = dma
        self.sigidx = 0


class Prog:
    ENGS = ("pe", "act", "dve", "pool", "sp")

    def __init__(self, nc, es):
        self.nc = nc
        self.es = es
        self.ops = {e: [] for e in self.ENGS}
        self.dma_cnt = [0] * NDMA
        self.dma_rr = {e: 0 for e in self.ENGS}
        self.dsem = [es.enter_context(nc.semaphore(f"dsem{k}")) for k in range(NDMA)]
        self.esem = {e: [] for e in self.ENGS}
        self.sigtot = {e: 0 for e in self.ENGS}
        self.start = {e: 0 for e in self.ENGS}

    def op(self, eng, fn, reads=(), writes=(), dma=False):
        self.nrec = getattr(self, "nrec", 0) + 1
        if DEBUG.get("oplimit") and self.nrec > DEBUG["oplimit"]:
            return None
        deps = set()
        for b in reads:
            if b.w is not None:
                deps.add(b.w)
        for b in writes:
            if b.w is not None:
                deps.add(b.w)
            deps.update(b.r.values())
        deps = {d for d in deps if not (d[0] == "e" and d[2] < self.start[d[1]])}
        if eng == "pe":
            deps = {d for d in deps if not (d[0] == "e" and d[1] == "pe")}
        dmatok = None
        if dma:
            lo, n = (0, 8) if eng == "pool" else (8, NDMA - 8)
            k = lo + self.dma_rr[eng] % n
            self.dma_rr[eng] += 1
            self.dma_cnt[k] += 1
            val = 16 * self.dma_cnt[k]
            if self.dma_cnt[k] > 1:
                deps.add(("d", k, val - 16))
            dmatok = ("d", k, val)
        for d in deps:
            if d[0] == "e":
                self.ops[d[1]][d[2]].sig = True
        o = Op(fn, deps, dmatok)
        seq = len(self.ops[eng])
        self.ops[eng].append(o)
        tok = dmatok if dma else ("e", eng, seq)
        for b in reads:
            key = ("d", tok[1]) if dma else eng
            b.r[key] = tok
        for b in writes:
            b.w = tok
            b.r = {}
        return tok

    def wait_all(self, eng, toks):
        deps = set(toks)
        for d in deps:
            if d[0] == "e":
                self.ops[d[1]][d[2]].sig = True
        self.ops[eng].append(Op(None, deps, None))

    def emit(self):
        nc, es = self.nc, self.es
        for e in self.ENGS:
            c = self.sigtot[e]
            for o in self.ops[e][self.start[e]:]:
                if o.sig:
                    c += 1
                o.sigidx = c if o.sig else 0
            need = (c + CAP - 1) // CAP
            while len(self.esem[e]) < max(need, 1):
                self.esem[e].append(es.enter_context(nc.semaphore(f"es_{e}{len(self.esem[e])}")))
            self.sigtot[e] = c

        def resolve(d):
            if d[0] == "e":
                si = self.ops[d[1]][d[2]].sigidx
                assert si > 0, d
                return ("e", d[1], (si - 1) // CAP), self.esem[d[1]][(si - 1) // CAP], (si - 1) % CAP + 1
            return ("d", d[1]), self.dsem[d[1]], d[2]

        def run(eng, handle):
            waited = {}
            for o in self.ops[eng][self.start[eng]:]:
                for d in sorted(o.deps, key=str):
                    key, sem, val = resolve(d)
                    if waited.get(key, 0) >= val:
                        continue
                    handle.wait_ge(sem, val)
                    waited[key] = val
                if o.fn is None:
                    continue
                ins = o.fn(handle)
                if o.sig:
                    si = o.sigidx
                    ins.then_inc(self.esem[eng][(si - 1) // CAP], 1)
                if o.dma is not None:
                    ins.then_inc(self.dsem[o.dma[1]], 16)

        with nc.Block() as block:
            @block.tensor
            def _(h):
                run("pe", h)

            @block.scalar
            def _(h):
                run("act", h)

            @block.vector
            def _(h):
                run("dve", h)

            @block.gpsimd
            def _(h):
                run("pool", h)

            @block.sync
            def _(h):
                run("sp", h)
        for e in self.ENGS:
            self.start[e] = len(self.ops[e])


def interleave(A, M):
    out = []
    na, nm = len(A), len(M)
    if na == 0:
        return list(M)
    mi = 0
    for i, a in enumerate(A):
        out.append(a)
        tgt = ((i + 1) * nm) // na
        while mi < tgt:
            out.append(M[mi])
            mi += 1
    out.extend(M[mi:])
    return out


def build_program():
    nc = bass.Bass("TRN2", target_bir_lowering=False)
    dt = nc.dram_tensor
    x_d = dt("x", [S, D], F32, kind="ExternalInput").ap()
    w_in_d = dt("w_in", [D, INC], F32, kind="ExternalInput").ap()
    w_out_d = dt("w_out", [D, D], F32, kind="ExternalInput").ap()
    w_up_d = dt("w_up", [D, 2 * DFF], F32, kind="ExternalInput").ap()
    w_down_d = dt("w_down", [DFF, D], F32, kind="ExternalInput").ap()
    nrm_d = dt("nrm", [3, D], F32, kind="ExternalInput").ap()
    mlnw_d = dt("mlnw", [1, 512], F32, kind="ExternalInput").ap()
    dnw_d = dt("dnw", [1, 128], F32, kind="ExternalInput").ap()
    lamv_d = dt("lamv", [1, 256], F32, kind="ExternalInput").ap()
    mcw_d = dt("mcw", [128, 8 * 4], F32, kind="ExternalInput").ap()
    mcb_d = dt("mcb", [128, 8], F32, kind="ExternalInput").ap()
    gb_d = dt("gb", [4, 2], F32, kind="ExternalInput").ap()
    fcw_d = dt("fcw", [128, 44 * 3], F32, kind="ExternalInput").ap()
    fcb_d = dt("fcb", [128, 44], F32, kind="ExternalInput").ap()
    out_d = dt("out", [S, D], F32, kind="ExternalOutput").ap()
    x1_d = out_d

    with ExitStack() as es:
        P = Prog(nc, es)

        def sb(name, shape, dtype, ctx=es):
            return ctx.enter_context(nc.sbuf_tensor(name, shape, dtype))

        def ps(name, shape, dtype, ctx=es):
            return ctx.enter_context(nc.psum_tensor(name, shape, dtype))

        phase_sem = es.enter_context(nc.semaphore("phase"))

        ident = sb("ident", [128, 128], BF16)
        identf = sb("identf", [128, 128], F32)
        maskc = sb("maskc", [128, 128], BF16)
        mask2 = sb("mask2", [128, 128], F32)
        biasT = sb("biasT", [128, 4 * 32], F32)
        biasI = sb("biasI", [128, 32], I32)
        cst = sb("cst", [128, 16], F32)
        nrmb = sb("nrmb", [128, D], F32)
        B_const = Buf("const")

        NEGH = cst[:, 0:1]
        NLAM = cst[:, 2:3]

        def bc_rows(ap_row, n):
            return ap_row.partition_broadcast(128)

        def setup_consts():
            P.op("pool", lambda g: g.memset(cst[:], 0.0), writes=[B_const])
            P.op("pool", lambda g: g.memset(cst[:, 0:1], -0.5), writes=[B_const])
            for t, dtp in ((ident, BF16), (identf, F32)):
                P.op("pool", lambda g, t=t: g.memset(t[:], 1.0), writes=[B_const])
                P.op("pool", lambda g, t=t: g.affine_select(out=t[:], in_=t[:], pattern=[[1, 128]],
                                                            compare_op=ALU.is_ge, fill=0.0, base=0,
                                                            channel_multiplier=-1), reads=[B_const], writes=[B_const])
                P.op("pool", lambda g, t=t: g.affine_select(out=t[:], in_=t[:], pattern=[[-1, 128]],
                                                            compare_op=ALU.is_ge, fill=0.0, base=0,
                                                            channel_multiplier=1), reads=[B_const], writes=[B_const])
            for t in (maskc, mask2):
                P.op("pool", lambda g, t=t: g.memset(t[:], 1.0), writes=[B_const])
                P.op("pool", lambda g, t=t: g.affine_select(out=t[:], in_=t[:], pattern=[[1, 128]],
                                                            compare_op=ALU.is_ge, fill=0.0, base=0,
                                                            channel_multiplier=-1), reads=[B_const], writes=[B_const])
            P.op("pool", lambda g: g.memset(mask2[0:64, 64:128], 0.0), reads=[B_const], writes=[B_const])
            P.op("pool", lambda g: g.iota(biasI[:], pattern=[[-128, 32]], base=-127, channel_multiplier=1),
                 writes=[B_const])
            for h in range(4):
                P.op("dve", lambda v, h=h: v.tensor_scalar(out=biasT[:, h * 32:(h + 1) * 32], in0=biasI[:],
                                                          scalar1=float(SLOPES[h]), scalar2=None, op0=ALU.mult),
                     reads=[B_const], writes=[B_const])
            P.op("sp", lambda s: s.dma_start(out=nrmb[:], in_=nrm_d[0, :].partition_broadcast(128)),
                 writes=[B_const], dma=True)

        setup_consts()

        with ExitStack() as ea:
            w_in = sb("w_in_sb", [128, 8, INC], BF16, ea)
            w_out = sb("w_out_sb", [128, 8, D], BF16, ea)
            kT = sb("kT", [128, 4, S], BF16, ea)
            vext = sb("vext", [128, NT, 4, 129], BF16, ea)
            xt = sb("xt", [128, TPB, D], F32, ea)
            xs = sb("xs", [128, 1, D], BF16, ea)
            hnT = sb("hnT", [128, 8, BLK], BF16, ea)
            qT0 = sb("qT0", [128, 4, BLK], BF16, ea)
            qT1 = sb("qT1", [128, 4, BLK], BF16, ea)
            veA = sb("veA", [128, 4, 129], BF16, ea)
            veB = sb("veB", [128, 4, 129], BF16, ea)
            mqkT = sb("mqkT", [128, 8, BLK], BF16, ea)
            pre = sb("pre", [128, 2, BLK + 3], F32, ea)
            acc = sb("acc", [128, 2, BLK], F32, ea)
            th = sb("th", [128, 1, 512], F32, ea)
            mk = sb("mk", [128, 2, 4, 128], BF16, ea)
            ve = sb("ve", [128, 2, 4, 129], BF16, ea)
            gate = sb("gate", [128, 1, 512], F32, ea)
            E = sb("E", [128, 2, 2, BLK], BF16, ea)
            mixed = sb("mixed", [128, TPB, D], BF16, ea)
            x1t = sb("x1t", [128, 2, 512], F32, ea)
            epiA = sb("epiA", [128, 2, 128], F32, ea)
            epiT = sb("epiT", [128, 2, 128], F32, ea)
            mlT = sb("mlT", [128, 2, 128], F32, ea)
            junk = sb("junk", [128, 2, 128], BF16, ea)
            sm = sb("sm", [128, 2, 128], BF16, ea)
            nwb_ml = sb("nwb_ml", [128, 512], F32, ea)
            dnw_b = sb("dnw_b", [128, 128], F32, ea)
            lamv = sb("lamv_sb", [128, 256], F32, ea)
            Cf = sb("Cf", [128, 4, 129], F32, ea)
            CbA = sb("CbA", [128, 4, 129], BF16, ea)
            CbB = sb("CbB", [128, 4, 129], BF16, ea)
            halo = sb("halo", [128, 8, 3], F32, ea)
            cw = sb("cw", [128, 8, 4], F32, ea)
            cb = sb("cb", [128, 8], F32, ea)
            gbias = sb("gbias", [4, 4], F32, ea)
            g_i = sb("g_i", [4, BLK], F32, ea)
            g_f = sb("g_f", [4, BLK], F32, ea)
            g_B = sb("g_B", [4, BLK], F32, ea)
            g_G = sb("g_G", [4, BLK], F32, ea)
            g_e = sb("g_e", [4, 2, BLK], F32, ea)
            g_c = sb("g_c", [4, 8], F32, ea)
            g_d = sb("g_d", [4, 4], F32, ea)
            ones4 = sb("ones4", [4, 128], F32, ea)
            etok = sb("etok", [128, TPB, 8], F32, ea)
            sbc = sb("sbc", [128, 4], F32, ea)
            st = sb("st", [128, 64], F32, ea)

            psT = ps("psT", [128, 1024], BF16, ea)
            ps_s = [ps(f"ps_s{i}", [128, 2, BLK], F32, ea) for i in range(2)]
            ps_o = [ps(f"ps_o{i}", [128, 512], F32, ea) for i in range(2)]
            ps_p = [ps(f"ps_p{i}", [128, 512], F32, ea) for i in range(2)]
            ps_m = ps("ps_m", [128, 512], F32, ea)

            B_win = Buf("w_in"); B_wout = Buf("w_out")
            B_kT = [Buf(f"kT{j}") for j in range(NB)]
            B_v = [Buf(f"v{t}") for t in range(NT)]
            B_xt = [Buf(f"xt{i}") for i in range(TPB)]
            B_xs = [Buf(f"xs{i}") for i in range(TPB)]
            B_hnT = [Buf(f"hnT{i}") for i in range(TPB)]
            B_qT = [Buf(f"qT{h}") for h in range(4)]
            B_mqk = [Buf(f"mqk{f}") for f in range(8)]
            B_pre = [Buf(f"pre{i}") for i in range(2)]
            B_acc = [Buf(f"acc{i}") for i in range(2)]
            B_th = [Buf(f"th{i}") for i in range(2)]
            B_mk = [Buf(f"mk{i}") for i in range(2)]
            B_ve = [Buf(f"ve{i}") for i in range(2)]
            B_gate = [Buf(f"gate{i}") for i in range(2)]
            B_E = [Buf(f"E{i}") for i in range(2)]
            B_veAB = Buf("veAB")
            B_mixed = [[Buf(f"mixed{i}_{c}") for c in range(8)] for i in range(TPB)]
            B_x1t = [Buf(f"x1t{i}") for i in range(2)]
            B_epiA = [Buf(f"epiA{i}") for i in range(2)]
            B_epiT = [Buf(f"epiT{i}") for i in range(2)]
            B_mlT = [Buf(f"mlT{i}") for i in range(2)]
            B_junk = [Buf(f"junk{i}") for i in range(2)]
            B_sm = [Buf(f"sm{i}") for i in range(2)]
            B_Cf = [Buf(f"Cf{h}") for h in range(4)]
            B_CbA = [Buf(f"CbA{h}") for h in range(4)]
            B_CbB = [Buf(f"CbB{h}") for h in range(4)]
            B_halo = [Buf(f"halo{f}") for f in range(8)]
            B_g = Buf("gates")
            B_etok = Buf("etok")
            B_sbc = Buf("sbc")
            B_st = [Buf(f"st{i}") for i in range(64)]
            B_psT = Buf("psT")
            B_ps_s = [Buf(f"ps_s{i}") for i in range(2)]
            B_ps_o = [Buf(f"ps_o{i}") for i in range(2)]
            B_ps_p = [Buf(f"ps_p{i}") for i in range(2)]
            B_psm_sc = Buf("psm"); B_psm_num = B_psm_sc
            B_psm_dc = [B_psm_sc, B_psm_sc]; B_psm_x = B_psm_sc
            PM_SC = ps_m[:, 0:128]
            PM_NUM = ps_m[:, 128:257]
            PM_DC = [ps_m[:, 257:386], ps_m[:, 386:515] if False else None]

            B_psm_dc[1] = B_psm_dc[0]
            PM_DC = [ps_m[:, 257:386], ps_m[:, 257:386]]
            PM_X = ps_m[:, 386:400]

            state = {"stc": 0, "pp": 0}

            def st_slot():
                i = state["stc"] % 64
                state["stc"] += 1
                return st[:, i:i + 1], B_st[i]

            def pp_next():
                i = state["pp"] % 2
                state["pp"] += 1
                return i

            half = INC // 2
            for kc in range(8):
                for hf in range(2):
                    P.op("pool", lambda g, kc=kc, hf=hf: g.dma_start(
                        out=w_in[:, kc, hf * half:(hf + 1) * half],
                        in_=w_in_d[kc * 128:(kc + 1) * 128, hf * half:(hf + 1) * half]),
                        writes=[B_win], dma=True)
            for kc in range(8):
                P.op("pool", lambda g, kc=kc: g.dma_start(out=w_out[:, kc, :], in_=w_out_d[kc * 128:(kc + 1) * 128, :]),
                     writes=[B_wout], dma=True)
            B_par = Buf("params")
            P.op("sp", lambda s: s.dma_start(out=nwb_ml[:], in_=mlnw_d.rearrange("a d -> (a d)").partition_broadcast(128)),
                 writes=[B_par], dma=True)
            P.op("sp", lambda s: s.dma_start(out=dnw_b[:], in_=dnw_d.rearrange("a d -> (a d)").partition_broadcast(128)),
                 writes=[B_par], dma=True)
            P.op("sp", lambda s: s.dma_start(out=lamv[:], in_=lamv_d.rearrange("a d -> (a d)").partition_broadcast(128)),
                 writes=[B_par], dma=True)
            P.op("sp", lambda s: s.dma_start(out=cw[:], in_=mcw_d.rearrange("p (f j) -> p f j", j=4)), writes=[B_par], dma=True)
            P.op("sp", lambda s: s.dma_start(out=cb[:], in_=mcb_d), writes=[B_par], dma=True)
            P.op("sp", lambda s: s.dma_start(out=gbias[:, 0:2], in_=gb_d), writes=[B_par], dma=True)
            P.op("dve", lambda v: v.tensor_scalar(out=nwb_ml[:], in0=nwb_ml[:], scalar1=0.5, scalar2=None, op0=ALU.mult),
                 reads=[B_par], writes=[B_par])
            P.op("dve", lambda v: v.tensor_scalar(out=dnw_b[:], in0=dnw_b[:], scalar1=float(1.0 - LAM_INIT), scalar2=None,
                                                  op0=ALU.mult), reads=[B_par], writes=[B_par])
            P.op("dve", lambda v: v.tensor_scalar(out=cw[:], in0=cw[:], scalar1=0.5, scalar2=None, op0=ALU.mult),
                 reads=[B_par], writes=[B_par])
            P.op("dve", lambda v: v.tensor_scalar(out=cb[:], in0=cb[:], scalar1=0.5, scalar2=None, op0=ALU.mult),
                 reads=[B_par], writes=[B_par])
            P.op("dve", lambda v: v.tensor_scalar(out=gbias[:, 2:3], in0=gbias[:, 1:2], scalar1=-1.0, scalar2=None,
                                                  op0=ALU.mult), reads=[B_par], writes=[B_par])
            P.op("dve", lambda v: v.tensor_tensor(out=lamv[:, 0:64], in0=lamv[:, 0:64], in1=lamv[:, 64:128], op=ALU.mult),
                 reads=[B_par], writes=[B_par])
            P.op("dve", lambda v: v.tensor_tensor(out=lamv[:, 128:192], in0=lamv[:, 128:192], in1=lamv[:, 192:256],
                                                  op=ALU.mult), reads=[B_par], writes=[B_par])
            P.op("dve", lambda v: v.reduce_sum(out=cst[:, 3:4], in_=lamv[:, 0:64], axis=mybir.AxisListType.X),
                 reads=[B_par, B_const], writes=[B_const])
            P.op("dve", lambda v: v.reduce_sum(out=cst[:, 4:5], in_=lamv[:, 128:192], axis=mybir.AxisListType.X),
                 reads=[B_par, B_const], writes=[B_const])
            P.op("act", lambda a: a.activation(out=cst[:, 5:7], in_=cst[:, 3:5], func=AF.Exp), reads=[B_const],
                 writes=[B_const])
            P.op("dve", lambda v: v.tensor_tensor(out=cst[:, 1:2], in0=cst[:, 5:6], in1=cst[:, 6:7], op=ALU.subtract),
                 reads=[B_const], writes=[B_const])
            P.op("dve", lambda v: v.tensor_scalar(out=cst[:, 1:2], in0=cst[:, 1:2], scalar1=float(LAM_INIT), scalar2=None,
                                                  op0=ALU.add), reads=[B_const], writes=[B_const])
            P.op("dve", lambda v: v.tensor_scalar(out=cst[:, 2:3], in0=cst[:, 1:2], scalar1=-1.0, scalar2=None,
                                                  op0=ALU.mult), reads=[B_const], writes=[B_const])
            P.op("pool", lambda g: g.memset(vext[:].rearrange("p a b c -> p (a b c)"), 1.0), writes=B_v)
            P.op("pool", lambda g: g.memset(Cf[:].rearrange("p a b -> p (a b)"), 0.0), writes=B_Cf)
            P.op("pool", lambda g: g.memset(CbA[:].rearrange("p a b -> p (a b)"), 0.0), writes=B_CbA)
            P.op("pool", lambda g: g.memset(CbB[:].rearrange("p a b -> p (a b)"), 0.0), writes=B_CbB)
            P.op("pool", lambda g: g.memset(halo[:].rearrange("p a b -> p (a b)"), 0.0), writes=B_halo)
            P.op("pool", lambda g: g.memset(g_c[:], 0.0), writes=[B_g])
            for tz in (qT0, qT1, veA, veB):
                P.op("pool", lambda g, tz=tz: g.memset(tz[:].rearrange("p a b -> p (a b)"), 0.0), writes=B_qT + [B_veAB])
            P.op("pool", lambda g: g.memset(ones4[:], 1.0), writes=[B_g])

            def rms_rstd(src_ap, junk_ap, n, reads, junk_buf):
                ssq, bs = st_slot()
                rs, br = st_slot()
                P.op("act", lambda a: a.activation(out=junk_ap, in_=src_ap, func=AF.Square, accum_out=ssq),
                     reads=reads, writes=[junk_buf, bs])
                P.op("pool", lambda g: g.tensor_scalar(out=ssq, in0=ssq, scalar1=1.0 / n, scalar2=EPS, op0=ALU.mult,
                                                       op1=ALU.add), reads=[bs], writes=[bs])
                P.op("pool", lambda g: g.tensor_tensor(out=rs, in0=ssq, in1=NEGH, op=ALU.pow), reads=[bs, B_const],
                     writes=[br])
                return rs, br

            def transposes_to(dst_ap, dst_bufs, src_fn, src_bufs, evac="act"):
                for kc in range(8):
                    P.op("pe", lambda t, kc=kc: t.transpose(out=psT[:, kc * 128:(kc + 1) * 128], in_=src_fn(kc),
                                                            identity=ident[:]),
                         reads=src_bufs + [B_const], writes=[B_psT])
                src = psT[:].rearrange("p (k t) -> p k t", k=8)
                if evac == "act":
                    P.op("act", lambda a: a.copy(out=dst_ap, in_=src), reads=[B_psT], writes=dst_bufs)
                else:
                    P.op("dve", lambda v: v.tensor_copy(out=dst_ap, in_=src), reads=[B_psT], writes=dst_bufs)

            def mm_group(out_ap, out_buf, pairs, reads):
                n = len(pairs)
                for i, (l, r) in enumerate(pairs):
                    P.op("pe", lambda t, l=l, r=r, i=i: t.matmul(out_ap, lhsT=l, rhs=r, start=(i == 0), stop=(i == n - 1)),
                         reads=reads, writes=[out_buf])

            def A1(jb):
                for ti in range(TPB):
                    gt = jb * TPB + ti
                    P.op("sp", lambda s, ti=ti, gt=gt: s.dma_start(out=xt[:, ti, :], in_=x_d[gt * 128:(gt + 1) * 128, :]),
                         writes=[B_xt[ti]], dma=True)
                    rs, br = rms_rstd(xt[:, ti, :], xs[:, 0, :], D, [B_xt[ti]], B_xs[0])
                    P.op("dve", lambda v, ti=ti, rs=rs: v.scalar_tensor_tensor(
                        out=xs[:, 0, :], in0=xt[:, ti, :], scalar=rs, in1=nrmb[:, 0:D], op0=ALU.mult, op1=ALU.mult),
                        reads=[B_xt[ti], br, B_const], writes=[B_xs[0]])
                    transposes_to(hnT[:, :, ti * 128:(ti + 1) * 128], [B_hnT[ti]],
                                  lambda kc, ti=ti: xs[:, 0, kc * 128:(kc + 1) * 128], [B_xs[0]])

            def proj_fm(col0, evac):
                i = pp_next()
                mm_group(ps_p[i][:, 0:BLK], B_ps_p[i],
                         [(w_in[:, kc, col0:col0 + 128], hnT[:, kc, :]) for kc in range(8)],
                         [B_win] + B_hnT)
                evac(ps_p[i][:, 0:BLK], B_ps_p[i])

            def proj_tm(ti, col0, ncols, evac):
                i = pp_next()
                mm_group(ps_p[i][:, 0:ncols], B_ps_p[i],
                         [(hnT[:, kc, ti * 128:(ti + 1) * 128], w_in[:, kc, col0:col0 + ncols]) for kc in range(8)],
                         [B_win, B_hnT[ti]])
                evac(ps_p[i][:, 0:ncols], B_ps_p[i])

            def A2_qkv(jb):
                for h in range(4):
                    def ev_q(p_ap, p_buf, h=h):
                        P.op("dve", lambda v: v.tensor_scalar(out=qT0[0:64, h, :], in0=p_ap[0:64, :], scalar1=0.125,
                                                              scalar2=None, op0=ALU.mult), reads=[p_buf], writes=[B_qT[h]])
                        P.op("dve", lambda v: v.tensor_scalar(out=qT1[64:128, h, :], in0=p_ap[64:128, :], scalar1=0.125,
                                                              scalar2=None, op0=ALU.mult), reads=[p_buf], writes=[B_qT[h]])
                    proj_fm(h * 128, ev_q)

                    def ev_k(p_ap, p_buf, h=h):
                        P.op("act", lambda a: a.copy(out=kT[:, h, jb * BLK:(jb + 1) * BLK], in_=p_ap), reads=[p_buf],
                             writes=[B_kT[jb]])
                    proj_fm(512 + h * 128, ev_k)
                for ti in range(TPB):
                    gt = jb * TPB + ti

                    def ev_v(p_ap, p_buf, gt=gt):
                        P.op("dve", lambda v: v.tensor_copy(out=vext[:, gt, :, 0:128],
                                                            in_=p_ap.rearrange("p (h d) -> p h d", h=4)),
                             reads=[p_buf], writes=[B_v[gt]])
                    proj_tm(ti, 1024, 512, ev_v)

            def attn_units(jb):
                units = []
                steps = []
                epis = []
                q0 = jb * BLK
                nk = 2 * jb + 2
                for h in range(4):
                    for kt in range(nk):
                        def step(h=h, kt=kt):
                            si = (h * nk + kt) % 2
                            ei = (h * nk + kt) % 2
                            diag = kt - 2 * jb
                            qlo = 128 if diag == 1 else 0
                            for c in range(2):
                                P.op("pe", lambda t, c=c: t.matmul(
                                    ps_s[si][:, c, qlo:BLK],
                                    lhsT=kT[:, h, kt * 128:(kt + 1) * 128],
                                    rhs=(qT0 if c == 0 else qT1)[:, h, qlo:BLK], start=True, stop=True),
                                    reads=[B_kT[kt // TPB], B_qT[h]], writes=[B_ps_s[si]])
                            yield
                            bi = (q0 - kt * 128) // 128 + 1
                            P.op("act", lambda a: a.activation(out=E[:, ei, :, qlo:BLK], in_=ps_s[si][:, :, qlo:BLK],
                                                               func=AF.Exp, bias=biasT[:, h * 32 + bi:h * 32 + bi + 1],
                                                               scale=1.0),
                                 reads=[B_ps_s[si], B_const], writes=[B_E[ei]])
                            if diag >= 0:
                                qs = diag * 128
                                P.op("pool", lambda g: g.tensor_tensor(
                                    out=E[:, ei, :, qs:qs + 128], in0=E[:, ei, :, qs:qs + 128],
                                    in1=maskc[:].unsqueeze(1).broadcast_to([128, 2, 128]), op=ALU.mult),
                                    reads=[B_E[ei], B_const], writes=[B_E[ei]])
                            yield
                            for qi in range(2):
                                if diag == 1 and qi == 0:
                                    continue
                                last = (kt == nk - 1) if qi == 1 else (kt == nk - 2)
                                for c in range(2):
                                    P.op("pe", lambda t, qi=qi, c=c, last=last: t.matmul(
                                        ps_o[qi][:, c * 129:(c + 1) * 129],
                                        lhsT=E[:, ei, c, qi * 128:(qi + 1) * 128],
                                        rhs=vext[:, kt, h, :], start=(kt == 0 and c == 0), stop=last,
                                        skip_group_check=True),
                                        reads=[B_E[ei], B_v[kt]], writes=[B_ps_o[qi]])
                        steps.append((step(), h, kt == nk - 1))

                    def epi(h=h):
                        for qi in range(2):
                            epi_q(h, qi)

                    def epi_q(h, qi):
                        if True:
                            po = ps_o[qi]
                            r2, b2 = st_slot()
                            r2b, b2b = st_slot()
                            nl, bn = st_slot()
                            P.op("dve", lambda v: v.reciprocal(out=r2, in_=po[:, 128:129]), reads=[B_ps_o[qi]], writes=[b2])
                            P.op("dve", lambda v: v.reciprocal(out=r2b, in_=po[:, 257:258]), reads=[B_ps_o[qi]],
                                 writes=[b2b])
                            P.op("dve", lambda v: v.tensor_tensor(out=nl, in0=r2b, in1=NLAM, op=ALU.mult),
                                 reads=[b2b, B_const], writes=[bn])
                            P.op("dve", lambda v: v.tensor_scalar(out=epiT[:, qi, :], in0=po[:, 129:257], scalar1=nl,
                                                                  scalar2=None, op0=ALU.mult),
                                 reads=[B_ps_o[qi], bn], writes=[B_epiT[qi]])
                            P.op("dve", lambda v: v.scalar_tensor_tensor(out=epiA[:, qi, :], in0=po[:, 0:128], scalar=r2,
                                                                         in1=epiT[:, qi, :], op0=ALU.mult, op1=ALU.add),
                                 reads=[B_ps_o[qi], b2, B_epiT[qi]], writes=[B_epiA[qi]])
                            rs, br = rms_rstd(epiA[:, qi, :], junk[:, qi, :], 128, [B_epiA[qi]], B_junk[qi])
                            P.op("dve", lambda v, rs=rs: v.scalar_tensor_tensor(
                                out=mixed[:, qi, h * 128:(h + 1) * 128], in0=epiA[:, qi, :], scalar=rs, in1=dnw_b[:],
                                op0=ALU.mult, op1=ALU.mult),
                                reads=[B_epiA[qi], br, B_par], writes=[B_mixed[qi][h]])
                    epis.append(epi)
                n = len(steps)
                units.append([steps[0][0]])
                for i in range(n):
                    u = []
                    if i + 1 < n:
                        u.append(steps[i + 1][0])
                    u.append(steps[i][0])
                    u.append(steps[i][0])
                    units.append(u)
                    if steps[i][2]:
                        units.append([epis[steps[i][1]]])
                return units

            def ml_units(jb):
                units = []
                m1s = []
                m3s = []
                m4s = [[] for _ in range(TPB)]
                for f in range(8):
                    def m1(f=f):
                        s_ = f % 2

                        def ev(p_ap, p_buf):
                            P.op("act", lambda a: a.copy(out=pre[:, s_, 3:BLK + 3], in_=p_ap), reads=[p_buf],
                                 writes=[B_pre[s_]])
                        P.op("pool", lambda g: g.tensor_copy(out=pre[:, s_, 0:3], in_=halo[:, f, :]), reads=[B_halo[f]],
                             writes=[B_pre[s_]])
                        proj_fm(1536 + f * 128, ev)
                        P.op("pool", lambda g: g.tensor_copy(out=halo[:, f, :], in_=pre[:, s_, BLK:BLK + 3]),
                             reads=[B_pre[s_]], writes=[B_halo[f]])
                        yield
                        P.op("dve", lambda v: v.tensor_scalar(out=acc[:, s_, :], in0=pre[:, s_, 3:BLK + 3],
                                                              scalar1=cw[:, f, 3:4], scalar2=cb[:, f:f + 1],
                                                              op0=ALU.mult, op1=ALU.add),
                             reads=[B_pre[s_], B_par], writes=[B_acc[s_]])
                        for j in (2, 1, 0):
                            P.op("dve", lambda v, j=j: v.scalar_tensor_tensor(
                                out=acc[:, s_, :], in0=pre[:, s_, j:j + BLK], scalar=cw[:, f, j:j + 1], in1=acc[:, s_, :],
                                op0=ALU.mult, op1=ALU.add), reads=[B_pre[s_], B_acc[s_], B_par], writes=[B_acc[s_]])
                        yield
                        P.op("act", lambda a: a.activation(out=th[:, 0, 0:BLK], in_=acc[:, s_, :], func=AF.Tanh),
                             reads=[B_acc[s_]], writes=[B_th[0]])
                        yield
                        P.op("dve", lambda v: v.scalar_tensor_tensor(out=mqkT[:, f, :], in0=th[:, 0, 0:BLK], scalar=1.0,
                                                                     in1=acc[:, s_, :], op0=ALU.add, op1=ALU.mult),
                             reads=[B_th[0], B_acc[s_]], writes=[B_mqk[f]])
                    m1s.append(m1())

                def m2():
                    ii = pp_next()
                    mm_group(ps_p[ii][0:4, 0:BLK], B_ps_p[ii],
                             [(w_in[:, kc, 3584:3588], hnT[:, kc, :]) for kc in range(8)], [B_win] + B_hnT)
                    P.op("act", lambda a: a.activation(out=g_i[:], in_=ps_p[ii][0:4, 0:BLK], func=AF.Identity,
                                                       bias=gbias[:, 0:1], scale=1.0),
                         reads=[B_ps_p[ii], B_par], writes=[B_g])
                    fi = pp_next()
                    mm_group(ps_p[fi][0:4, 0:BLK], B_ps_p[fi],
                             [(w_in[:, kc, 3588:3592], hnT[:, kc, :]) for kc in range(8)], [B_win] + B_hnT)
                    P.op("act", lambda a: a.activation(out=g_f[:], in_=ps_p[fi][0:4, 0:BLK], func=AF.Exp,
                                                       bias=gbias[:, 2:3], scale=-1.0),
                         reads=[B_ps_p[fi], B_par, B_g], writes=[B_g])
                    P.op("act", lambda a: a.activation(out=g_f[:], in_=g_f[:], func=AF.Ln, bias=1.0, scale=1.0),
                         reads=[B_g], writes=[B_g])
                    yield
                    P.op("dve", lambda v: v.tensor_tensor_scan(out=g_B[:], data0=g_f[:], data1=g_f[:],
                                                               initial=g_c[:, 0:1], op0=ALU.add, op1=ALU.max),
                         reads=[B_g], writes=[B_g])
                    P.op("dve", lambda v: v.tensor_copy(out=g_c[:, 0:1], in_=g_B[:, BLK - 1:BLK]), reads=[B_g], writes=[B_g])
                    P.op("dve", lambda v: v.tensor_tensor(out=g_i[:], in0=g_i[:], in1=g_B[:], op=ALU.add),
                         reads=[B_g], writes=[B_g])
                    P.op("dve", lambda v: v.tensor_tensor_scan(out=g_G[:], data0=g_i[:], data1=g_i[:],
                                                               initial=g_c[:, 1:2], op0=ALU.max, op1=ALU.max),
                         reads=[B_g], writes=[B_g])
                    yield
                    P.op("dve", lambda v: v.tensor_scalar(out=g_c[:, 2:3], in0=g_G[:, BLK - 1:BLK], scalar1=-1.0,
                                                          scalar2=None, op0=ALU.mult), reads=[B_g], writes=[B_g])
                    P.op("dve", lambda v: v.tensor_scalar(out=g_c[:, 4:5], in0=g_c[:, 2:3], scalar1=float(-LN_SQRT_D),
                                                          scalar2=None, op0=ALU.add), reads=[B_g], writes=[B_g])
                    P.op("act", lambda a: a.activation(out=g_c[:, 3:4], in_=g_c[:, 1:2], func=AF.Exp, bias=g_c[:, 2:3],
                                                       scale=1.0), reads=[B_g], writes=[B_g])
                    P.op("dve", lambda v: v.tensor_copy(out=g_c[:, 1:2], in_=g_G[:, BLK - 1:BLK]), reads=[B_g], writes=[B_g])
                    P.op("act", lambda a: a.activation(out=g_e[:, 0, :], in_=g_i[:], func=AF.Exp, bias=g_c[:, 4:5],
                                                       scale=1.0), reads=[B_g], writes=[B_g])
                    P.op("act", lambda a: a.activation(out=g_e[:, 1, :], in_=g_B[:], func=AF.Exp, bias=g_c[:, 2:3],
                                                       scale=1.0), reads=[B_g], writes=[B_g])
                    yield
                    for ti in range(TPB):
                        for w_ in range(2):
                            P.op("pe", lambda t, ti=ti, w_=w_: t.transpose(
                                out=PM_X[:, w_ * 4:(w_ + 1) * 4], in_=g_e[:, w_, ti * 128:(ti + 1) * 128],
                                identity=identf[0:4, 0:4]), reads=[B_g, B_const], writes=[B_psm_x])
                        P.op("dve", lambda v, ti=ti: v.tensor_copy(out=etok[:, ti, :], in_=PM_X[:, 0:8]),
                             reads=[B_psm_x], writes=[B_etok])
                    yield
                    P.op("dve", lambda v: v.tensor_scalar(out=g_d[:], in0=identf[0:4, 0:4], scalar1=g_c[:, 3:4],
                                                          scalar2=None, op0=ALU.mult), reads=[B_g, B_const], writes=[B_g])
                    P.op("pe", lambda t: t.matmul(PM_X[:, 8:12], lhsT=ones4[:], rhs=g_d[:], start=True, stop=True),
                         reads=[B_g], writes=[B_psm_x])
                    P.op("dve", lambda v: v.tensor_copy(out=sbc[:], in_=PM_X[:, 8:12]), reads=[B_psm_x], writes=[B_sbc])
                    for hh in range(4):
                        P.op("dve", lambda v, hh=hh: v.tensor_scalar(out=Cf[:, hh, :], in0=Cf[:, hh, :],
                                                                    scalar1=sbc[:, hh:hh + 1], scalar2=None,
                                                                    op0=ALU.mult), reads=[B_sbc, B_Cf[hh]], writes=[B_Cf[hh]])
                    P.op("pool", lambda g: g.tensor_copy(out=CbA[:], in_=Cf[:]), reads=B_Cf, writes=B_CbA)
                m2g = m2()

                for ti in range(TPB):
                    def m3(ti=ti):
                        sl = ti % 2

                        def ev_v(p_ap, p_buf):
                            for hh in range(4):
                                P.op("dve", lambda v, hh=hh: v.tensor_scalar(
                                    out=ve[:, sl, hh, 0:128], in0=p_ap[:, hh * 128:(hh + 1) * 128],
                                    scalar1=etok[:, ti, hh:hh + 1], scalar2=None, op0=ALU.mult),
                                    reads=[p_buf, B_etok], writes=[B_ve[sl]])
                                P.op("dve", lambda v, hh=hh: v.tensor_copy(out=ve[:, sl, hh, 128:129],
                                                                          in_=etok[:, ti, hh:hh + 1]),
                                     reads=[B_etok], writes=[B_ve[sl]])
                        proj_tm(ti, 2560, 512, ev_v)
                        P.op("dve", lambda v: v.tensor_copy(out=veA[0:64, :, :], in_=ve[0:64, sl, :, :]),
                             reads=[B_ve[sl]], writes=[B_veAB])
                        P.op("dve", lambda v: v.tensor_copy(out=veB[64:128, :, :], in_=ve[64:128, sl, :, :]),
                             reads=[B_ve[sl]], writes=[B_veAB])

                        yield

                        def ev_o(p_ap, p_buf):
                            P.op("act", lambda a: a.activation(out=th[:, 0, :], in_=p_ap, func=AF.Tanh, scale=0.5),
                                 reads=[p_buf], writes=[B_th[0]])
                            P.op("dve", lambda v: v.scalar_tensor_tensor(out=gate[:, 0, :], in0=th[:, 0, :], scalar=1.0,
                                                                         in1=nwb_ml[:], op0=ALU.add, op1=ALU.mult),
                                 reads=[B_th[0], B_par], writes=[B_gate[0]])
                        proj_tm(ti, 3072, 512, ev_o)
                        yield
                        for h in range(4):
                            P.op("pe", lambda t, h=h: t.transpose(out=psT[:, h * 128:(h + 1) * 128],
                                                                  in_=mqkT[:, 4 + h, ti * 128:(ti + 1) * 128],
                                                                  identity=ident[:]),
                                 reads=[B_mqk[4 + h], B_const], writes=[B_psT])
                        P.op("act", lambda a: a.copy(out=mk[:, sl, :, :], in_=psT[:, 0:512].rearrange("p (h d) -> p h d", h=4)),
                             reads=[B_psT], writes=[B_mk[sl]])
                    m3s.append(m3())
                    for h in range(4):
                        def m4(ti=ti, h=h):
                            if DEBUG.get("printops"):
                                print("m4 start nrec", getattr(P, "nrec", 0), ti, h)
                            sl = ti % 2
                            si = h % 2
                            tok = slice(ti * 128, (ti + 1) * 128)
                            P.op("pe", lambda t: t.matmul(PM_SC, lhsT=mqkT[:, 4 + h, tok], rhs=mqkT[:, h, tok],
                                                          start=True, stop=True),
                                 reads=[B_mqk[4 + h], B_mqk[h]], writes=[B_psm_sc])
                            yield
                            P.op("dve", lambda v: v.tensor_tensor(out=sm[:, si, :], in0=PM_SC, in1=mask2[:], op=ALU.mult),
                                 reads=[B_psm_sc, B_const], writes=[B_sm[si]])
                            yield
                            P.op("pe", lambda t: t.matmul(PM_DC[0], lhsT=mk[:, sl, h, :], rhs=veA[:, h, :],
                                                          start=True, stop=True),
                                 reads=[B_mk[sl], B_veAB], writes=[B_psm_dc[0]])
                            P.op("pe", lambda t: t.matmul(PM_NUM, lhsT=sm[:, si, :], rhs=ve[:, sl, h, :], start=True,
                                                          stop=False, skip_group_check=True),
                                 reads=[B_sm[si], B_ve[sl]], writes=[B_psm_num])
                            P.op("pe", lambda t: t.matmul(ps_m[0:64, 128:257], lhsT=mqkT[:, h, ti * 128:ti * 128 + 64],
                                                          rhs=CbA[:, h, :], start=False, stop=False, skip_group_check=True),
                                 reads=[B_mqk[h], B_CbA[h]], writes=[B_psm_num])
                            yield
                            P.op("dve", lambda v: v.tensor_tensor(out=Cf[:, h, :], in0=Cf[:, h, :], in1=PM_DC[0], op=ALU.add),
                                 reads=[B_Cf[h], B_psm_dc[0]], writes=[B_Cf[h]])
                            P.op("pool", lambda g: g.tensor_copy(out=CbB[:, h, :], in_=Cf[:, h, :]), reads=[B_Cf[h]],
                                 writes=[B_CbB[h]])
                            yield
                            P.op("pe", lambda t: t.matmul(ps_m[64:128, 128:257],
                                                          lhsT=mqkT[:, h, ti * 128 + 64:ti * 128 + 128],
                                                          rhs=CbB[:, h, :], start=False, stop=True, skip_group_check=True),
                                 reads=[B_mqk[h], B_CbB[h]], writes=[B_psm_num])
                            P.op("pe", lambda t: t.matmul(PM_DC[1], lhsT=mk[:, sl, h, :], rhs=veB[:, h, :],
                                                          start=True, stop=True),
                                 reads=[B_mk[sl], B_veAB], writes=[B_psm_dc[1]])
                            yield
                            P.op("dve", lambda v: v.tensor_tensor(out=Cf[:, h, :], in0=Cf[:, h, :], in1=PM_DC[1], op=ALU.add),
                                 reads=[B_Cf[h], B_psm_dc[1]], writes=[B_Cf[h]])
                            P.op("pool", lambda g: g.tensor_copy(out=CbA[:, h, :], in_=Cf[:, h, :]), reads=[B_Cf[h]],
                                 writes=[B_CbA[h]])
                            dn, bd = st_slot()
                            r_, brr = st_slot()
                            ssq, bs = st_slot()
                            t3, b3 = st_slot()
                            sfin, bsf = st_slot()
                            P.op("dve", lambda v: v.tensor_scalar(out=t3, in0=ps_m[:, 256:257], scalar1=-1.0, scalar2=None,
                                                                  op0=ALU.mult), reads=[B_psm_num], writes=[b3])
                            P.op("dve", lambda v: v.scalar_tensor_tensor(out=dn, in0=ps_m[:, 256:257], scalar=1.0, in1=t3,
                                                                         op0=ALU.mult, op1=ALU.max),
                                 reads=[B_psm_num, b3], writes=[bd])
                            P.op("dve", lambda v: v.tensor_scalar(out=dn, in0=dn, scalar1=etok[:, ti, 4 + h:5 + h],
                                                                  scalar2=None, op0=ALU.max),
                                 reads=[bd, B_etok], writes=[bd])
                            P.op("dve", lambda v: v.reciprocal(out=r_, in_=dn), reads=[bd], writes=[brr])
                            P.op("dve", lambda v: v.tensor_scalar(out=mlT[:, si, :], in0=ps_m[:, 128:256], scalar1=r_,
                                                                  scalar2=None, op0=ALU.mult),
                                 reads=[B_psm_num, brr], writes=[B_mlT[si]])
                            yield
                            rs, br = rms_rstd(mlT[:, si, :], junk[:, si, :], 128, [B_mlT[si]], B_junk[si])
                            yield
                            P.op("dve", lambda v: v.scalar_tensor_tensor(
                                out=mixed[:, ti, 512 + h * 128:512 + (h + 1) * 128], in0=mlT[:, si, :], scalar=rs,
                                in1=gate[:, 0, h * 128:(h + 1) * 128], op0=ALU.mult, op1=ALU.mult),
                                reads=[B_mlT[si], br, B_gate[0]], writes=[B_mixed[ti][4 + h]])
                        m4s[ti].append(m4())
                units += [m2g] * 5
                NS1 = 4
                for t in range(len(m1s) + NS1 - 1):
                    for st_ in reversed(range(NS1)):
                        u = t - st_
                        if 0 <= u < len(m1s):
                            units.append(m1s[u])
                for ti in range(TPB):
                    units += [m3s[ti]] * 3
                    for g in m4s[ti]:
                        units += [g] * 8
                return units

            def out_proj(jb):
                for ti in range(TPB):
                    gt = jb * TPB + ti
                    transposes_to(hnT[:, :, ti * 128:(ti + 1) * 128], [B_hnT[ti]],
                                  lambda kc, ti=ti: mixed[:, ti, kc * 128:(kc + 1) * 128], B_mixed[ti], evac="act")
                    for hf in range(2):
                        i = pp_next()
                        xi = (ti * 2 + hf) % 2
                        mm_group(ps_p[i][:, :], B_ps_p[i],
                                 [(hnT[:, kc, ti * 128:(ti + 1) * 128], w_out[:, kc, hf * 512:(hf + 1) * 512])
                                  for kc in range(8)], [B_wout, B_hnT[ti]])
                        P.op("dve", lambda v, i=i, xi=xi, ti=ti, hf=hf: v.tensor_tensor(
                            out=x1t[:, xi, :], in0=ps_p[i][:, :], in1=xt[:, ti, hf * 512:(hf + 1) * 512], op=ALU.add),
                            reads=[B_ps_p[i], B_xt[ti]], writes=[B_x1t[xi]])
                        P.op("sp", lambda s, xi=xi, gt=gt, hf=hf: s.dma_start(
                            out=x1_d[gt * 128:(gt + 1) * 128, hf * 512:(hf + 1) * 512], in_=x1t[:, xi, :]),
                            reads=[B_x1t[xi]], dma=True)

            nblk = DEBUG.get("nblk", NB)
            stage = DEBUG.get("stage", 99)
            for jb in range(nblk):
                if stage >= 1:
                    A1(jb)
                if stage >= 2:
                    A2_qkv(jb)
                def adv(g):
                    if isinstance(g, list):
                        for x in g:
                            adv(x)
                    elif callable(g):
                        g()
                    else:
                        try:
                            next(g)
                        except StopIteration:
                            pass
                if stage >= 5:
                    au, mu = attn_units(jb), ml_units(jb)
                    for u in interleave(au, mu):
                        adv(u)
                    for u in au + mu:
                        for g in (u if isinstance(u, list) else [u]):
                            if not callable(g):
                                for _ in g:
                                    pass
                if stage >= 6:
                    out_proj(jb)

            if "dbg" in DEBUG:
                dbg_d = dt("dbg", [128, TPB * D], BF16, kind="ExternalOutput").ap()
                P.op("sp", lambda s: s.dma_start(out=dbg_d, in_=mixed[:].rearrange("p t d -> p (t d)")),
                     reads=[b for bb in B_mixed for b in bb], dma=True)

            P.wait_all("sp", [("d", k, 16 * P.dma_cnt[k]) for k in range(NDMA) if P.dma_cnt[k] > 0])
            for e in P.ENGS:
                last = {e2: len(P.ops[e2]) - 1 for e2 in P.ENGS if e2 != e and len(P.ops[e2]) > 0}
                toks = []
                for e2, s2 in last.items():
                    while s2 >= 0 and (P.ops[e2][s2].fn is None or P.ops[e2][s2].dma is not None):
                        s2 -= 1
                    if s2 >= 0:
                        toks.append(("e", e2, s2))
                toks += [("d", k, 16 * P.dma_cnt[k]) for k in range(NDMA) if P.dma_cnt[k] > 0]
                P.wait_all(e, toks)
            P.emit()

        if DEBUG.get("phaseA_only"):
            return nc
        with ExitStack() as eb:
            w_up = sb("w_up_sb", [128, 8, 2 * DFF], BF16, eb)
            w_dn = sb("w_dn_sb", [128, NF, D], BF16, eb)
            x1s = sb("x1s", [128, TPB, D], F32, eb)
            xs2 = sb("xs2", [128, TPB, D], BF16, eb)
            hn2 = sb("hn2", [128, 8, BLK], BF16, eb)
            yT = sb("yT", [128, NF, BLK], BF16, eb)
            pbuf = sb("pbuf", [128, 4, BLK + 2], F32, eb)
            acc2 = sb("acc2", [128, 4, BLK], F32, eb)
            sg = sb("sg", [128, 2, BLK], F32, eb)
            x2 = sb("x2", [128, 2, D], F32, eb)
            junk2 = sb("junk2", [128, D], BF16, eb)
            halo2 = sb("halo2", [128, 44, 2], F32, eb)
            fcw = sb("fcw_sb", [128, 44, 3], F32, eb)
            fcb = sb("fcb_sb", [128, 44], F32, eb)
            st2 = sb("st2", [128, 64], F32, eb)
            nrmb2 = sb("nrmb2", [128, 2 * D], F32, eb)
            psT2 = ps("psT2", [128, 1024], BF16, eb)
            ps_u = [ps(f"ps_u{i}", [128, 512], F32, eb) for i in range(4)]
            ps_d = [ps(f"ps_d{i}", [128, 512], F32, eb) for i in range(2)]

            B_wup = Buf("wup"); B_wdn = Buf("wdn"); B_par2 = Buf("par2")
            B_x1s = [Buf(f"x1s{i}") for i in range(TPB)]
            B_xs2 = [Buf(f"xs2{i}") for i in range(TPB)]
            B_hn2 = [Buf(f"hn2{i}") for i in range(TPB)]
            B_yT = [Buf(f"yT{f}") for f in range(NF)]
            B_pbuf = [Buf(f"pbuf{i}") for i in range(4)]
            B_acc2 = [Buf(f"acc2{i}") for i in range(4)]
            B_sg = [Buf(f"sg{i}") for i in range(2)]
            B_x2 = [Buf(f"x2{i}") for i in range(2)]
            B_junk2 = Buf("junk2")
            B_halo2 = [Buf(f"halo2{i}") for i in range(44)]
            B_st2 = [Buf(f"st2{i}") for i in range(64)]
            B_psT2 = Buf("psT2")
            B_ps_u = [Buf(f"ps_u{i}") for i in range(4)]
            B_ps_d = [Buf(f"ps_d{i}") for i in range(2)]
            stb = {"c": 0, "u": 0, "d": 0}

            def st2_slot():
                i = stb["c"] % 64
                stb["c"] += 1
                return st2[:, i:i + 1], B_st2[i]

            for kc in range(8):
                for q in range(4):
                    c0 = q * 1408
                    P.op("pool", lambda g, kc=kc, c0=c0: g.dma_start(out=w_up[:, kc, c0:c0 + 1408],
                                                                   in_=w_up_d[kc * 128:(kc + 1) * 128, c0:c0 + 1408]),
                         writes=[B_wup], dma=True)
            for kc in range(NF):
                P.op("pool", lambda g, kc=kc: g.dma_start(out=w_dn[:, kc, :], in_=w_down_d[kc * 128:(kc + 1) * 128, :]),
                     writes=[B_wdn], dma=True)
            P.op("sp", lambda s: s.dma_start(out=fcw[:], in_=fcw_d.rearrange("p (f j) -> p f j", j=3)), writes=[B_par2],
                 dma=True)
            P.op("sp", lambda s: s.dma_start(out=fcb[:], in_=fcb_d), writes=[B_par2], dma=True)
            P.op("sp", lambda s: s.dma_start(out=nrmb2[:, 0:D], in_=nrm_d[1, :].partition_broadcast(128)), writes=[B_par2], dma=True)
            P.op("sp", lambda s: s.dma_start(out=nrmb2[:, D:2 * D], in_=nrm_d[2, :].partition_broadcast(128)), writes=[B_par2], dma=True)
            P.op("pool", lambda g: g.memset(halo2[:].rearrange("p a b -> p (a b)"), 0.0), writes=B_halo2)

            def rms_rstd2(src_ap, junk_ap, n, reads, junk_buf):
                ssq, bs = st2_slot()
                rs, br = st2_slot()
                P.op("act", lambda a: a.activation(out=junk_ap, in_=src_ap, func=AF.Square, accum_out=ssq),
                     reads=reads, writes=[junk_buf, bs])
                P.op("pool", lambda g: g.tensor_scalar(out=ssq, in0=ssq, scalar1=1.0 / n, scalar2=EPS, op0=ALU.mult,
                                                       op1=ALU.add), reads=[bs], writes=[bs])
                P.op("pool", lambda g: g.tensor_tensor(out=rs, in0=ssq, in1=NEGH, op=ALU.pow), reads=[bs, B_const],
                     writes=[br])
                return rs, br

            def B1(jb):
                for ti in range(TPB):
                    gt = jb * TPB + ti
                    P.op("sp", lambda s, ti=ti, gt=gt: s.dma_start(out=x1s[:, ti, :], in_=x1_d[gt * 128:(gt + 1) * 128, :]),
                         writes=[B_x1s[ti]], dma=True)
                    rs, br = rms_rstd2(x1s[:, ti, :], xs2[:, ti, :], D, [B_x1s[ti]], B_xs2[ti])
                    P.op("dve", lambda v, ti=ti, rs=rs: v.scalar_tensor_tensor(
                        out=xs2[:, ti, :], in0=x1s[:, ti, :], scalar=rs, in1=nrmb2[:, 0:D], op0=ALU.mult, op1=ALU.mult),
                        reads=[B_x1s[ti], br, B_par2], writes=[B_xs2[ti]])
                    for kc in range(8):
                        P.op("pe", lambda t, kc=kc, ti=ti: t.transpose(out=psT2[:, kc * 128:(kc + 1) * 128],
                                                                       in_=xs2[:, ti, kc * 128:(kc + 1) * 128],
                                                                       identity=ident[:]),
                             reads=[B_xs2[ti], B_const], writes=[B_psT2])
                    P.op("act", lambda a, ti=ti: a.copy(out=hn2[:, :, ti * 128:(ti + 1) * 128],
                                                        in_=psT2[:].rearrange("p (k t) -> p k t", k=8)),
                         reads=[B_psT2], writes=[B_hn2[ti]])

            def B2(jb):
                for f in range(NF):
                    slots = []
                    for part in range(2):
                        tidx = part * NF + f
                        col0 = part * DFF + f * 128
                        ui = stb["u"] % 4
                        stb["u"] += 1
                        slots.append(ui)
                        n = 8
                        for kc in range(8):
                            P.op("pe", lambda t, kc=kc, ui=ui, col0=col0: t.matmul(
                                ps_u[ui][:, 0:BLK], lhsT=w_up[:, kc, col0:col0 + 128], rhs=hn2[:, kc, :],
                                start=(kc == 0), stop=(kc == 7)), reads=[B_wup] + B_hn2, writes=[B_ps_u[ui]])
                        P.op("pool", lambda g, ui=ui, tidx=tidx: g.tensor_copy(out=pbuf[:, ui, 0:2], in_=halo2[:, tidx, :]),
                             reads=[B_halo2[tidx]], writes=[B_pbuf[ui]])
                        P.op("act", lambda a, ui=ui: a.copy(out=pbuf[:, ui, 2:BLK + 2], in_=ps_u[ui][:, 0:BLK]),
                             reads=[B_ps_u[ui]], writes=[B_pbuf[ui]])
                        P.op("act", lambda a, ui=ui, tidx=tidx: a.activation(
                            out=acc2[:, ui, :], in_=ps_u[ui][:, 0:BLK], func=AF.Identity, bias=fcb[:, tidx:tidx + 1],
                            scale=fcw[:, tidx, 2:3]), reads=[B_ps_u[ui], B_par2], writes=[B_acc2[ui]])
                        P.op("pool", lambda g, ui=ui, tidx=tidx: g.tensor_copy(out=halo2[:, tidx, :],
                                                                               in_=pbuf[:, ui, BLK:BLK + 2]),
                             reads=[B_pbuf[ui]], writes=[B_halo2[tidx]])
                        for j in (1, 0):
                            P.op("dve", lambda v, ui=ui, tidx=tidx, j=j: v.scalar_tensor_tensor(
                                out=acc2[:, ui, :], in0=pbuf[:, ui, j:j + BLK], scalar=fcw[:, tidx, j:j + 1],
                                in1=acc2[:, ui, :], op0=ALU.mult, op1=ALU.add),
                                reads=[B_pbuf[ui], B_acc2[ui], B_par2], writes=[B_acc2[ui]])
                    ug, uu = slots
                    gi = f % 2
                    P.op("act", lambda a, ug=ug, gi=gi: a.activation(out=sg[:, gi, :], in_=acc2[:, ug, :], func=AF.Silu),
                         reads=[B_acc2[ug]], writes=[B_sg[gi]])
                    P.op("dve", lambda v, uu=uu, gi=gi, f=f: v.tensor_tensor(out=yT[:, f, :], in0=sg[:, gi, :],
                                                                           in1=acc2[:, uu, :], op=ALU.mult),
                         reads=[B_sg[gi], B_acc2[uu]], writes=[B_yT[f]])

            def B3(jb):
                for ti in range(TPB):
                    gt = jb * TPB + ti
                    xi = gt % 2
                    for hf in range(2):
                        di = stb["d"] % 2
                        stb["d"] += 1
                        for kc in range(NF):
                            P.op("pe", lambda t, kc=kc, di=di, ti=ti, hf=hf: t.matmul(
                                ps_d[di][:, :], lhsT=yT[:, kc, ti * 128:(ti + 1) * 128],
                                rhs=w_dn[:, kc, hf * 512:(hf + 1) * 512], start=(kc == 0), stop=(kc == NF - 1)),
                                reads=[B_wdn, B_yT[kc]], writes=[B_ps_d[di]])
                        P.op("dve", lambda v, di=di, xi=xi, ti=ti, hf=hf: v.tensor_tensor(
                            out=x2[:, xi, hf * 512:(hf + 1) * 512], in0=ps_d[di][:, :],
                            in1=x1s[:, ti, hf * 512:(hf + 1) * 512], op=ALU.add),
                            reads=[B_ps_d[di], B_x1s[ti]], writes=[B_x2[xi]])
                    rs, br = rms_rstd2(x2[:, xi, :], junk2[:], D, [B_x2[xi]], B_junk2)
                    P.op("dve", lambda v, xi=xi, rs=rs: v.scalar_tensor_tensor(
                        out=x2[:, xi, :], in0=x2[:, xi, :], scalar=rs, in1=nrmb2[:, D:2 * D], op0=ALU.mult,
                        op1=ALU.mult), reads=[B_x2[xi], br, B_par2], writes=[B_x2[xi]])
                    P.op("sp", lambda s, xi=xi, gt=gt: s.dma_start(out=out_d[gt * 128:(gt + 1) * 128, :], in_=x2[:, xi, :]),
                         reads=[B_x2[xi]], dma=True)

            for jb in range(nblk):
                B1(jb)
                B2(jb)
                B3(jb)
            P.wait_all("sp", [("d", k, 16 * P.dma_cnt[k]) for k in range(NDMA) if P.dma_cnt[k] > 0])
            P.emit()
    return nc


def _prep_inputs(inputs):
    f = lambda a: np.ascontiguousarray(np.asarray(a, dtype=np.float32))
    x = f(inputs["x"])
    shared = {
        "w_in": f(inputs["w_in"][0]),
        "w_out": f(inputs["w_out"][0]),
        "w_up": f(inputs["w_up"][0]),
        "w_down": f(inputs["w_down"][0]),
        "nrm": f(np.stack([np.asarray(inputs["attn_norm_w"][0]), np.asarray(inputs["ffn_norm_w"][0]),
                           np.asarray(inputs["final_norm_w"])], axis=0)),
        "mlnw": f(np.asarray(inputs["mlstm_norm_w"]).reshape(1, 512)),
        "dnw": f(np.asarray(inputs["diff_norm_w"]).reshape(1, 128)),
        "lamv": f(np.concatenate([np.asarray(inputs[k]).reshape(-1) for k in
                                  ("lambda_q1", "lambda_k1", "lambda_q2", "lambda_k2")]).reshape(1, 256)),
        "mcw": f(np.asarray(inputs["mlstm_conv_w"][0]).reshape(4, 8, 128).transpose(2, 1, 0).reshape(128, 32)),
        "mcb": f(np.asarray(inputs["mlstm_conv_b"][0]).reshape(8, 128).T),
        "gb": f(np.stack([np.asarray(inputs["mlstm_igate_b"][0]), np.asarray(inputs["mlstm_fgate_b"][0])], axis=1)),
        "fcw": f(np.asarray(inputs["ffn_conv_w"][0]).reshape(3, 44, 128).transpose(2, 1, 0).reshape(128, 132)),
        "fcb": f(np.asarray(inputs["ffn_conv_b"][0]).reshape(44, 128).T),
    }
    return x, shared


def kernel(**inputs):
    x, shared = _prep_inputs(inputs)
    nc = build_program()
    in_maps = [dict(shared, x=np.ascontiguousarray(x[b])) for b in range(8)]
    res = run_bass_kernel_spmd(nc, in_maps, core_ids=list(range(8)))
    out = np.stack([np.asarray(r["out"], dtype=np.float32) for r in res.results], axis=0)
    return out
```

```python
import math
from contextlib import ExitStack

import numpy as np
import concourse.bass as bass
import concourse.mybir as mybir
from concourse.bass_utils import run_bass_kernel_spmd

F32 = mybir.dt.float32
BF16 = mybir.dt.bfloat16
I32 = mybir.dt.int32
AF = mybir.ActivationFunctionType
ALU = mybir.AluOpType

S = 4096
D = 1024
NT = S // 128
BLK = 256
TPB = BLK // 128
NB = S // BLK
INC = 3592
DFF = 2816
NF = DFF // 128
EPS = 1e-6
LAM_INIT = 0.8 - 0.6 * math.exp(0.0)
SLOPES = [2.0 ** (-8.0 * (i + 1) / 4) for i in range(4)]
LN_SQRT_D = 0.5 * math.log(128.0)
CAP = 2000
NDMA = 24

DEBUG = {}


class Buf:
    __slots__ = ("name", "w", "r")

    def __init__(self, name):
        self.name = name
        self.w = None
        self.r = {}


class Op:
    __slots__ = ("fn", "deps", "sig", "dma", "sigidx")

    def __init__(self, fn, deps, dma):
        self.fn = fn
        self.deps = deps
        self.sig = False
        self.dma = dma
        self.sigidx = 0


class Prog:
    ENGS = ("pe", "act", "dve", "pool", "sp")

    def __init__(self, nc, es):
        self.nc = nc
        self.es = es
        self.ops = {e: [] for e in self.ENGS}
        self.dma_cnt = [0] * NDMA
        self.dma_rr = {e: 0 for e in self.ENGS}
        self.dsem = [es.enter_context(nc.semaphore(f"dsem{k}")) for k in range(NDMA)]
        self.esem = {e: [] for e in self.ENGS}
        self.sigtot = {e: 0 for e in self.ENGS}
        self.start = {e: 0 for e in self.ENGS}

    def op(self, eng, fn, reads=(), writes=(), dma=False):
        self.nrec = getattr(self, "nrec", 0) + 1
        if DEBUG.get("oplimit") and self.nrec > DEBUG["oplimit"]:
            return None
        deps = set()
        for b in reads:
            if b.w is not None:
                deps.add(b.w)
        for b in writes:
            if b.w is not None:
                deps.add(b.w)
            deps.update(b.r.values())
        deps = {d for d in deps if not (d[0] == "e" and d[2] < self.start[d[1]])}
        if eng == "pe":
            deps = {d for d in deps if not (d[0] == "e" and d[1] == "pe")}
        dmatok = None
        if dma:
            lo, n = (0, 8) if eng == "pool" else (8, NDMA - 8)
            k = lo + self.dma_rr[eng] % n
            self.dma_rr[eng] += 1
            self.dma_cnt[k] += 1
            val = 16 * self.dma_cnt[k]
            if self.dma_cnt[k] > 1:
                deps.add(("d", k, val - 16))
            dmatok = ("d", k, val)
        for d in deps:
            if d[0] == "e":
                self.ops[d[1]][d[2]].sig = True
        o = Op(fn, deps, dmatok)
        seq = len(self.ops[eng])
        self.ops[eng].append(o)
        tok = dmatok if dma else ("e", eng, seq)
        for b in reads:
            key = ("d", tok[1]) if dma else eng
            b.r[key] = tok
        for b in writes:
            b.w = tok
            b.r = {}
        return tok

    def wait_all(self, eng, toks):
        deps = set(toks)
        for d in deps:
            if d[0] == "e":
                self.ops[d[1]][d[2]].sig = True
        self.ops[eng].append(Op(None, deps, None))

    def emit(self):
        nc, es = self.nc, self.es
        for e in self.ENGS:
            c = self.sigtot[e]
            for o in self.ops[e][self.start[e]:]:
                if o.sig:
                    c += 1
                o.sigidx = c if o.sig else 0
            need = (c + CAP - 1) // CAP
            while len(self.esem[e]) < max(need, 1):
                self.esem[e].append(es.enter_context(nc.semaphore(f"es_{e}{len(self.esem[e])}")))
            self.sigtot[e] = c

        def resolve(d):
            if d[0] == "e":
                si = self.ops[d[1]][d[2]].sigidx
                assert si > 0, d
                return ("e", d[1], (si - 1) // CAP), self.esem[d[1]][(si - 1) // CAP], (si - 1) % CAP + 1
            return ("d", d[1]), self.dsem[d[1]], d[2]

        def run(eng, handle):
            waited = {}
            for o in self.ops[eng][self.start[eng]:]:
                for d in sorted(o.deps, key=str):
                    key, sem, val = resolve(d)
                    if waited.get(key, 0) >= val:
                        continue
                    handle.wait_ge(sem, val)
                    waited[key] = val
                if o.fn is None:
                    continue
                ins = o.fn(handle)
                if o.sig:
                    si = o.sigidx
                    ins.then_inc(self.esem[eng][(si - 1) // CAP], 1)
                if o.dma is not None:
                    ins.then_inc(self.dsem[o.dma[1]], 16)

        with nc.Block() as block:
            @block.tensor
            def _(h):
                run("pe", h)

            @block.scalar
            def _(h):
                run("act", h)

            @block.vector
            def _(h):
                run("dve", h)

            @block.gpsimd
            def _(h):
                run("pool", h)

            @block.sync
            def _(h):
                run("sp", h)
        for e in self.ENGS:
            self.start[e] = len(self.ops[e])


def interleave(A, M):
    out = []
    na, nm = len(A), len(M)
    if na == 0:
        return list(M)
    mi = 0
    for i, a in enumerate(A):
        out.append(a)
        tgt = ((i + 1) * nm) // na
        while mi < tgt:
            out.append(M[mi])
            mi += 1
    out.extend(M[mi:])
    return out


def build_program():
    nc = bass.Bass("TRN2", target_bir_lowering=False)
    dt = nc.dram_tensor
    x_d = dt("x", [S, D], F32, kind="ExternalInput").ap()
    w_in_d = dt("w_in", [D, INC], F32, kind="ExternalInput").ap()
    w_out_d = dt("w_out", [D, D], F32, kind="ExternalInput").ap()
    w_up_d = dt("w_up", [D, 2 * DFF], F32, kind="ExternalInput").ap()
    w_down_d = dt("w_down", [DFF, D], F32, kind="ExternalInput").ap()
    nrm_d = dt("nrm", [3, D], F32, kind="ExternalInput").ap()
    mlnw_d = dt("mlnw", [1, 512], F32, kind="ExternalInput").ap()
    dnw_d = dt("dnw", [1, 128], F32, kind="ExternalInput").ap()
    lamv_d = dt("lamv", [1, 256], F32, kind="ExternalInput").ap()
    mcw_d = dt("mcw", [128, 8 * 4], F32, kind="ExternalInput").ap()
    mcb_d = dt("mcb", [128, 8], F32, kind="ExternalInput").ap()
    gb_d = dt("gb", [4, 2], F32, kind="ExternalInput").ap()
    fcw_d = dt("fcw", [128, 44 * 3], F32, kind="ExternalInput").ap()
    fcb_d = dt("fcb", [128, 44], F32, kind="ExternalInput").ap()
    out_d = dt("out", [S, D], F32, kind="ExternalOutput").ap()
    x1_d = out_d

    with ExitStack() as es:
        P = Prog(nc, es)

        def sb(name, shape, dtype, ctx=es):
            return ctx.enter_context(nc.sbuf_tensor(name, shape, dtype))

        def ps(name, shape, dtype, ctx=es):
            return ctx.enter_context(nc.psum_tensor(name, shape, dtype))

        phase_sem = es.enter_context(nc.semaphore("phase"))

        ident = sb("ident", [128, 128], BF16)
        identf = sb("identf", [128, 128], F32)
        maskc = sb("maskc", [128, 128], BF16)
        mask2 = sb("mask2", [128, 128], F32)
        biasT = sb("biasT", [128, 4 * 32], F32)
        biasI = sb("biasI", [128, 32], I32)
        cst = sb("cst", [128, 16], F32)
        nrmb = sb("nrmb", [128, D], F32)
        B_const = Buf("const")

        NEGH = cst[:, 0:1]
        NLAM = cst[:, 2:3]

        def bc_rows(ap_row, n):
            return ap_row.partition_broadcast(128)

        def setup_consts():
            P.op("pool", lambda g: g.memset(cst[:], 0.0), writes=[B_const])
            P.op("pool", lambda g: g.memset(cst[:, 0:1], -0.5), writes=[B_const])
            for t, dtp in ((ident, BF16), (identf, F32)):
                P.op("pool", lambda g, t=t: g.memset(t[:], 1.0), writes=[B_const])
                P.op("pool", lambda g, t=t: g.affine_select(out=t[:], in_=t[:], pattern=[[1, 128]],
                                                            compare_op=ALU.is_ge, fill=0.0, base=0,
                                                            channel_multiplier=-1), reads=[B_const], writes=[B_const])
                P.op("pool", lambda g, t=t: g.affine_select(out=t[:], in_=t[:], pattern=[[-1, 128]],
                                                            compare_op=ALU.is_ge, fill=0.0, base=0,
                                                            channel_multiplier=1), reads=[B_const], writes=[B_const])
            for t in (maskc, mask2):
                P.op("pool", lambda g, t=t: g.memset(t[:], 1.0), writes=[B_const])
                P.op("pool", lambda g, t=t: g.affine_select(out=t[:], in_=t[:], pattern=[[1, 128]],
                                                            compare_op=ALU.is_ge, fill=0.0, base=0,
                                                            channel_multiplier=-1), reads=[B_const], writes=[B_const])
            P.op("pool", lambda g: g.iota(biasI[:], pattern=[[-128, 32]], base=-127, channel_multiplier=1),
                 writes=[B_const])
            for h in range(4):
                P.op("dve", lambda v, h=h: v.tensor_scalar(out=biasT[:, h * 32:(h + 1) * 32], in0=biasI[:],
                                                          scalar1=float(SLOPES[h]), scalar2=None, op0=ALU.mult),
                     reads=[B_const], writes=[B_const])
            P.op("sp", lambda s: s.dma_start(out=nrmb[:], in_=nrm_d[0, :].partition_broadcast(128)),
                 writes=[B_const], dma=True)

        setup_consts()

        with ExitStack() as ea:
            w_in = sb("w_in_sb", [128, 8, INC], BF16, ea)
            w_out = sb("w_out_sb", [128, 8, D], BF16, ea)
            kT = sb("kT", [128, 4, S], BF16, ea)
            vext = sb("vext", [128, NT, 4, 129], BF16, ea)
            xt = sb("xt", [128, TPB, D], F32, ea)
            xs = sb("xs", [128, 1, D], BF16, ea)
            hnT = sb("hnT", [128, 8, BLK], BF16, ea)
            qT0 = sb("qT0", [128, 4, BLK], BF16, ea)
            qT1 = sb("qT1", [128, 4, BLK], BF16, ea)
            mqkT = sb("mqkT", [128, 8, BLK], BF16, ea)
            pre = sb("pre", [128, 2, BLK + 3], F32, ea)
            acc = sb("acc", [128, 2, BLK], F32, ea)
            th = sb("th", [128, 1, 512], F32, ea)
            mk = sb("mk", [128, 2, 4, 128], BF16, ea)
            ve = sb("ve", [128, 2, 4, 129], BF16, ea)
            gate = sb("gate", [128, 1, 512], F32, ea)
            E = sb("E", [128, 2, 2, BLK], BF16, ea)
            mixed = sb("mixed", [128, TPB, D], BF16, ea)
            x1t = sb("x1t", [128, 2, 512], F32, ea)
            epiA = sb("epiA", [128, 2, 128], F32, ea)
            epiT = sb("epiT", [128, 2, 128], F32, ea)
            mlT = sb("mlT", [128, 2, 128], F32, ea)
            junk = sb("junk", [128, 2, 128], BF16, ea)
            sm = sb("sm", [128, 2, 128], BF16, ea)
            nwb_ml = sb("nwb_ml", [128, 512], F32, ea)
            dnw_b = sb("dnw_b", [128, 128], F32, ea)
            lamv = sb("lamv_sb", [128, 256], F32, ea)
            Cf = sb("Cf", [128, 4, 129], F32, ea)
            CbA = sb("CbA", [128, 4, 129], BF16, ea)
            halo = sb("halo", [128, 8, 3], F32, ea)
            cw = sb("cw", [128, 8, 4], F32, ea)
            cb = sb("cb", [128, 8], F32, ea)
            gbias = sb("gbias", [4, 4], F32, ea)
            g_i = sb("g_i", [4, BLK], F32, ea)
            g_f = sb("g_f", [4, BLK], F32, ea)
            g_B = sb("g_B", [4, BLK], F32, ea)
            g_G = sb("g_G", [4, BLK], F32, ea)
            g_e = sb("g_e", [4, 2, BLK], F32, ea)
            g_c = sb("g_c", [4, 8], F32, ea)
            g_d = sb("g_d", [4, 4], F32, ea)
            ones4 = sb("ones4", [4, 128], F32, ea)
            etok = sb("etok", [128, TPB, 8], F32, ea)
            sbc = sb("sbc", [128, 4], F32, ea)
            st = sb("st", [128, 64], F32, ea)

            psT_f = ps("psT_f", [128, 512], F32, ea)
            psT = psT_f[:].bitcast(BF16)
            ps_s = [ps(f"ps_s{i}", [128, 2, BLK], F32, ea) for i in range(2)]
            ps_o = [ps(f"ps_o{i}", [128, 512], F32, ea) for i in range(2)]
            ps_p = [ps(f"ps_p{i}", [128, 512], F32, ea) for i in range(2)]
            ps_m = ps("ps_m", [128, 512], F32, ea)

            B_win = Buf("w_in"); B_wout = Buf("w_out")
            B_kT = [Buf(f"kT{j}") for j in range(NB)]
            B_v = [Buf(f"v{t}") for t in range(NT)]
            B_xt = [Buf(f"xt{i}") for i in range(TPB)]
            B_xs = [Buf(f"xs{i}") for i in range(TPB)]
            B_hnT = [Buf(f"hnT{i}") for i in range(TPB)]
            B_qT = [Buf(f"qT{h}") for h in range(4)]
            B_mqk = [Buf(f"mqk{f}") for f in range(8)]
            B_pre = [Buf(f"pre{i}") for i in range(2)]
            B_acc = [Buf(f"acc{i}") for i in range(2)]
            B_th = [Buf(f"th{i}") for i in range(2)]
            B_mk = [Buf(f"mk{i}") for i in range(2)]
            B_ve = [Buf(f"ve{i}") for i in range(2)]
            B_gate = [Buf(f"gate{i}") for i in range(2)]
            B_E = [Buf(f"E{i}") for i in range(2)]
            B_veAB = Buf("veAB")
            B_mixed = [[Buf(f"mixed{i}_{c}") for c in range(8)] for i in range(TPB)]
            B_x1t = [Buf(f"x1t{i}") for i in range(2)]
            B_epiA = [Buf(f"epiA{i}") for i in range(2)]
            B_epiT = [Buf(f"epiT{i}") for i in range(2)]
            B_mlT = [Buf(f"mlT{i}") for i in range(2)]
            B_junk = [Buf(f"junk{i}") for i in range(2)]
            B_sm = [Buf(f"sm{i}") for i in range(2)]
            B_Cf = [Buf(f"Cf{h}") for h in range(4)]
            B_CbA = [Buf(f"CbA{h}") for h in range(4)]
            B_CbB = [Buf(f"CbB{h}") for h in range(4)]
            B_halo = [Buf(f"halo{f}") for f in range(8)]
            B_g = Buf("gates")
            B_etok = Buf("etok")
            B_sbc = Buf("sbc")
            B_st = [Buf(f"st{i}") for i in range(64)]
            B_psT = Buf("psT")
            B_ps_s = [Buf(f"ps_s{i}") for i in range(2)]
            B_ps_o = [Buf(f"ps_o{i}") for i in range(2)]
            B_ps_p = [Buf(f"ps_p{i}") for i in range(2)]
            B_psm_sc = Buf("psm"); B_psm_num = B_psm_sc
            B_psm_dc = [B_psm_sc, B_psm_sc]; B_psm_x = B_psm_sc
            PM_SC = ps_m[:, 0:128]
            PM_NUM = ps_m[:, 128:257]
            PM_DC = [ps_m[:, 257:386], ps_m[:, 386:515] if False else None]

            B_psm_dc[1] = B_psm_dc[0]
            PM_DC = [ps_m[:, 257:386], ps_m[:, 257:386]]
            PM_X = ps_m[:, 386:400]

            state = {"stc": 0, "pp": 0}

            def st_slot():
                i = state["stc"] % 64
                state["stc"] += 1
                return st[:, i:i + 1], B_st[i]

            def pp_next():
                i = state["pp"] % 2
                state["pp"] += 1
                return i

            half = INC // 2
            for kc in range(8):
                for hf in range(2):
                    P.op("pool", lambda g, kc=kc, hf=hf: g.dma_start(
                        out=w_in[:, kc, hf * half:(hf + 1) * half],
                        in_=w_in_d[kc * 128:(kc + 1) * 128, hf * half:(hf + 1) * half]),
                        writes=[B_win], dma=True)
            for kc in range(8):
                P.op("pool", lambda g, kc=kc: g.dma_start(out=w_out[:, kc, :], in_=w_out_d[kc * 128:(kc + 1) * 128, :]),
                     writes=[B_wout], dma=True)
            B_par = Buf("params")
            P.op("sp", lambda s: s.dma_start(out=nwb_ml[:], in_=mlnw_d.rearrange("a d -> (a d)").partition_broadcast(128)),
                 writes=[B_par], dma=True)
            P.op("sp", lambda s: s.dma_start(out=dnw_b[:], in_=dnw_d.rearrange("a d -> (a d)").partition_broadcast(128)),
                 writes=[B_par], dma=True)
            P.op("sp", lambda s: s.dma_start(out=lamv[:], in_=lamv_d.rearrange("a d -> (a d)").partition_broadcast(128)),
                 writes=[B_par], dma=True)
            P.op("sp", lambda s: s.dma_start(out=cw[:], in_=mcw_d.rearrange("p (f j) -> p f j", j=4)), writes=[B_par], dma=True)
            P.op("sp", lambda s: s.dma_start(out=cb[:], in_=mcb_d), writes=[B_par], dma=True)
            P.op("sp", lambda s: s.dma_start(out=gbias[:, 0:2], in_=gb_d), writes=[B_par], dma=True)
            P.op("dve", lambda v: v.tensor_scalar(out=nwb_ml[:], in0=nwb_ml[:], scalar1=0.5, scalar2=None, op0=ALU.mult),
                 reads=[B_par], writes=[B_par])
            P.op("dve", lambda v: v.tensor_scalar(out=dnw_b[:], in0=dnw_b[:], scalar1=float(1.0 - LAM_INIT), scalar2=None,
                                                  op0=ALU.mult), reads=[B_par], writes=[B_par])
            P.op("dve", lambda v: v.tensor_scalar(out=cw[:], in0=cw[:], scalar1=0.5, scalar2=None, op0=ALU.mult),
                 reads=[B_par], writes=[B_par])
            P.op("dve", lambda v: v.tensor_scalar(out=cb[:], in0=cb[:], scalar1=0.5, scalar2=None, op0=ALU.mult),
                 reads=[B_par], writes=[B_par])
            P.op("dve", lambda v: v.tensor_scalar(out=gbias[:, 2:3], in0=gbias[:, 1:2], scalar1=-1.0, scalar2=None,
                                                  op0=ALU.mult), reads=[B_par], writes=[B_par])
            P.op("dve", lambda v: v.tensor_tensor(out=lamv[:, 0:64], in0=lamv[:, 0:64], in1=lamv[:, 64:128], op=ALU.mult),
                 reads=[B_par], writes=[B_par])
            P.op("dve", lambda v: v.tensor_tensor(out=lamv[:, 128:192], in0=lamv[:, 128:192], in1=lamv[:, 192:256],
                                                  op=ALU.mult), reads=[B_par], writes=[B_par])
            P.op("dve", lambda v: v.reduce_sum(out=cst[:, 3:4], in_=lamv[:, 0:64], axis=mybir.AxisListType.X),
                 reads=[B_par, B_const], writes=[B_const])
            P.op("dve", lambda v: v.reduce_sum(out=cst[:, 4:5], in_=lamv[:, 128:192], axis=mybir.AxisListType.X),
                 reads=[B_par, B_const], writes=[B_const])
            P.op("act", lambda a: a.activation(out=cst[:, 5:7], in_=cst[:, 3:5], func=AF.Exp), reads=[B_const],
                 writes=[B_const])
            P.op("dve", lambda v: v.tensor_tensor(out=cst[:, 1:2], in0=cst[:, 5:6], in1=cst[:, 6:7], op=ALU.subtract),
                 reads=[B_const], writes=[B_const])
            P.op("dve", lambda v: v.tensor_scalar(out=cst[:, 1:2], in0=cst[:, 1:2], scalar1=float(LAM_INIT), scalar2=None,
                                                  op0=ALU.add), reads=[B_const], writes=[B_const])
            P.op("dve", lambda v: v.tensor_scalar(out=cst[:, 2:3], in0=cst[:, 1:2], scalar1=-1.0, scalar2=None,
                                                  op0=ALU.mult), reads=[B_const], writes=[B_const])
            P.op("pool", lambda g: g.memset(vext[:].rearrange("p a b c -> p (a b c)"), 1.0), writes=B_v)
            P.op("pool", lambda g: g.memset(Cf[:].rearrange("p a b -> p (a b)"), 0.0), writes=B_Cf)
            P.op("pool", lambda g: g.memset(CbA[:].rearrange("p a b -> p (a b)"), 0.0), writes=B_CbA)
            P.op("pool", lambda g: g.memset(halo[:].rearrange("p a b -> p (a b)"), 0.0), writes=B_halo)
            P.op("pool", lambda g: g.memset(g_c[:], 0.0), writes=[B_g])
            for tz in (qT0, qT1):
                P.op("pool", lambda g, tz=tz: g.memset(tz[:].rearrange("p a b -> p (a b)"), 0.0), writes=B_qT + [B_veAB])
            P.op("pool", lambda g: g.memset(ones4[:], 1.0), writes=[B_g])

            def rms_rstd(src_ap, junk_ap, n, reads, junk_buf):
                ssq, bs = st_slot()
                rs, br = st_slot()
                P.op("act", lambda a: a.activation(out=junk_ap, in_=src_ap, func=AF.Square, accum_out=ssq),
                     reads=reads, writes=[junk_buf, bs])
                P.op("pool", lambda g: g.tensor_scalar(out=ssq, in0=ssq, scalar1=1.0 / n, scalar2=EPS, op0=ALU.mult,
                                                       op1=ALU.add), reads=[bs], writes=[bs])
                P.op("pool", lambda g: g.tensor_tensor(out=rs, in0=ssq, in1=NEGH, op=ALU.pow), reads=[bs, B_const],
                     writes=[br])
                return rs, br

            def transposes_to(dst_ap, dst_bufs, src_fn, src_bufs, evac="act"):
                for kc in range(8):
                    P.op("pe", lambda t, kc=kc: t.transpose(out=psT[:, kc * 128:(kc + 1) * 128], in_=src_fn(kc),
                                                            identity=ident[:]),
                         reads=src_bufs + [B_const], writes=[B_psT])
                src = psT[:].rearrange("p (k t) -> p k t", k=8)
                if evac == "act":
                    P.op("act", lambda a: a.copy(out=dst_ap, in_=src), reads=[B_psT], writes=dst_bufs)
                else:
                    P.op("dve", lambda v: v.tensor_copy(out=dst_ap, in_=src), reads=[B_psT], writes=dst_bufs)

            def mm_group(out_ap, out_buf, pairs, reads):
                n = len(pairs)
                for i, (l, r) in enumerate(pairs):
                    P.op("pe", lambda t, l=l, r=r, i=i: t.matmul(out_ap, lhsT=l, rhs=r, start=(i == 0), stop=(i == n - 1)),
                         reads=reads, writes=[out_buf])

            def A1(jb):
                for ti in range(TPB):
                    gt = jb * TPB + ti
                    P.op("sp", lambda s, ti=ti, gt=gt: s.dma_start(out=xt[:, ti, :], in_=x_d[gt * 128:(gt + 1) * 128, :]),
                         writes=[B_xt[ti]], dma=True)
                    rs, br = rms_rstd(xt[:, ti, :], xs[:, 0, :], D, [B_xt[ti]], B_xs[0])
                    P.op("dve", lambda v, ti=ti, rs=rs: v.scalar_tensor_tensor(
                        out=xs[:, 0, :], in0=xt[:, ti, :], scalar=rs, in1=nrmb[:, 0:D], op0=ALU.mult, op1=ALU.mult),
                        reads=[B_xt[ti], br, B_const], writes=[B_xs[0]])
                    transposes_to(hnT[:, :, ti * 128:(ti + 1) * 128], [B_hnT[ti]],
                                  lambda kc, ti=ti: xs[:, 0, kc * 128:(kc + 1) * 128], [B_xs[0]])

            def proj_fm(col0, evac):
                i = pp_next()
                mm_group(ps_p[i][:, 0:BLK], B_ps_p[i],
                         [(w_in[:, kc, col0:col0 + 128], hnT[:, kc, :]) for kc in range(8)],
                         [B_win] + B_hnT)
                evac(ps_p[i][:, 0:BLK], B_ps_p[i])

            def proj_tm(ti, col0, ncols, evac):
                i = pp_next()
                mm_group(ps_p[i][:, 0:ncols], B_ps_p[i],
                         [(hnT[:, kc, ti * 128:(ti + 1) * 128], w_in[:, kc, col0:col0 + ncols]) for kc in range(8)],
                         [B_win, B_hnT[ti]])
                evac(ps_p[i][:, 0:ncols], B_ps_p[i])

            def A2_qkv(jb):
                for h in range(4):
                    def ev_q(p_ap, p_buf, h=h):
                        P.op("dve", lambda v: v.tensor_scalar(out=qT0[0:64, h, :], in0=p_ap[0:64, :], scalar1=0.125,
                                                              scalar2=None, op0=ALU.mult), reads=[p_buf], writes=[B_qT[h]])
                        P.op("dve", lambda v: v.tensor_scalar(out=qT1[64:128, h, :], in0=p_ap[64:128, :], scalar1=0.125,
                                                              scalar2=None, op0=ALU.mult), reads=[p_buf], writes=[B_qT[h]])
                    proj_fm(h * 128, ev_q)

                    def ev_k(p_ap, p_buf, h=h):
                        P.op("act", lambda a: a.copy(out=kT[:, h, jb * BLK:(jb + 1) * BLK], in_=p_ap), reads=[p_buf],
                             writes=[B_kT[jb]])
                    proj_fm(512 + h * 128, ev_k)
                for ti in range(TPB):
                    gt = jb * TPB + ti

                    def ev_v(p_ap, p_buf, gt=gt):
                        P.op("dve", lambda v: v.tensor_copy(out=vext[:, gt, :, 0:128],
                                                            in_=p_ap.rearrange("p (h d) -> p h d", h=4)),
                             reads=[p_buf], writes=[B_v[gt]])
                    proj_tm(ti, 1024, 512, ev_v)

            def attn_units(jb):
                units = []
                steps = []
                epis = []
                q0 = jb * BLK
                nk = 2 * jb + 2
                for h in range(4):
                    for kt in range(nk):
                        def step(h=h, kt=kt):
                            si = (h * nk + kt) % 2
                            ei = (h * nk + kt) % 2
                            diag = kt - 2 * jb
                            qlo = 128 if diag == 1 else 0
                            for c in range(2):
                                P.op("pe", lambda t, c=c: t.matmul(
                                    ps_s[si][:, c, qlo:BLK],
                                    lhsT=kT[:, h, kt * 128:(kt + 1) * 128],
                                    rhs=(qT0 if c == 0 else qT1)[:, h, qlo:BLK], start=True, stop=True),
                                    reads=[B_kT[kt // TPB], B_qT[h]], writes=[B_ps_s[si]])
                            yield
                            bi = (q0 - kt * 128) // 128 + 1
                            P.op("act", lambda a: a.activation(out=E[:, ei, :, qlo:BLK], in_=ps_s[si][:, :, qlo:BLK],
                                                               func=AF.Exp, bias=biasT[:, h * 32 + bi:h * 32 + bi + 1],
                                                               scale=1.0),
                                 reads=[B_ps_s[si], B_const], writes=[B_E[ei]])
                            if diag >= 0:
                                qs = diag * 128
                                P.op("pool", lambda g: g.tensor_tensor(
                                    out=E[:, ei, :, qs:qs + 128], in0=E[:, ei, :, qs:qs + 128],
                                    in1=maskc[:].unsqueeze(1).broadcast_to([128, 2, 128]), op=ALU.mult),
                                    reads=[B_E[ei], B_const], writes=[B_E[ei]])
                            yield
                            for qi in range(2):
                                if diag == 1 and qi == 0:
                                    continue
                                last = (kt == nk - 1) if qi == 1 else (kt == nk - 2)
                                for c in range(2):
                                    P.op("pe", lambda t, qi=qi, c=c, last=last: t.matmul(
                                        ps_o[qi][:, c * 129:(c + 1) * 129],
                                        lhsT=E[:, ei, c, qi * 128:(qi + 1) * 128],
                                        rhs=vext[:, kt, h, :], start=(kt == 0 and c == 0), stop=last,
                                        skip_group_check=True),
                                        reads=[B_E[ei], B_v[kt]], writes=[B_ps_o[qi]])
                        steps.append((step(), h, kt == nk - 1))

                    def epi(h=h):
                        for qi in range(2):
                            epi_q(h, qi)

                    def epi_q(h, qi):
                        if True:
                            po = ps_o[qi]
                            r2, b2 = st_slot()
                            r2b, b2b = st_slot()
                            nl, bn = st_slot()
                            P.op("dve", lambda v: v.reciprocal(out=r2, in_=po[:, 128:129]), reads=[B_ps_o[qi]], writes=[b2])
                            P.op("dve", lambda v: v.reciprocal(out=r2b, in_=po[:, 257:258]), reads=[B_ps_o[qi]],
                                 writes=[b2b])
                            P.op("dve", lambda v: v.tensor_tensor(out=nl, in0=r2b, in1=NLAM, op=ALU.mult),
                                 reads=[b2b, B_const], writes=[bn])
                            P.op("dve", lambda v: v.tensor_scalar(out=epiT[:, qi, :], in0=po[:, 129:257], scalar1=nl,
                                                                  scalar2=None, op0=ALU.mult),
                                 reads=[B_ps_o[qi], bn], writes=[B_epiT[qi]])
                            P.op("dve", lambda v: v.scalar_tensor_tensor(out=epiA[:, qi, :], in0=po[:, 0:128], scalar=r2,
                                                                         in1=epiT[:, qi, :], op0=ALU.mult, op1=ALU.add),
                                 reads=[B_ps_o[qi], b2, B_epiT[qi]], writes=[B_epiA[qi]])
                            rs, br = rms_rstd(epiA[:, qi, :], junk[:, qi, :], 128, [B_epiA[qi]], B_junk[qi])
                            P.op("dve", lambda v, rs=rs: v.scalar_tensor_tensor(
                                out=mixed[:, qi, h * 128:(h + 1) * 128], in0=epiA[:, qi, :], scalar=rs, in1=dnw_b[:],
                                op0=ALU.mult, op1=ALU.mult),
                                reads=[B_epiA[qi], br, B_par], writes=[B_mixed[qi][h]])
                    epis.append(epi)
                n = len(steps)
                units.append([steps[0][0]])
                for i in range(n):
                    u = []
                    if i + 1 < n:
                        u.append(steps[i + 1][0])
                    u.append(steps[i][0])
                    u.append(steps[i][0])
                    units.append(u)
                    if steps[i][2]:
                        units.append([epis[steps[i][1]]])
                return units

            def ml_units(jb):
                units = []
                m1s = []
                m3s = []
                m4s = [[] for _ in range(TPB)]
                for f in range(8):
                    def m1(f=f):
                        s_ = f % 2

                        def ev(p_ap, p_buf):
                            P.op("act", lambda a: a.copy(out=pre[:, s_, 3:BLK + 3], in_=p_ap), reads=[p_buf],
                                 writes=[B_pre[s_]])
                        P.op("pool", lambda g: g.tensor_copy(out=pre[:, s_, 0:3], in_=halo[:, f, :]), reads=[B_halo[f]],
                             writes=[B_pre[s_]])
                        proj_fm(1536 + f * 128, ev)
                        P.op("pool", lambda g: g.tensor_copy(out=halo[:, f, :], in_=pre[:, s_, BLK:BLK + 3]),
                             reads=[B_pre[s_]], writes=[B_halo[f]])
                        yield
                        P.op("dve", lambda v: v.tensor_scalar(out=acc[:, s_, :], in0=pre[:, s_, 3:BLK + 3],
                                                              scalar1=cw[:, f, 3:4], scalar2=cb[:, f:f + 1],
                                                              op0=ALU.mult, op1=ALU.add),
                             reads=[B_pre[s_], B_par], writes=[B_acc[s_]])
                        for j in (2, 1, 0):
                            P.op("dve", lambda v, j=j: v.scalar_tensor_tensor(
                                out=acc[:, s_, :], in0=pre[:, s_, j:j + BLK], scalar=cw[:, f, j:j + 1], in1=acc[:, s_, :],
                                op0=ALU.mult, op1=ALU.add), reads=[B_pre[s_], B_acc[s_], B_par], writes=[B_acc[s_]])
                        yield
                        P.op("act", lambda a: a.activation(out=th[:, 0, 0:BLK], in_=acc[:, s_, :], func=AF.Tanh),
                             reads=[B_acc[s_]], writes=[B_th[0]])
                        yield
                        P.op("dve", lambda v: v.scalar_tensor_tensor(out=mqkT[:, f, :], in0=th[:, 0, 0:BLK], scalar=1.0,
                                                                     in1=acc[:, s_, :], op0=ALU.add, op1=ALU.mult),
                             reads=[B_th[0], B_acc[s_]], writes=[B_mqk[f]])
                    m1s.append(m1())

                def m2():
                    ii = pp_next()
                    mm_group(ps_p[ii][0:4, 0:BLK], B_ps_p[ii],
                             [(w_in[:, kc, 3584:3588], hnT[:, kc, :]) for kc in range(8)], [B_win] + B_hnT)
                    P.op("act", lambda a: a.activation(out=g_i[:], in_=ps_p[ii][0:4, 0:BLK], func=AF.Identity,
                                                       bias=gbias[:, 0:1], scale=1.0),
                         reads=[B_ps_p[ii], B_par], writes=[B_g])
                    fi = pp_next()
                    mm_group(ps_p[fi][0:4, 0:BLK], B_ps_p[fi],
                             [(w_in[:, kc, 3588:3592], hnT[:, kc, :]) for kc in range(8)], [B_win] + B_hnT)
                    P.op("act", lambda a: a.activation(out=g_f[:], in_=ps_p[fi][0:4, 0:BLK], func=AF.Exp,
                                                       bias=gbias[:, 2:3], scale=-1.0),
                         reads=[B_ps_p[fi], B_par, B_g], writes=[B_g])
                    P.op("act", lambda a: a.activation(out=g_f[:], in_=g_f[:], func=AF.Ln, bias=1.0, scale=1.0),
                         reads=[B_g], writes=[B_g])
                    yield
                    P.op("dve", lambda v: v.tensor_tensor_scan(out=g_B[:], data0=g_f[:], data1=g_f[:],
                                                               initial=g_c[:, 0:1], op0=ALU.add, op1=ALU.max),
                         reads=[B_g], writes=[B_g])
                    P.op("dve", lambda v: v.tensor_copy(out=g_c[:, 0:1], in_=g_B[:, BLK - 1:BLK]), reads=[B_g], writes=[B_g])
                    P.op("dve", lambda v: v.tensor_tensor(out=g_i[:], in0=g_i[:], in1=g_B[:], op=ALU.add),
                         reads=[B_g], writes=[B_g])
                    P.op("dve", lambda v: v.tensor_tensor_scan(out=g_G[:], data0=g_i[:], data1=g_i[:],
                                                               initial=g_c[:, 1:2], op0=ALU.max, op1=ALU.max),
                         reads=[B_g], writes=[B_g])
                    yield
                    P.op("dve", lambda v: v.tensor_scalar(out=g_c[:, 2:3], in0=g_G[:, BLK - 1:BLK], scalar1=-1.0,
                                                          scalar2=None, op0=ALU.mult), reads=[B_g], writes=[B_g])
                    P.op("dve", lambda v: v.tensor_scalar(out=g_c[:, 4:5], in0=g_c[:, 2:3], scalar1=float(-LN_SQRT_D),
                                                          scalar2=None, op0=ALU.add), reads=[B_g], writes=[B_g])
                    P.op("act", lambda a: a.activation(out=g_c[:, 3:4], in_=g_c[:, 1:2], func=AF.Exp, bias=g_c[:, 2:3],
                                                       scale=1.0), reads=[B_g], writes=[B_g])
                    P.op("dve", lambda v: v.tensor_copy(out=g_c[:, 1:2], in_=g_G[:, BLK - 1:BLK]), reads=[B_g], writes=[B_g])
                    P.op("act", lambda a: a.activation(out=g_e[:, 0, :], in_=g_i[:], func=AF.Exp, bias=g_c[:, 4:5],
                                                       scale=1.0), reads=[B_g], writes=[B_g])
                    P.op("act", lambda a: a.activation(out=g_e[:, 1, :], in_=g_B[:], func=AF.Exp, bias=g_c[:, 2:3],
                                                       scale=1.0), reads=[B_g], writes=[B_g])
                    yield
                    for ti in range(TPB):
                        for w_ in range(2):
                            P.op("pe", lambda t, ti=ti, w_=w_: t.transpose(
                                out=PM_X[:, w_ * 4:(w_ + 1) * 4], in_=g_e[:, w_, ti * 128:(ti + 1) * 128],
                                identity=identf[0:4, 0:4]), reads=[B_g, B_const], writes=[B_psm_x])
                        P.op("dve", lambda v, ti=ti: v.tensor_copy(out=etok[:, ti, :], in_=PM_X[:, 0:8]),
                             reads=[B_psm_x], writes=[B_etok])
                    yield
                    P.op("dve", lambda v: v.tensor_scalar(out=g_d[:], in0=identf[0:4, 0:4], scalar1=g_c[:, 3:4],
                                                          scalar2=None, op0=ALU.mult), reads=[B_g, B_const], writes=[B_g])
                    P.op("pe", lambda t: t.matmul(PM_X[:, 8:12], lhsT=ones4[:], rhs=g_d[:], start=True, stop=True),
                         reads=[B_g], writes=[B_psm_x])
                    P.op("dve", lambda v: v.tensor_copy(out=sbc[:], in_=PM_X[:, 8:12]), reads=[B_psm_x], writes=[B_sbc])
                    for hh in range(4):
                        P.op("dve", lambda v, hh=hh: v.tensor_scalar(out=Cf[:, hh, :], in0=Cf[:, hh, :],
                                                                    scalar1=sbc[:, hh:hh + 1], scalar2=None,
                                                                    op0=ALU.mult), reads=[B_sbc, B_Cf[hh]], writes=[B_Cf[hh]])
                    P.op("pool", lambda g: g.tensor_copy(out=CbA[:], in_=Cf[:]), reads=B_Cf, writes=B_CbA)
                m2g = m2()

                for ti in range(TPB):
                    def m3(ti=ti):
                        sl = ti % 2

                        def ev_v(p_ap, p_buf):
                            for hh in range(4):
                                P.op("dve", lambda v, hh=hh: v.tensor_scalar(
                                    out=ve[:, sl, hh, 0:128], in0=p_ap[:, hh * 128:(hh + 1) * 128],
                                    scalar1=etok[:, ti, hh:hh + 1], scalar2=None, op0=ALU.mult),
                                    reads=[p_buf, B_etok], writes=[B_ve[sl]])
                                P.op("dve", lambda v, hh=hh: v.tensor_copy(out=ve[:, sl, hh, 128:129],
                                                                          in_=etok[:, ti, hh:hh + 1]),
                                     reads=[B_etok], writes=[B_ve[sl]])
                        proj_tm(ti, 2560, 512, ev_v)

                        yield

                        def ev_o(p_ap, p_buf):
                            P.op("act", lambda a: a.activation(out=th[:, 0, :], in_=p_ap, func=AF.Tanh, scale=0.5),
                                 reads=[p_buf], writes=[B_th[0]])
                            P.op("dve", lambda v: v.scalar_tensor_tensor(out=gate[:, 0, :], in0=th[:, 0, :], scalar=1.0,
                                                                         in1=nwb_ml[:], op0=ALU.add, op1=ALU.mult),
                                 reads=[B_th[0], B_par], writes=[B_gate[0]])
                        proj_tm(ti, 3072, 512, ev_o)
                        yield
                        for h in range(4):
                            P.op("pe", lambda t, h=h: t.transpose(out=psT[:, h * 128:(h + 1) * 128],
                                                                  in_=mqkT[:, 4 + h, ti * 128:(ti + 1) * 128],
                                                                  identity=ident[:]),
                                 reads=[B_mqk[4 + h], B_const], writes=[B_psT])
                        P.op("act", lambda a: a.copy(out=mk[:, sl, :, :], in_=psT[:, 0:512].rearrange("p (h d) -> p h d", h=4)),
                             reads=[B_psT], writes=[B_mk[sl]])
                    m3s.append(m3())
                    for h in range(4):
                        def m4(ti=ti, h=h):
                            sl = ti % 2
                            si = h % 2
                            tok = slice(ti * 128, (ti + 1) * 128)
                            bank, B_bank = (ps_m, B_psm_sc) if h % 2 == 0 else (psT_f, B_psT)
                            SC, NUM, DC = bank[:, 0:128], bank[:, 128:257], bank[:, 257:386]
                            P.op("pe", lambda t: t.matmul(SC, lhsT=mqkT[:, 4 + h, tok], rhs=mqkT[:, h, tok],
                                                          start=True, stop=True),
                                 reads=[B_mqk[4 + h], B_mqk[h]], writes=[B_bank])
                            yield
                            P.op("dve", lambda v: v.tensor_tensor(out=sm[:, si, :], in0=SC, in1=mask2[:], op=ALU.mult),
                                 reads=[B_bank, B_const], writes=[B_sm[si]])
                            yield
                            P.op("pe", lambda t: t.matmul(NUM, lhsT=sm[:, si, :], rhs=ve[:, sl, h, :], start=True,
                                                          stop=False),
                                 reads=[B_sm[si], B_ve[sl]], writes=[B_bank])
                            P.op("pe", lambda t: t.matmul(NUM, lhsT=mqkT[:, h, tok], rhs=CbA[:, h, :], start=False,
                                                          stop=True),
                                 reads=[B_mqk[h], B_CbA[h]], writes=[B_bank])
                            P.op("pe", lambda t: t.matmul(DC, lhsT=mk[:, sl, h, :], rhs=ve[:, sl, h, :], start=True,
                                                          stop=True, skip_group_check=True),
                                 reads=[B_mk[sl], B_ve[sl]], writes=[B_bank])
                            yield
                            P.op("dve", lambda v: v.tensor_tensor(out=Cf[:, h, :], in0=Cf[:, h, :], in1=DC, op=ALU.add),
                                 reads=[B_Cf[h], B_bank], writes=[B_Cf[h]])
                            P.op("pool", lambda g: g.tensor_copy(out=CbA[:, h, :], in_=Cf[:, h, :]), reads=[B_Cf[h]],
                                 writes=[B_CbA[h]])
                            dn, bd = st_slot()
                            r_, brr = st_slot()
                            t3, b3 = st_slot()
                            P.op("dve", lambda v: v.tensor_scalar(out=t3, in0=bank[:, 256:257], scalar1=-1.0, scalar2=None,
                                                                  op0=ALU.mult), reads=[B_bank], writes=[b3])
                            P.op("dve", lambda v: v.scalar_tensor_tensor(out=dn, in0=bank[:, 256:257], scalar=1.0, in1=t3,
                                                                         op0=ALU.mult, op1=ALU.max),
                                 reads=[B_bank, b3], writes=[bd])
                            P.op("dve", lambda v: v.tensor_scalar(out=dn, in0=dn, scalar1=etok[:, ti, 4 + h:5 + h],
                                                                  scalar2=None, op0=ALU.max),
                                 reads=[bd, B_etok], writes=[bd])
                            P.op("dve", lambda v: v.reciprocal(out=r_, in_=dn), reads=[bd], writes=[brr])
                            P.op("dve", lambda v: v.tensor_scalar(out=mlT[:, si, :], in0=bank[:, 128:256], scalar1=r_,
                                                                  scalar2=None, op0=ALU.mult),
                                 reads=[B_bank, brr], writes=[B_mlT[si]])
                            yield
                            rs, br = rms_rstd(mlT[:, si, :], junk[:, si, :], 128, [B_mlT[si]], B_junk[si])
                            yield
                            P.op("dve", lambda v: v.scalar_tensor_tensor(
                                out=mixed[:, ti, 512 + h * 128:512 + (h + 1) * 128], in0=mlT[:, si, :], scalar=rs,
                                in1=gate[:, 0, h * 128:(h + 1) * 128], op0=ALU.mult, op1=ALU.mult),
                                reads=[B_mlT[si], br, B_gate[0]], writes=[B_mixed[ti][4 + h]])
                        m4s[ti].append(m4())
                units += [m2g] * 5
                NS1 = 4
                for t in range(len(m1s) + NS1 - 1):
                    for st_ in reversed(range(NS1)):
                        u = t - st_
                        if 0 <= u < len(m1s):
                            units.append(m1s[u])
                for ti in range(TPB):
                    units += [m3s[ti]] * 3
                    for pr in range(2):
                        ga, gb = m4s[ti][2 * pr], m4s[ti][2 * pr + 1]
                        for _ in range(6):
                            units += [ga, gb]
                return units

            def out_proj(jb):
                for ti in range(TPB):
                    gt = jb * TPB + ti
                    transposes_to(hnT[:, :, ti * 128:(ti + 1) * 128], [B_hnT[ti]],
                                  lambda kc, ti=ti: mixed[:, ti, kc * 128:(kc + 1) * 128], B_mixed[ti], evac="act")
                    for hf in range(2):
                        i = pp_next()
                        xi = (ti * 2 + hf) % 2
                        mm_group(ps_p[i][:, :], B_ps_p[i],
                                 [(hnT[:, kc, ti * 128:(ti + 1) * 128], w_out[:, kc, hf * 512:(hf + 1) * 512])
                                  for kc in range(8)], [B_wout, B_hnT[ti]])
                        P.op("dve", lambda v, i=i, xi=xi, ti=ti, hf=hf: v.tensor_tensor(
                            out=x1t[:, xi, :], in0=ps_p[i][:, :], in1=xt[:, ti, hf * 512:(hf + 1) * 512], op=ALU.add),
                            reads=[B_ps_p[i], B_xt[ti]], writes=[B_x1t[xi]])
                        P.op("sp", lambda s, xi=xi, gt=gt, hf=hf: s.dma_start(
                            out=x1_d[gt * 128:(gt + 1) * 128, hf * 512:(hf + 1) * 512], in_=x1t[:, xi, :]),
                            reads=[B_x1t[xi]], dma=True)

            nblk = DEBUG.get("nblk", NB)
            stage = DEBUG.get("stage", 99)
            for jb in range(nblk):
                if stage >= 1:
                    A1(jb)
                if stage >= 2:
                    A2_qkv(jb)
                def adv(g):
                    if isinstance(g, list):
                        for x in g:
                            adv(x)
                    elif callable(g):
                        g()
                    else:
                        try:
                            next(g)
                        except StopIteration:
                            pass
                if stage >= 5:
                    au, mu = attn_units(jb), ml_units(jb)
                    for u in interleave(au, mu):
                        adv(u)
                    for u in au + mu:
                        for g in (u if isinstance(u, list) else [u]):
                            if not callable(g):
                                for _ in g:
                                    pass
                if stage >= 6:
                    out_proj(jb)

            if "dbg" in DEBUG:
                dbg_d = dt("dbg", [128, TPB * D], BF16, kind="ExternalOutput").ap()
                P.op("sp", lambda s: s.dma_start(out=dbg_d, in_=mixed[:].rearrange("p t d -> p (t d)")),
                     reads=[b for bb in B_mixed for b in bb], dma=True)

            P.wait_all("sp", [("d", k, 16 * P.dma_cnt[k]) for k in range(NDMA) if P.dma_cnt[k] > 0])
            for e in P.ENGS:
                last = {e2: len(P.ops[e2]) - 1 for e2 in P.ENGS if e2 != e and len(P.ops[e2]) > 0}
                toks = []
                for e2, s2 in last.items():
                    while s2 >= 0 and (P.ops[e2][s2].fn is None or P.ops[e2][s2].dma is not None):
                        s2 -= 1
                    if s2 >= 0:
                        toks.append(("e", e2, s2))
                toks += [("d", k, 16 * P.dma_cnt[k]) for k in range(NDMA) if P.dma_cnt[k] > 0]
                P.wait_all(e, toks)
            P.emit()

        if DEBUG.get("phaseA_only"):
            return nc
        with ExitStack() as eb:
            w_up = sb("w_up_sb", [128, 8, 2 * DFF], BF16, eb)
            w_dn = sb("w_dn_sb", [128, NF, D], BF16, eb)
            x1s = sb("x1s", [128, TPB, D], F32, eb)
            xs2 = sb("xs2", [128, TPB, D], BF16, eb)
            hn2 = sb("hn2", [128, 8, BLK], BF16, eb)
            yT = sb("yT", [128, NF, BLK], BF16, eb)
            pbuf = sb("pbuf", [128, 4, BLK + 2], F32, eb)
            acc2 = sb("acc2", [128, 4, BLK], F32, eb)
            sg = sb("sg", [128, 2, BLK], F32, eb)
            x2 = sb("x2", [128, 2, D], F32, eb)
            junk2 = sb("junk2", [128, D], BF16, eb)
            halo2 = sb("halo2", [128, 44, 2], F32, eb)
            fcw = sb("fcw_sb", [128, 44, 3], F32, eb)
            fcb = sb("fcb_sb", [128, 44], F32, eb)
            st2 = sb("st2", [128, 64], F32, eb)
            nrmb2 = sb("nrmb2", [128, 2 * D], F32, eb)
            psT2 = ps("psT2", [128, 1024], BF16, eb)
            ps_u = [ps(f"ps_u{i}", [128, 512], F32, eb) for i in range(4)]
            ps_d = [ps(f"ps_d{i}", [128, 512], F32, eb) for i in range(2)]

            B_wup = Buf("wup"); B_wdn = Buf("wdn"); B_par2 = Buf("par2")
            B_x1s = [Buf(f"x1s{i}") for i in range(TPB)]
            B_xs2 = [Buf(f"xs2{i}") for i in range(TPB)]
            B_hn2 = [Buf(f"hn2{i}") for i in range(TPB)]
            B_yT = [Buf(f"yT{f}") for f in range(NF)]
            B_pbuf = [Buf(f"pbuf{i}") for i in range(4)]
            B_acc2 = [Buf(f"acc2{i}") for i in range(4)]
            B_sg = [Buf(f"sg{i}") for i in range(2)]
            B_x2 = [Buf(f"x2{i}") for i in range(2)]
            B_junk2 = Buf("junk2")
            B_halo2 = [Buf(f"halo2{i}") for i in range(44)]
            B_st2 = [Buf(f"st2{i}") for i in range(64)]
            B_psT2 = Buf("psT2")
            B_ps_u = [Buf(f"ps_u{i}") for i in range(4)]
            B_ps_d = [Buf(f"ps_d{i}") for i in range(2)]
            stb = {"c": 0, "u": 0, "d": 0}

            def st2_slot():
                i = stb["c"] % 64
                stb["c"] += 1
                return st2[:, i:i + 1], B_st2[i]

            for kc in range(8):
                for q in range(4):
                    c0 = q * 1408
                    P.op("pool", lambda g, kc=kc, c0=c0: g.dma_start(out=w_up[:, kc, c0:c0 + 1408],
                                                                   in_=w_up_d[kc * 128:(kc + 1) * 128, c0:c0 + 1408]),
                         writes=[B_wup], dma=True)
            for kc in range(NF):
                P.op("pool", lambda g, kc=kc: g.dma_start(out=w_dn[:, kc, :], in_=w_down_d[kc * 128:(kc + 1) * 128, :]),
                     writes=[B_wdn], dma=True)
            P.op("sp", lambda s: s.dma_start(out=fcw[:], in_=fcw_d.rearrange("p (f j) -> p f j", j=3)), writes=[B_par2],
                 dma=True)
            P.op("sp", lambda s: s.dma_start(out=fcb[:], in_=fcb_d), writes=[B_par2], dma=True)
            P.op("sp", lambda s: s.dma_start(out=nrmb2[:, 0:D], in_=nrm_d[1, :].partition_broadcast(128)), writes=[B_par2], dma=True)
            P.op("sp", lambda s: s.dma_start(out=nrmb2[:, D:2 * D], in_=nrm_d[2, :].partition_broadcast(128)), writes=[B_par2], dma=True)
            P.op("pool", lambda g: g.memset(halo2[:].rearrange("p a b -> p (a b)"), 0.0), writes=B_halo2)

            def rms_rstd2(src_ap, junk_ap, n, reads, junk_buf):
                ssq, bs = st2_slot()
                rs, br = st2_slot()
                P.op("act", lambda a: a.activation(out=junk_ap, in_=src_ap, func=AF.Square, accum_out=ssq),
                     reads=reads, writes=[junk_buf, bs])
                P.op("pool", lambda g: g.tensor_scalar(out=ssq, in0=ssq, scalar1=1.0 / n, scalar2=EPS, op0=ALU.mult,
                                                       op1=ALU.add), reads=[bs], writes=[bs])
                P.op("pool", lambda g: g.tensor_tensor(out=rs, in0=ssq, in1=NEGH, op=ALU.pow), reads=[bs, B_const],
                     writes=[br])
                return rs, br

            def B1(jb):
                for ti in range(TPB):
                    gt = jb * TPB + ti
                    P.op("sp", lambda s, ti=ti, gt=gt: s.dma_start(out=x1s[:, ti, :], in_=x1_d[gt * 128:(gt + 1) * 128, :]),
                         writes=[B_x1s[ti]], dma=True)
                    rs, br = rms_rstd2(x1s[:, ti, :], xs2[:, ti, :], D, [B_x1s[ti]], B_xs2[ti])
                    P.op("dve", lambda v, ti=ti, rs=rs: v.scalar_tensor_tensor(
                        out=xs2[:, ti, :], in0=x1s[:, ti, :], scalar=rs, in1=nrmb2[:, 0:D], op0=ALU.mult, op1=ALU.mult),
                        reads=[B_x1s[ti], br, B_par2], writes=[B_xs2[ti]])
                    for kc in range(8):
                        P.op("pe", lambda t, kc=kc, ti=ti: t.transpose(out=psT2[:, kc * 128:(kc + 1) * 128],
                                                                       in_=xs2[:, ti, kc * 128:(kc + 1) * 128],
                                                                       identity=ident[:]),
                             reads=[B_xs2[ti], B_const], writes=[B_psT2])
                    P.op("act", lambda a, ti=ti: a.copy(out=hn2[:, :, ti * 128:(ti + 1) * 128],
                                                        in_=psT2[:].rearrange("p (k t) -> p k t", k=8)),
                         reads=[B_psT2], writes=[B_hn2[ti]])

            def B2(jb):
                for f in range(NF):
                    slots = []
                    for part in range(2):
                        tidx = part * NF + f
                        col0 = part * DFF + f * 128
                        ui = stb["u"] % 4
                        stb["u"] += 1
                        slots.append(ui)
                        n = 8
                        for kc in range(8):
                            P.op("pe", lambda t, kc=kc, ui=ui, col0=col0: t.matmul(
                                ps_u[ui][:, 0:BLK], lhsT=w_up[:, kc, col0:col0 + 128], rhs=hn2[:, kc, :],
                                start=(kc == 0), stop=(kc == 7)), reads=[B_wup] + B_hn2, writes=[B_ps_u[ui]])
                        P.op("pool", lambda g, ui=ui, tidx=tidx: g.tensor_copy(out=pbuf[:, ui, 0:2], in_=halo2[:, tidx, :]),
                             reads=[B_halo2[tidx]], writes=[B_pbuf[ui]])
                        P.op("act", lambda a, ui=ui: a.copy(out=pbuf[:, ui, 2:BLK + 2], in_=ps_u[ui][:, 0:BLK]),
                             reads=[B_ps_u[ui]], writes=[B_pbuf[ui]])
                        P.op("act", lambda a, ui=ui, tidx=tidx: a.activation(
                            out=acc2[:, ui, :], in_=ps_u[ui][:, 0:BLK], func=AF.Identity, bias=fcb[:, tidx:tidx + 1],
                            scale=fcw[:, tidx, 2:3]), reads=[B_ps_u[ui], B_par2], writes=[B_acc2[ui]])
                        P.op("pool", lambda g, ui=ui, tidx=tidx: g.tensor_copy(out=halo2[:, tidx, :],
                                                                               in_=pbuf[:, ui, BLK:BLK + 2]),
                             reads=[B_pbuf[ui]], writes=[B_halo2[tidx]])
                        for j in (1, 0):
                            P.op("dve", lambda v, ui=ui, tidx=tidx, j=j: v.scalar_tensor_tensor(
                                out=acc2[:, ui, :], in0=pbuf[:, ui, j:j + BLK], scalar=fcw[:, tidx, j:j + 1],
                                in1=acc2[:, ui, :], op0=ALU.mult, op1=ALU.add),
                                reads=[B_pbuf[ui], B_acc2[ui], B_par2], writes=[B_acc2[ui]])
                    ug, uu = slots
                    gi = f % 2
                    P.op("act", lambda a, ug=ug, gi=gi: a.activation(out=sg[:, gi, :], in_=acc2[:, ug, :], func=AF.Silu),
                         reads=[B_acc2[ug]], writes=[B_sg[gi]])
                    P.op("dve", lambda v, uu=uu, gi=gi, f=f: v.tensor_tensor(out=yT[:, f, :], in0=sg[:, gi, :],
                                                                           in1=acc2[:, uu, :], op=ALU.mult),
                         reads=[B_sg[gi], B_acc2[uu]], writes=[B_yT[f]])

            def B3(jb):
                for ti in range(TPB):
                    gt = jb * TPB + ti
                    xi = gt % 2
                    for hf in range(2):
                        di = stb["d"] % 2
                        stb["d"] += 1
                        for kc in range(NF):
                            P.op("pe", lambda t, kc=kc, di=di, ti=ti, hf=hf: t.matmul(
                                ps_d[di][:, :], lhsT=yT[:, kc, ti * 128:(ti + 1) * 128],
                                rhs=w_dn[:, kc, hf * 512:(hf + 1) * 512], start=(kc == 0), stop=(kc == NF - 1)),
                                reads=[B_wdn, B_yT[kc]], writes=[B_ps_d[di]])
                        P.op("dve", lambda v, di=di, xi=xi, ti=ti, hf=hf: v.tensor_tensor(
                            out=x2[:, xi, hf * 512:(hf + 1) * 512], in0=ps_d[di][:, :],
                            in1=x1s[:, ti, hf * 512:(hf + 1) * 512], op=ALU.add),
                            reads=[B_ps_d[di], B_x1s[ti]], writes=[B_x2[xi]])
                    rs, br = rms_rstd2(x2[:, xi, :], junk2[:], D, [B_x2[xi]], B_junk2)
                    P.op("dve", lambda v, xi=xi, rs=rs: v.scalar_tensor_tensor(
                        out=x2[:, xi, :], in0=x2[:, xi, :], scalar=rs, in1=nrmb2[:, D:2 * D], op0=ALU.mult,
                        op1=ALU.mult), reads=[B_x2[xi], br, B_par2], writes=[B_x2[xi]])
                    P.op("sp", lambda s, xi=xi, gt=gt: s.dma_start(out=out_d[gt * 128:(gt + 1) * 128, :], in_=x2[:, xi, :]),
                         reads=[B_x2[xi]], dma=True)

            for jb in range(nblk):
                B1(jb)
                B2(jb)
                B3(jb)
            P.wait_all("sp", [("d", k, 16 * P.dma_cnt[k]) for k in range(NDMA) if P.dma_cnt[k] > 0])
            P.emit()
    return nc


def _prep_inputs(inputs):
    f = lambda a: np.ascontiguousarray(np.asarray(a, dtype=np.float32))
    x = f(inputs["x"])
    shared = {
        "w_in": f(inputs["w_in"][0]),
        "w_out": f(inputs["w_out"][0]),
        "w_up": f(inputs["w_up"][0]),
        "w_down": f(inputs["w_down"][0]),
        "nrm": f(np.stack([np.asarray(inputs["attn_norm_w"][0]), np.asarray(inputs["ffn_norm_w"][0]),
                           np.asarray(inputs["final_norm_w"])], axis=0)),
        "mlnw": f(np.asarray(inputs["mlstm_norm_w"]).reshape(1, 512)),
        "dnw": f(np.asarray(inputs["diff_norm_w"]).reshape(1, 128)),
        "lamv": f(np.concatenate([np.asarray(inputs[k]).reshape(-1) for k in
                                  ("lambda_q1", "lambda_k1", "lambda_q2", "lambda_k2")]).reshape(1, 256)),
        "mcw": f(np.asarray(inputs["mlstm_conv_w"][0]).reshape(4, 8, 128).transpose(2, 1, 0).reshape(128, 32)),
        "mcb": f(np.asarray(inputs["mlstm_conv_b"][0]).reshape(8, 128).T),
        "gb": f(np.stack([np.asarray(inputs["mlstm_igate_b"][0]), np.asarray(inputs["mlstm_fgate_b"][0])], axis=1)),
        "fcw": f(np.asarray(inputs["ffn_conv_w"][0]).reshape(3, 44, 128).transpose(2, 1, 0).reshape(128, 132)),
        "fcb": f(np.asarray(inputs["ffn_conv_b"][0]).reshape(44, 128).T),
    }
    return x, shared


def kernel(**inputs):
    x, shared = _prep_inputs(inputs)
    nc = build_program()
    in_maps = [dict(shared, x=np.ascontiguousarray(x[b])) for b in range(8)]
    res = run_bass_kernel_spmd(nc, in_maps, core_ids=list(range(8)))
    out = np.stack([np.asarray(r["out"], dtype=np.float32) for r in res.results], axis=0)
    return out
```

```python
import math
from contextlib import ExitStack

import numpy as np
import concourse.bass as bass
import concourse.mybir as mybir
from concourse.bass_utils import run_bass_kernel_spmd

F32 = mybir.dt.float32
BF16 = mybir.dt.bfloat16
I32 = mybir.dt.int32
AF = mybir.ActivationFunctionType
ALU = mybir.AluOpType

S = 4096
D = 1024
NT = S // 128
BLK = 256
TPB = BLK // 128
NB = S // BLK
INC = 3592
DFF = 2816
NF = DFF // 128
EPS = 1e-6
LAM_INIT = 0.8 - 0.6 * math.exp(0.0)
SLOPES = [2.0 ** (-8.0 * (i + 1) / 4) for i in range(4)]
LN_SQRT_D = 0.5 * math.log(128.0)
CAP = 2000
NDMA = 24

DEBUG = {}


class Buf:
    __slots__ = ("name", "w", "r")

    def __init__(self, name):
        self.name = name
        self.w = None
        self.r = {}


class Op:
    __slots__ = ("fn", "deps", "sig", "dma", "sigidx")

    def __init__(self, fn, deps, dma):
        self.fn = fn
        self.deps = deps
        self.sig = False
        self.dma = dma
        self.sigidx = 0


class Prog:
    ENGS = ("pe", "act", "dve", "pool", "sp")

    def __init__(self, nc, es):
        self.nc = nc
        self.es = es
        self.ops = {e: [] for e in self.ENGS}
        self.dma_cnt = [0] * NDMA
        self.dma_rr = {e: 0 for e in self.ENGS}
        self.dsem = [es.enter_context(nc.semaphore(f"dsem{k}")) for k in range(NDMA)]
        self.esem = {e: [] for e in self.ENGS}
        self.sigtot = {e: 0 for e in self.ENGS}
        self.start = {e: 0 for e in self.ENGS}

    def op(self, eng, fn, reads=(), writes=(), dma=False):
        self.nrec = getattr(self, "nrec", 0) + 1
        if DEBUG.get("oplimit") and self.nrec > DEBUG["oplimit"]:
            return None
        deps = set()
        for b in reads:
            if b.w is not None:
                deps.add(b.w)
        for b in writes:
            if b.w is not None:
                deps.add(b.w)
            deps.update(b.r.values())
        deps = {d for d in deps if not (d[0] == "e" and d[2] < self.start[d[1]])}
        if eng == "pe":
            deps = {d for d in deps if not (d[0] == "e" and d[1] == "pe")}
        dmatok = None
        if dma:
            lo, n = (0, 8) if eng == "pool" else (8, NDMA - 8)
            k = lo + self.dma_rr[eng] % n
            self.dma_rr[eng] += 1
            self.dma_cnt[k] += 1
            val = 16 * self.dma_cnt[k]
            if self.dma_cnt[k] > 1:
                deps.add(("d", k, val - 16))
            dmatok = ("d", k, val)
        for d in deps:
            if d[0] == "e":
                self.ops[d[1]][d[2]].sig = True
        o = Op(fn, deps, dmatok)
        seq = len(self.ops[eng])
        self.ops[eng].append(o)
        tok = dmatok if dma else ("e", eng, seq)
        for b in reads:
            key = ("d", tok[1]) if dma else eng
            b.r[key] = tok
        for b in writes:
            b.w = tok
            b.r = {}
        return tok

    def wait_all(self, eng, toks):
        deps = set(toks)
        for d in deps:
            if d[0] == "e":
                self.ops[d[1]][d[2]].sig = True
        self.ops[eng].append(Op(None, deps, None))

    def emit(self):
        nc, es = self.nc, self.es
        for e in self.ENGS:
            c = self.sigtot[e]
            for o in self.ops[e][self.start[e]:]:
                if o.sig:
                    c += 1
                o.sigidx = c if o.sig else 0
            need = (c + CAP - 1) // CAP
            while len(self.esem[e]) < max(need, 1):
                self.esem[e].append(es.enter_context(nc.semaphore(f"es_{e}{len(self.esem[e])}")))
            self.sigtot[e] = c

        def resolve(d):
            if d[0] == "e":
                si = self.ops[d[1]][d[2]].sigidx
                assert si > 0, d
                return ("e", d[1], (si - 1) // CAP), self.esem[d[1]][(si - 1) // CAP], (si - 1) % CAP + 1
            return ("d", d[1]), self.dsem[d[1]], d[2]

        def run(eng, handle):
            waited = {}
            for o in self.ops[eng][self.start[eng]:]:
                for d in sorted(o.deps, key=str):
                    key, sem, val = resolve(d)
                    if waited.get(key, 0) >= val:
                        continue
                    handle.wait_ge(sem, val)
                    waited[key] = val
                if o.fn is None:
                    continue
                ins = o.fn(handle)
                if o.sig:
                    si = o.sigidx
                    ins.then_inc(self.esem[eng][(si - 1) // CAP], 1)
                if o.dma is not None:
                    ins.then_inc(self.dsem[o.dma[1]], 16)

        with nc.Block() as block:
            @block.tensor
            def _(h):
                run("pe", h)

            @block.scalar
            def _(h):
                run("act", h)

            @block.vector
            def _(h):
                run("dve", h)

            @block.gpsimd
            def _(h):
                run("pool", h)

            @block.sync
            def _(h):
                run("sp", h)
        for e in self.ENGS:
            self.start[e] = len(self.ops[e])


def interleave(A, M):
    out = []
    na, nm = len(A), len(M)
    if na == 0:
        return list(M)
    mi = 0
    for i, a in enumerate(A):
        out.append(a)
        tgt = ((i + 1) * nm) // na
        while mi < tgt:
            out.append(M[mi])
            mi += 1
    out.extend(M[mi:])
    return out


def build_program():
    nc = bass.Bass("TRN2", target_bir_lowering=False)
    dt = nc.dram_tensor
    x_d = dt("x", [S, D], F32, kind="ExternalInput").ap()
    w_in_d = dt("w_in", [D, INC], F32, kind="ExternalInput").ap()
    w_out_d = dt("w_out", [D, D], F32, kind="ExternalInput").ap()
    w_up_d = dt("w_up", [D, 2 * DFF], F32, kind="ExternalInput").ap()
    w_down_d = dt("w_down", [DFF, D], F32, kind="ExternalInput").ap()
    nrm_d = dt("nrm", [3, D], F32, kind="ExternalInput").ap()
    mlnw_d = dt("mlnw", [1, 512], F32, kind="ExternalInput").ap()
    dnw_d = dt("dnw", [1, 128], F32, kind="ExternalInput").ap()
    lamv_d = dt("lamv", [1, 256], F32, kind="ExternalInput").ap()
    mcw_d = dt("mcw", [128, 8 * 4], F32, kind="ExternalInput").ap()
    mcb_d = dt("mcb", [128, 8], F32, kind="ExternalInput").ap()
    gb_d = dt("gb", [4, 2], F32, kind="ExternalInput").ap()
    fcw_d = dt("fcw", [128, 44 * 3], F32, kind="ExternalInput").ap()
    fcb_d = dt("fcb", [128, 44], F32, kind="ExternalInput").ap()
    out_d = dt("out", [S, D], F32, kind="ExternalOutput").ap()
    x1_d = out_d

    with ExitStack() as es:
        P = Prog(nc, es)

        def sb(name, shape, dtype, ctx=es):
            return ctx.enter_context(nc.sbuf_tensor(name, shape, dtype))

        def ps(name, shape, dtype, ctx=es):
            return ctx.enter_context(nc.psum_tensor(name, shape, dtype))

        phase_sem = es.enter_context(nc.semaphore("phase"))

        ident = sb("ident", [128, 128], BF16)
        identf = sb("identf", [128, 128], F32)
        maskc = sb("maskc", [128, 128], BF16)
        mask2 = sb("mask2", [128, 128], F32)
        biasT = sb("biasT", [128, 4 * 32], F32)
        biasI = sb("biasI", [128, 32], I32)
        cst = sb("cst", [128, 16], F32)
        nrmb = sb("nrmb", [128, D], F32)
        B_const = Buf("const")

        NEGH = cst[:, 0:1]
        NLAM = cst[:, 2:3]

        def bc_rows(ap_row, n):
            return ap_row.partition_broadcast(128)

        def setup_consts():
            P.op("pool", lambda g: g.memset(cst[:], 0.0), writes=[B_const])
            P.op("pool", lambda g: g.memset(cst[:, 0:1], -0.5), writes=[B_const])
            for t, dtp in ((ident, BF16), (identf, F32)):
                P.op("pool", lambda g, t=t: g.memset(t[:], 1.0), writes=[B_const])
                P.op("pool", lambda g, t=t: g.affine_select(out=t[:], in_=t[:], pattern=[[1, 128]],
                                                            compare_op=ALU.is_ge, fill=0.0, base=0,
                                                            channel_multiplier=-1), reads=[B_const], writes=[B_const])
                P.op("pool", lambda g, t=t: g.affine_select(out=t[:], in_=t[:], pattern=[[-1, 128]],
                                                            compare_op=ALU.is_ge, fill=0.0, base=0,
                                                            channel_multiplier=1), reads=[B_const], writes=[B_const])
            for t in (maskc, mask2):
                P.op("pool", lambda g, t=t: g.memset(t[:], 1.0), writes=[B_const])
                P.op("pool", lambda g, t=t: g.affine_select(out=t[:], in_=t[:], pattern=[[1, 128]],
                                                            compare_op=ALU.is_ge, fill=0.0, base=0,
                                                            channel_multiplier=-1), reads=[B_const], writes=[B_const])
            P.op("pool", lambda g: g.iota(biasI[:], pattern=[[-128, 32]], base=-127, channel_multiplier=1),
                 writes=[B_const])
            for h in range(4):
                P.op("dve", lambda v, h=h: v.tensor_scalar(out=biasT[:, h * 32:(h + 1) * 32], in0=biasI[:],
                                                          scalar1=float(SLOPES[h]), scalar2=None, op0=ALU.mult),
                     reads=[B_const], writes=[B_const])
            P.op("sp", lambda s: s.dma_start(out=nrmb[:], in_=nrm_d[0, :].partition_broadcast(128)),
                 writes=[B_const], dma=True)

        setup_consts()

        with ExitStack() as ea:
            w_in = sb("w_in_sb", [128, 8, INC], BF16, ea)
            w_out = sb("w_out_sb", [128, 8, D], BF16, ea)
            kT = sb("kT", [128, 4, S], BF16, ea)
            vext = sb("vext", [128, NT, 4, 129], BF16, ea)
            xt = sb("xt", [128, TPB, D], F32, ea)
            xs = sb("xs", [128, 1, D], BF16, ea)
            hnT = sb("hnT", [128, 8, BLK], BF16, ea)
            qT0 = sb("qT0", [128, 4, BLK], BF16, ea)
            qT1 = sb("qT1", [128, 4, BLK], BF16, ea)
            mqkT = sb("mqkT", [128, 8, BLK], BF16, ea)
            pre = sb("pre", [128, 2, BLK + 3], F32, ea)
            acc = sb("acc", [128, 2, BLK], F32, ea)
            th = sb("th", [128, 1, 512], F32, ea)
            mk = sb("mk", [128, 2, 4, 128], BF16, ea)
            ve = sb("ve", [128, 2, 4, 129], BF16, ea)
            gate = sb("gate", [128, 2, 512], F32, ea)
            E = sb("E", [128, 2, 2, BLK], BF16, ea)
            mixed = sb("mixed", [128, TPB, D], BF16, ea)
            x1t = sb("x1t", [128, 2, 512], F32, ea)
            epiA = sb("epiA", [128, 2, 128], F32, ea)
            epiT = sb("epiT", [128, 2, 128], F32, ea)
            mlT = sb("mlT", [128, 2, 128], F32, ea)
            junk = sb("junk", [128, 2, 128], BF16, ea)
            sm = sb("sm", [128, 2, 128], BF16, ea)
            nwb_ml = sb("nwb_ml", [128, 512], F32, ea)
            dnw_b = sb("dnw_b", [128, 128], F32, ea)
            lamv = sb("lamv_sb", [128, 256], F32, ea)
            Cf = sb("Cf", [128, 4, 129], F32, ea)
            CbA = sb("CbA", [128, 4, 129], BF16, ea)
            halo = sb("halo", [128, 8, 3], F32, ea)
            cw = sb("cw", [128, 8, 4], F32, ea)
            cb = sb("cb", [128, 8], F32, ea)
            gbias = sb("gbias", [4, 4], F32, ea)
            g_i = sb("g_i", [4, BLK], F32, ea)
            g_f = sb("g_f", [4, BLK], F32, ea)
            g_B = sb("g_B", [4, BLK], F32, ea)
            g_G = sb("g_G", [4, BLK], F32, ea)
            g_e = sb("g_e", [4, 2, BLK], F32, ea)
            g_c = sb("g_c", [4, 8], F32, ea)
            g_d = sb("g_d", [4, 4], F32, ea)
            ones4 = sb("ones4", [4, 128], F32, ea)
            etok = sb("etok", [128, TPB, 8], F32, ea)
            sbc = sb("sbc", [128, 4], F32, ea)
            st = sb("st", [128, 64], F32, ea)

            psT_f = ps("psT_f", [128, 512], F32, ea)
            psT = psT_f[:].bitcast(BF16)
            ps_s = [ps(f"ps_s{i}", [128, 2, BLK], F32, ea) for i in range(2)]
            ps_o = [ps(f"ps_o{i}", [128, 512], F32, ea) for i in range(2)]
            ps_p = [ps(f"ps_p{i}", [128, 512], F32, ea) for i in range(2)]
            ps_m = ps("ps_m", [128, 512], F32, ea)

            B_win = Buf("w_in"); B_wout = Buf("w_out")
            B_kT = [Buf(f"kT{j}") for j in range(NB)]
            B_v = [Buf(f"v{t}") for t in range(NT)]
            B_xt = [Buf(f"xt{i}") for i in range(TPB)]
            B_xs = [Buf(f"xs{i}") for i in range(TPB)]
            B_hnT = [Buf(f"hnT{i}") for i in range(TPB)]
            B_qT = [Buf(f"qT{h}") for h in range(4)]
            B_mqk = [Buf(f"mqk{f}") for f in range(8)]
            B_pre = [Buf(f"pre{i}") for i in range(2)]
            B_acc = [Buf(f"acc{i}") for i in range(2)]
            B_th = [Buf(f"th{i}") for i in range(2)]
            B_mk = [Buf(f"mk{i}") for i in range(2)]
            B_ve = [Buf(f"ve{i}") for i in range(2)]
            B_gate = [Buf(f"gate{i}") for i in range(2)]
            B_E = [Buf(f"E{i}") for i in range(2)]
            B_veAB = Buf("veAB")
            B_mixed = [[Buf(f"mixed{i}_{c}") for c in range(8)] for i in range(TPB)]
            B_x1t = [Buf(f"x1t{i}") for i in range(2)]
            B_epiA = [Buf(f"epiA{i}") for i in range(2)]
            B_epiT = [Buf(f"epiT{i}") for i in range(2)]
            B_mlT = [Buf(f"mlT{i}") for i in range(2)]
            B_junk = [Buf(f"junk{i}") for i in range(2)]
            B_sm = [Buf(f"sm{i}") for i in range(2)]
            B_Cf = [Buf(f"Cf{h}") for h in range(4)]
            B_CbA = [Buf(f"CbA{h}") for h in range(4)]
            B_CbB = [Buf(f"CbB{h}") for h in range(4)]
            B_halo = [Buf(f"halo{f}") for f in range(8)]
            B_g = Buf("gates")
            B_etok = Buf("etok")
            B_sbc = Buf("sbc")
            B_st = [Buf(f"st{i}") for i in range(64)]
            B_psT = Buf("psT")
            B_ps_s = [Buf(f"ps_s{i}") for i in range(2)]
            B_ps_o = [Buf(f"ps_o{i}") for i in range(2)]
            B_ps_p = [Buf(f"ps_p{i}") for i in range(2)]
            B_psm_sc = Buf("psm"); B_psm_num = B_psm_sc
            B_psm_dc = [B_psm_sc, B_psm_sc]; B_psm_x = B_psm_sc
            PM_SC = ps_m[:, 0:128]
            PM_NUM = ps_m[:, 128:257]
            PM_DC = [ps_m[:, 257:386], ps_m[:, 386:515] if False else None]

            B_psm_dc[1] = B_psm_dc[0]
            PM_DC = [ps_m[:, 257:386], ps_m[:, 257:386]]
            PM_X = ps_m[:, 386:400]

            state = {"stc": 0, "pp": 0}

            def st_slot():
                i = state["stc"] % 64
                state["stc"] += 1
                return st[:, i:i + 1], B_st[i]

            def pp_next():
                i = state["pp"] % 2
                state["pp"] += 1
                return i

            half = INC // 2
            for kc in range(8):
                for hf in range(2):
                    P.op("pool", lambda g, kc=kc, hf=hf: g.dma_start(
                        out=w_in[:, kc, hf * half:(hf + 1) * half],
                        in_=w_in_d[kc * 128:(kc + 1) * 128, hf * half:(hf + 1) * half]),
                        writes=[B_win], dma=True)
            for kc in range(8):
                P.op("pool", lambda g, kc=kc: g.dma_start(out=w_out[:, kc, :], in_=w_out_d[kc * 128:(kc + 1) * 128, :]),
                     writes=[B_wout], dma=True)
            B_par = Buf("params")
            P.op("sp", lambda s: s.dma_start(out=nwb_ml[:], in_=mlnw_d.rearrange("a d -> (a d)").partition_broadcast(128)),
                 writes=[B_par], dma=True)
            P.op("sp", lambda s: s.dma_start(out=dnw_b[:], in_=dnw_d.rearrange("a d -> (a d)").partition_broadcast(128)),
                 writes=[B_par], dma=True)
            P.op("sp", lambda s: s.dma_start(out=lamv[:], in_=lamv_d.rearrange("a d -> (a d)").partition_broadcast(128)),
                 writes=[B_par], dma=True)
            P.op("sp", lambda s: s.dma_start(out=cw[:], in_=mcw_d.rearrange("p (f j) -> p f j", j=4)), writes=[B_par], dma=True)
            P.op("sp", lambda s: s.dma_start(out=cb[:], in_=mcb_d), writes=[B_par], dma=True)
            P.op("sp", lambda s: s.dma_start(out=gbias[:, 0:2], in_=gb_d), writes=[B_par], dma=True)
            P.op("dve", lambda v: v.tensor_scalar(out=nwb_ml[:], in0=nwb_ml[:], scalar1=0.5, scalar2=None, op0=ALU.mult),
                 reads=[B_par], writes=[B_par])
            P.op("dve", lambda v: v.tensor_scalar(out=dnw_b[:], in0=dnw_b[:], scalar1=float(1.0 - LAM_INIT), scalar2=None,
                                                  op0=ALU.mult), reads=[B_par], writes=[B_par])
            P.op("dve", lambda v: v.tensor_scalar(out=cw[:], in0=cw[:], scalar1=0.5, scalar2=None, op0=ALU.mult),
                 reads=[B_par], writes=[B_par])
            P.op("dve", lambda v: v.tensor_scalar(out=cb[:], in0=cb[:], scalar1=0.5, scalar2=None, op0=ALU.mult),
                 reads=[B_par], writes=[B_par])
            P.op("dve", lambda v: v.tensor_scalar(out=gbias[:, 2:3], in0=gbias[:, 1:2], scalar1=-1.0, scalar2=None,
                                                  op0=ALU.mult), reads=[B_par], writes=[B_par])
            P.op("dve", lambda v: v.tensor_tensor(out=lamv[:, 0:64], in0=lamv[:, 0:64], in1=lamv[:, 64:128], op=ALU.mult),
                 reads=[B_par], writes=[B_par])
            P.op("dve", lambda v: v.tensor_tensor(out=lamv[:, 128:192], in0=lamv[:, 128:192], in1=lamv[:, 192:256],
                                                  op=ALU.mult), reads=[B_par], writes=[B_par])
            P.op("dve", lambda v: v.reduce_sum(out=cst[:, 3:4], in_=lamv[:, 0:64], axis=mybir.AxisListType.X),
                 reads=[B_par, B_const], writes=[B_const])
            P.op("dve", lambda v: v.reduce_sum(out=cst[:, 4:5], in_=lamv[:, 128:192], axis=mybir.AxisListType.X),
                 reads=[B_par, B_const], writes=[B_const])
            P.op("act", lambda a: a.activation(out=cst[:, 5:7], in_=cst[:, 3:5], func=AF.Exp), reads=[B_const],
                 writes=[B_const])
            P.op("dve", lambda v: v.tensor_tensor(out=cst[:, 1:2], in0=cst[:, 5:6], in1=cst[:, 6:7], op=ALU.subtract),
                 reads=[B_const], writes=[B_const])
            P.op("dve", lambda v: v.tensor_scalar(out=cst[:, 1:2], in0=cst[:, 1:2], scalar1=float(LAM_INIT), scalar2=None,
                                                  op0=ALU.add), reads=[B_const], writes=[B_const])
            P.op("dve", lambda v: v.tensor_scalar(out=cst[:, 2:3], in0=cst[:, 1:2], scalar1=-1.0, scalar2=None,
                                                  op0=ALU.mult), reads=[B_const], writes=[B_const])
            P.op("pool", lambda g: g.memset(vext[:].rearrange("p a b c -> p (a b c)"), 1.0), writes=B_v)
            P.op("pool", lambda g: g.memset(Cf[:].rearrange("p a b -> p (a b)"), 0.0), writes=B_Cf)
            P.op("pool", lambda g: g.memset(CbA[:].rearrange("p a b -> p (a b)"), 0.0), writes=B_CbA)
            P.op("pool", lambda g: g.memset(halo[:].rearrange("p a b -> p (a b)"), 0.0), writes=B_halo)
            P.op("pool", lambda g: g.memset(g_c[:], 0.0), writes=[B_g])
            for tz in (qT0, qT1):
                P.op("pool", lambda g, tz=tz: g.memset(tz[:].rearrange("p a b -> p (a b)"), 0.0), writes=B_qT + [B_veAB])
            P.op("pool", lambda g: g.memset(ones4[:], 1.0), writes=[B_g])

            def rms_rstd(src_ap, junk_ap, n, reads, junk_buf):
                ssq, bs = st_slot()
                rs, br = st_slot()
                P.op("act", lambda a: a.activation(out=junk_ap, in_=src_ap, func=AF.Square, accum_out=ssq),
                     reads=reads, writes=[junk_buf, bs])
                P.op("pool", lambda g: g.tensor_scalar(out=ssq, in0=ssq, scalar1=1.0 / n, scalar2=EPS, op0=ALU.mult,
                                                       op1=ALU.add), reads=[bs], writes=[bs])
                P.op("pool", lambda g: g.tensor_tensor(out=rs, in0=ssq, in1=NEGH, op=ALU.pow), reads=[bs, B_const],
                     writes=[br])
                return rs, br

            def transposes_to(dst_ap, dst_bufs, src_fn, src_bufs, evac="act"):
                for kc in range(8):
                    P.op("pe", lambda t, kc=kc: t.transpose(out=psT[:, kc * 128:(kc + 1) * 128], in_=src_fn(kc),
                                                            identity=ident[:]),
                         reads=src_bufs + [B_const], writes=[B_psT])
                src = psT[:].rearrange("p (k t) -> p k t", k=8)
                if evac == "act":
                    P.op("act", lambda a: a.copy(out=dst_ap, in_=src), reads=[B_psT], writes=dst_bufs)
                else:
                    P.op("dve", lambda v: v.tensor_copy(out=dst_ap, in_=src), reads=[B_psT], writes=dst_bufs)

            def mm_group(out_ap, out_buf, pairs, reads):
                n = len(pairs)
                for i, (l, r) in enumerate(pairs):
                    P.op("pe", lambda t, l=l, r=r, i=i: t.matmul(out_ap, lhsT=l, rhs=r, start=(i == 0), stop=(i == n - 1)),
                         reads=reads, writes=[out_buf])

            def A1(jb):
                for ti in range(TPB):
                    gt = jb * TPB + ti
                    P.op("sp", lambda s, ti=ti, gt=gt: s.dma_start(out=xt[:, ti, :], in_=x_d[gt * 128:(gt + 1) * 128, :]),
                         writes=[B_xt[ti]], dma=True)
                    rs, br = rms_rstd(xt[:, ti, :], xs[:, 0, :], D, [B_xt[ti]], B_xs[0])
                    P.op("dve", lambda v, ti=ti, rs=rs: v.scalar_tensor_tensor(
                        out=xs[:, 0, :], in0=xt[:, ti, :], scalar=rs, in1=nrmb[:, 0:D], op0=ALU.mult, op1=ALU.mult),
                        reads=[B_xt[ti], br, B_const], writes=[B_xs[0]])
                    transposes_to(hnT[:, :, ti * 128:(ti + 1) * 128], [B_hnT[ti]],
                                  lambda kc, ti=ti: xs[:, 0, kc * 128:(kc + 1) * 128], [B_xs[0]])

            def proj_fm(col0, evac):
                i = pp_next()
                mm_group(ps_p[i][:, 0:BLK], B_ps_p[i],
                         [(w_in[:, kc, col0:col0 + 128], hnT[:, kc, :]) for kc in range(8)],
                         [B_win] + B_hnT)
                evac(ps_p[i][:, 0:BLK], B_ps_p[i])

            def proj_tm(ti, col0, ncols, evac):
                i = pp_next()
                mm_group(ps_p[i][:, 0:ncols], B_ps_p[i],
                         [(hnT[:, kc, ti * 128:(ti + 1) * 128], w_in[:, kc, col0:col0 + ncols]) for kc in range(8)],
                         [B_win, B_hnT[ti]])
                evac(ps_p[i][:, 0:ncols], B_ps_p[i])

            def A2_qkv(jb):
                for h in range(4):
                    def ev_q(p_ap, p_buf, h=h):
                        P.op("dve", lambda v: v.tensor_scalar(out=qT0[0:64, h, :], in0=p_ap[0:64, :], scalar1=0.125,
                                                              scalar2=None, op0=ALU.mult), reads=[p_buf], writes=[B_qT[h]])
                        P.op("dve", lambda v: v.tensor_scalar(out=qT1[64:128, h, :], in0=p_ap[64:128, :], scalar1=0.125,
                                                              scalar2=None, op0=ALU.mult), reads=[p_buf], writes=[B_qT[h]])
                    proj_fm(h * 128, ev_q)
                    yield

                    def ev_k(p_ap, p_buf, h=h):
                        P.op("act", lambda a: a.copy(out=kT[:, h, jb * BLK:(jb + 1) * BLK], in_=p_ap), reads=[p_buf],
                             writes=[B_kT[jb]])
                    proj_fm(512 + h * 128, ev_k)
                    yield
                for ti in range(TPB):
                    gt = jb * TPB + ti

                    def ev_v(p_ap, p_buf, gt=gt):
                        P.op("dve", lambda v: v.tensor_copy(out=vext[:, gt, :, 0:128],
                                                            in_=p_ap.rearrange("p (h d) -> p h d", h=4)),
                             reads=[p_buf], writes=[B_v[gt]])
                    proj_tm(ti, 1024, 512, ev_v)
                    yield

            def attn_units(jb):
                units = []
                steps = []
                epis = []
                q0 = jb * BLK
                nk = 2 * jb + 2
                for h in range(4):
                    for kt in range(nk):
                        def step(h=h, kt=kt):
                            si = (h * nk + kt) % 2
                            ei = (h * nk + kt) % 2
                            diag = kt - 2 * jb
                            qlo = 128 if diag == 1 else 0
                            for c in range(2):
                                P.op("pe", lambda t, c=c: t.matmul(
                                    ps_s[si][:, c, qlo:BLK],
                                    lhsT=kT[:, h, kt * 128:(kt + 1) * 128],
                                    rhs=(qT0 if c == 0 else qT1)[:, h, qlo:BLK], start=True, stop=True),
                                    reads=[B_kT[kt // TPB], B_qT[h]], writes=[B_ps_s[si]])
                            yield
                            bi = (q0 - kt * 128) // 128 + 1
                            P.op("act", lambda a: a.activation(out=E[:, ei, :, qlo:BLK], in_=ps_s[si][:, :, qlo:BLK],
                                                               func=AF.Exp, bias=biasT[:, h * 32 + bi:h * 32 + bi + 1],
                                                               scale=1.0),
                                 reads=[B_ps_s[si], B_const], writes=[B_E[ei]])
                            if diag >= 0:
                                qs = diag * 128
                                P.op("pool", lambda g: g.tensor_tensor(
                                    out=E[:, ei, :, qs:qs + 128], in0=E[:, ei, :, qs:qs + 128],
                                    in1=maskc[:].unsqueeze(1).broadcast_to([128, 2, 128]), op=ALU.mult),
                                    reads=[B_E[ei], B_const], writes=[B_E[ei]])
                            yield
                            for qi in range(2):
                                if diag == 1 and qi == 0:
                                    continue
                                last = (kt == nk - 1) if qi == 1 else (kt == nk - 2)
                                for c in range(2):
                                    P.op("pe", lambda t, qi=qi, c=c, last=last: t.matmul(
                                        ps_o[qi][:, c * 129:(c + 1) * 129],
                                        lhsT=E[:, ei, c, qi * 128:(qi + 1) * 128],
                                        rhs=vext[:, kt, h, :], start=(kt == 0 and c == 0), stop=last,
                                        skip_group_check=True),
                                        reads=[B_E[ei], B_v[kt]], writes=[B_ps_o[qi]])
                        steps.append((step(), h, kt == nk - 1))

                    def epi(h=h):
                        for qi in range(2):
                            epi_q(h, qi)

                    def epi_q(h, qi):
                        if True:
                            po = ps_o[qi]
                            r2, b2 = st_slot()
                            r2b, b2b = st_slot()
                            nl, bn = st_slot()
                            P.op("dve", lambda v: v.reciprocal(out=r2, in_=po[:, 128:129]), reads=[B_ps_o[qi]], writes=[b2])
                            P.op("dve", lambda v: v.reciprocal(out=r2b, in_=po[:, 257:258]), reads=[B_ps_o[qi]],
                                 writes=[b2b])
                            P.op("dve", lambda v: v.tensor_tensor(out=nl, in0=r2b, in1=NLAM, op=ALU.mult),
                                 reads=[b2b, B_const], writes=[bn])
                            P.op("dve", lambda v: v.tensor_scalar(out=epiT[:, qi, :], in0=po[:, 129:257], scalar1=nl,
                                                                  scalar2=None, op0=ALU.mult),
                                 reads=[B_ps_o[qi], bn], writes=[B_epiT[qi]])
                            P.op("dve", lambda v: v.scalar_tensor_tensor(out=epiA[:, qi, :], in0=po[:, 0:128], scalar=r2,
                                                                         in1=epiT[:, qi, :], op0=ALU.mult, op1=ALU.add),
                                 reads=[B_ps_o[qi], b2, B_epiT[qi]], writes=[B_epiA[qi]])
                            rs, br = rms_rstd(epiA[:, qi, :], junk[:, qi, :], 128, [B_epiA[qi]], B_junk[qi])
                            P.op("dve", lambda v, rs=rs: v.scalar_tensor_tensor(
                                out=mixed[:, qi, h * 128:(h + 1) * 128], in0=epiA[:, qi, :], scalar=rs, in1=dnw_b[:],
                                op0=ALU.mult, op1=ALU.mult),
                                reads=[B_epiA[qi], br, B_par], writes=[B_mixed[qi][h]])
                    epis.append(epi)
                n = len(steps)
                units.append([steps[0][0]])
                for i in range(n):
                    u = []
                    if i + 1 < n:
                        u.append(steps[i + 1][0])
                    u.append(steps[i][0])
                    u.append(steps[i][0])
                    units.append(u)
                    if steps[i][2]:
                        units.append([epis[steps[i][1]]])
                return units

            def ml_units(jb):
                units = []
                m1s = []
                m3s = []
                m4s = [[] for _ in range(TPB)]
                for f in range(8):
                    def m1(f=f):
                        s_ = f % 2

                        def ev(p_ap, p_buf):
                            P.op("act", lambda a: a.copy(out=pre[:, s_, 3:BLK + 3], in_=p_ap), reads=[p_buf],
                                 writes=[B_pre[s_]])
                        P.op("pool", lambda g: g.tensor_copy(out=pre[:, s_, 0:3], in_=halo[:, f, :]), reads=[B_halo[f]],
                             writes=[B_pre[s_]])
                        proj_fm(1536 + f * 128, ev)
                        P.op("pool", lambda g: g.tensor_copy(out=halo[:, f, :], in_=pre[:, s_, BLK:BLK + 3]),
                             reads=[B_pre[s_]], writes=[B_halo[f]])
                        yield
                        P.op("dve", lambda v: v.tensor_scalar(out=acc[:, s_, :], in0=pre[:, s_, 3:BLK + 3],
                                                              scalar1=cw[:, f, 3:4], scalar2=cb[:, f:f + 1],
                                                              op0=ALU.mult, op1=ALU.add),
                             reads=[B_pre[s_], B_par], writes=[B_acc[s_]])
                        for j in (2, 1, 0):
                            P.op("dve", lambda v, j=j: v.scalar_tensor_tensor(
                                out=acc[:, s_, :], in0=pre[:, s_, j:j + BLK], scalar=cw[:, f, j:j + 1], in1=acc[:, s_, :],
                                op0=ALU.mult, op1=ALU.add), reads=[B_pre[s_], B_acc[s_], B_par], writes=[B_acc[s_]])
                        yield
                        P.op("act", lambda a: a.activation(out=th[:, 0, 0:BLK], in_=acc[:, s_, :], func=AF.Tanh),
                             reads=[B_acc[s_]], writes=[B_th[0]])
                        yield
                        P.op("dve", lambda v: v.scalar_tensor_tensor(out=mqkT[:, f, :], in0=th[:, 0, 0:BLK], scalar=1.0,
                                                                     in1=acc[:, s_, :], op0=ALU.add, op1=ALU.mult),
                             reads=[B_th[0], B_acc[s_]], writes=[B_mqk[f]])
                    m1s.append(m1())

                def m2():
                    ii = pp_next()
                    mm_group(ps_p[ii][0:4, 0:BLK], B_ps_p[ii],
                             [(w_in[:, kc, 3584:3588], hnT[:, kc, :]) for kc in range(8)], [B_win] + B_hnT)
                    P.op("act", lambda a: a.activation(out=g_i[:], in_=ps_p[ii][0:4, 0:BLK], func=AF.Identity,
                                                       bias=gbias[:, 0:1], scale=1.0),
                         reads=[B_ps_p[ii], B_par], writes=[B_g])
                    fi = pp_next()
                    mm_group(ps_p[fi][0:4, 0:BLK], B_ps_p[fi],
                             [(w_in[:, kc, 3588:3592], hnT[:, kc, :]) for kc in range(8)], [B_win] + B_hnT)
                    P.op("act", lambda a: a.activation(out=g_f[:], in_=ps_p[fi][0:4, 0:BLK], func=AF.Exp,
                                                       bias=gbias[:, 2:3], scale=-1.0),
                         reads=[B_ps_p[fi], B_par, B_g], writes=[B_g])
                    P.op("act", lambda a: a.activation(out=g_f[:], in_=g_f[:], func=AF.Ln, bias=1.0, scale=1.0),
                         reads=[B_g], writes=[B_g])
                    yield
                    P.op("dve", lambda v: v.tensor_tensor_scan(out=g_B[:], data0=g_f[:], data1=g_f[:],
                                                               initial=g_c[:, 0:1], op0=ALU.add, op1=ALU.max),
                         reads=[B_g], writes=[B_g])
                    P.op("dve", lambda v: v.tensor_copy(out=g_c[:, 0:1], in_=g_B[:, BLK - 1:BLK]), reads=[B_g], writes=[B_g])
                    P.op("dve", lambda v: v.tensor_tensor(out=g_i[:], in0=g_i[:], in1=g_B[:], op=ALU.add),
                         reads=[B_g], writes=[B_g])
                    P.op("dve", lambda v: v.tensor_tensor_scan(out=g_G[:], data0=g_i[:], data1=g_i[:],
                                                               initial=g_c[:, 1:2], op0=ALU.max, op1=ALU.max),
                         reads=[B_g], writes=[B_g])
                    yield
                    P.op("dve", lambda v: v.tensor_scalar(out=g_c[:, 2:3], in0=g_G[:, BLK - 1:BLK], scalar1=-1.0,
                                                          scalar2=None, op0=ALU.mult), reads=[B_g], writes=[B_g])
                    P.op("dve", lambda v: v.tensor_scalar(out=g_c[:, 4:5], in0=g_c[:, 2:3], scalar1=float(-LN_SQRT_D),
                                                          scalar2=None, op0=ALU.add), reads=[B_g], writes=[B_g])
                    P.op("act", lambda a: a.activation(out=g_c[:, 3:4], in_=g_c[:, 1:2], func=AF.Exp, bias=g_c[:, 2:3],
                                                       scale=1.0), reads=[B_g], writes=[B_g])
                    P.op("dve", lambda v: v.tensor_copy(out=g_c[:, 1:2], in_=g_G[:, BLK - 1:BLK]), reads=[B_g], writes=[B_g])
                    P.op("act", lambda a: a.activation(out=g_e[:, 0, :], in_=g_i[:], func=AF.Exp, bias=g_c[:, 4:5],
                                                       scale=1.0), reads=[B_g], writes=[B_g])
                    P.op("act", lambda a: a.activation(out=g_e[:, 1, :], in_=g_B[:], func=AF.Exp, bias=g_c[:, 2:3],
                                                       scale=1.0), reads=[B_g], writes=[B_g])
                    yield
                    for ti in range(TPB):
                        for w_ in range(2):
                            P.op("pe", lambda t, ti=ti, w_=w_: t.transpose(
                                out=PM_X[:, w_ * 4:(w_ + 1) * 4], in_=g_e[:, w_, ti * 128:(ti + 1) * 128],
                                identity=identf[0:4, 0:4]), reads=[B_g, B_const], writes=[B_psm_x])
                        P.op("dve", lambda v, ti=ti: v.tensor_copy(out=etok[:, ti, :], in_=PM_X[:, 0:8]),
                             reads=[B_psm_x], writes=[B_etok])
                    yield
                    P.op("dve", lambda v: v.tensor_scalar(out=g_d[:], in0=identf[0:4, 0:4], scalar1=g_c[:, 3:4],
                                                          scalar2=None, op0=ALU.mult), reads=[B_g, B_const], writes=[B_g])
                    P.op("pe", lambda t: t.matmul(PM_X[:, 8:12], lhsT=ones4[:], rhs=g_d[:], start=True, stop=True),
                         reads=[B_g], writes=[B_psm_x])
                    P.op("dve", lambda v: v.tensor_copy(out=sbc[:], in_=PM_X[:, 8:12]), reads=[B_psm_x], writes=[B_sbc])
                    for hh in range(4):
                        P.op("dve", lambda v, hh=hh: v.tensor_scalar(out=Cf[:, hh, :], in0=Cf[:, hh, :],
                                                                    scalar1=sbc[:, hh:hh + 1], scalar2=None,
                                                                    op0=ALU.mult), reads=[B_sbc, B_Cf[hh]], writes=[B_Cf[hh]])
                    P.op("pool", lambda g: g.tensor_copy(out=CbA[:], in_=Cf[:]), reads=B_Cf, writes=B_CbA)
                m2g = m2()

                for ti in range(TPB):
                    def m3(ti=ti):
                        sl = ti % 2

                        def ev_v(p_ap, p_buf):
                            for hh in range(4):
                                P.op("dve", lambda v, hh=hh: v.tensor_scalar(
                                    out=ve[:, sl, hh, 0:128], in0=p_ap[:, hh * 128:(hh + 1) * 128],
                                    scalar1=etok[:, ti, hh:hh + 1], scalar2=None, op0=ALU.mult),
                                    reads=[p_buf, B_etok], writes=[B_ve[sl]])
                                P.op("dve", lambda v, hh=hh: v.tensor_copy(out=ve[:, sl, hh, 128:129],
                                                                          in_=etok[:, ti, hh:hh + 1]),
                                     reads=[B_etok], writes=[B_ve[sl]])
                        proj_tm(ti, 2560, 512, ev_v)

                        yield

                        def ev_o(p_ap, p_buf):
                            P.op("act", lambda a: a.activation(out=gate[:, sl, :], in_=p_ap, func=AF.Tanh, scale=0.5),
                                 reads=[p_buf], writes=[B_gate[sl]])
                            P.op("dve", lambda v: v.scalar_tensor_tensor(out=gate[:, sl, :], in0=gate[:, sl, :], scalar=1.0,
                                                                         in1=nwb_ml[:], op0=ALU.add, op1=ALU.mult),
                                 reads=[B_gate[sl], B_par], writes=[B_gate[sl]])
                        proj_tm(ti, 3072, 512, ev_o)
                        yield
                        for h in range(4):
                            P.op("pe", lambda t, h=h: t.transpose(out=psT[:, h * 128:(h + 1) * 128],
                                                                  in_=mqkT[:, 4 + h, ti * 128:(ti + 1) * 128],
                                                                  identity=ident[:]),
                                 reads=[B_mqk[4 + h], B_const], writes=[B_psT])
                        P.op("act", lambda a: a.copy(out=mk[:, sl, :, :], in_=psT[:, 0:512].rearrange("p (h d) -> p h d", h=4)),
                             reads=[B_psT], writes=[B_mk[sl]])
                    m3s.append(m3())
                    for h in range(4):
                        def m4(ti=ti, h=h):
                            sl = ti % 2
                            si = h % 2
                            tok = slice(ti * 128, (ti + 1) * 128)
                            bank, B_bank = (ps_m, B_psm_sc) if h % 2 == 0 else (psT_f, B_psT)
                            SC, NUM, DC = bank[:, 0:128], bank[:, 128:257], bank[:, 257:386]
                            P.op("pe", lambda t: t.matmul(SC, lhsT=mqkT[:, 4 + h, tok], rhs=mqkT[:, h, tok],
                                                          start=True, stop=True),
                                 reads=[B_mqk[4 + h], B_mqk[h]], writes=[B_bank])
                            yield
                            P.op("dve", lambda v: v.tensor_tensor(out=sm[:, si, :], in0=SC, in1=mask2[:], op=ALU.mult),
                                 reads=[B_bank, B_const], writes=[B_sm[si]])
                            yield
                            P.op("pe", lambda t: t.matmul(NUM, lhsT=sm[:, si, :], rhs=ve[:, sl, h, :], start=True,
                                                          stop=False),
                                 reads=[B_sm[si], B_ve[sl]], writes=[B_bank])
                            P.op("pe", lambda t: t.matmul(NUM, lhsT=mqkT[:, h, tok], rhs=CbA[:, h, :], start=False,
                                                          stop=True),
                                 reads=[B_mqk[h], B_CbA[h]], writes=[B_bank])
                            P.op("pe", lambda t: t.matmul(DC, lhsT=mk[:, sl, h, :], rhs=ve[:, sl, h, :], start=True,
                                                          stop=True, skip_group_check=True),
                                 reads=[B_mk[sl], B_ve[sl]], writes=[B_bank])
                            yield
                            P.op("dve", lambda v: v.tensor_tensor(out=Cf[:, h, :], in0=Cf[:, h, :], in1=DC, op=ALU.add),
                                 reads=[B_Cf[h], B_bank], writes=[B_Cf[h]])
                            P.op("pool", lambda g: g.tensor_copy(out=CbA[:, h, :], in_=Cf[:, h, :]), reads=[B_Cf[h]],
                                 writes=[B_CbA[h]])
                            dn, bd = st_slot()
                            r_, brr = st_slot()
                            t3, b3 = st_slot()
                            P.op("dve", lambda v: v.tensor_scalar(out=t3, in0=bank[:, 256:257], scalar1=-1.0, scalar2=None,
                                                                  op0=ALU.mult), reads=[B_bank], writes=[b3])
                            P.op("dve", lambda v: v.scalar_tensor_tensor(out=dn, in0=bank[:, 256:257], scalar=1.0, in1=t3,
                                                                         op0=ALU.mult, op1=ALU.max),
                                 reads=[B_bank, b3], writes=[bd])
                            P.op("dve", lambda v: v.tensor_scalar(out=dn, in0=dn, scalar1=etok[:, ti, 4 + h:5 + h],
                                                                  scalar2=None, op0=ALU.max),
                                 reads=[bd, B_etok], writes=[bd])
                            P.op("dve", lambda v: v.reciprocal(out=r_, in_=dn), reads=[bd], writes=[brr])
                            P.op("dve", lambda v: v.tensor_scalar(out=mlT[:, si, :], in0=bank[:, 128:256], scalar1=r_,
                                                                  scalar2=None, op0=ALU.mult),
                                 reads=[B_bank, brr], writes=[B_mlT[si]])
                            yield
                            rs, br = rms_rstd(mlT[:, si, :], junk[:, si, :], 128, [B_mlT[si]], B_junk[si])
                            yield
                            P.op("dve", lambda v: v.scalar_tensor_tensor(
                                out=mixed[:, ti, 512 + h * 128:512 + (h + 1) * 128], in0=mlT[:, si, :], scalar=rs,
                                in1=gate[:, sl, h * 128:(h + 1) * 128], op0=ALU.mult, op1=ALU.mult),
                                reads=[B_mlT[si], br, B_gate[sl]], writes=[B_mixed[ti][4 + h]])
                        m4s[ti].append(m4())
                order = [4, 5, 6, 7, 0, 1, 2, 3]
                NS1 = 4
                extra = []
                for ti in range(TPB):
                    extra += [m3s[ti], m3s[ti]]
                seq1 = []
                for t in range(len(m1s) + NS1 - 1):
                    for st_ in reversed(range(NS1)):
                        u = t - st_
                        if 0 <= u < len(m1s):
                            seq1.append(m1s[order[u]])
                units += interleave(seq1, extra)
                for ti in range(TPB):
                    units.append(m3s[ti])
                for ti in range(TPB):
                    for pr in range(2):
                        ga, gb = m4s[ti][2 * pr], m4s[ti][2 * pr + 1]
                        for _ in range(6):
                            units += [ga, gb]
                units = [m2g] + units
                return units

            def out_proj(jb):
                for ti in range(TPB):
                    gt = jb * TPB + ti
                    transposes_to(hnT[:, :, ti * 128:(ti + 1) * 128], [B_hnT[ti]],
                                  lambda kc, ti=ti: mixed[:, ti, kc * 128:(kc + 1) * 128], B_mixed[ti], evac="act")
                    for hf in range(2):
                        i = pp_next()
                        xi = (ti * 2 + hf) % 2
                        mm_group(ps_p[i][:, :], B_ps_p[i],
                                 [(hnT[:, kc, ti * 128:(ti + 1) * 128], w_out[:, kc, hf * 512:(hf + 1) * 512])
                                  for kc in range(8)], [B_wout, B_hnT[ti]])
                        P.op("dve", lambda v, i=i, xi=xi, ti=ti, hf=hf: v.tensor_tensor(
                            out=x1t[:, xi, :], in0=ps_p[i][:, :], in1=xt[:, ti, hf * 512:(hf + 1) * 512], op=ALU.add),
                            reads=[B_ps_p[i], B_xt[ti]], writes=[B_x1t[xi]])
                        P.op("sp", lambda s, xi=xi, gt=gt, hf=hf: s.dma_start(
                            out=x1_d[gt * 128:(gt + 1) * 128, hf * 512:(hf + 1) * 512], in_=x1t[:, xi, :]),
                            reads=[B_x1t[xi]], dma=True)

            nblk = DEBUG.get("nblk", NB)
            stage = DEBUG.get("stage", 99)
            for jb in range(nblk):
                if stage >= 1:
                    A1(jb)
                def adv(g):
                    if isinstance(g, list):
                        for x in g:
                            adv(x)
                    elif callable(g):
                        g()
                    else:
                        try:
                            next(g)
                        except StopIteration:
                            pass
                au, mu = attn_units(jb), ml_units(jb)
                m2g_, mu = mu[0], mu[1:]
                for u in interleave([A2_qkv(jb)] * 10, [m2g_] * 5):
                    adv(u)
                if stage >= 5:
                    for u in interleave(au, mu):
                        adv(u)
                    for u in au + mu:
                        for g in (u if isinstance(u, list) else [u]):
                            if not callable(g):
                                for _ in g:
                                    pass
                if stage >= 6:
                    out_proj(jb)

            if "dbg" in DEBUG:
                dbg_d = dt("dbg", [128, TPB * D], BF16, kind="ExternalOutput").ap()
                P.op("sp", lambda s: s.dma_start(out=dbg_d, in_=mixed[:].rearrange("p t d -> p (t d)")),
                     reads=[b for bb in B_mixed for b in bb], dma=True)

            P.wait_all("sp", [("d", k, 16 * P.dma_cnt[k]) for k in range(NDMA) if P.dma_cnt[k] > 0])
            for e in P.ENGS:
                last = {e2: len(P.ops[e2]) - 1 for e2 in P.ENGS if e2 != e and len(P.ops[e2]) > 0}
                toks = []
                for e2, s2 in last.items():
                    while s2 >= 0 and (P.ops[e2][s2].fn is None or P.ops[e2][s2].dma is not None):
                        s2 -= 1
                    if s2 >= 0:
                        toks.append(("e", e2, s2))
                toks += [("d", k, 16 * P.dma_cnt[k]) for k in range(NDMA) if P.dma_cnt[k] > 0]
                P.wait_all(e, toks)
            P.emit()

        if DEBUG.get("phaseA_only"):
            return nc
        with ExitStack() as eb:
            w_up = sb("w_up_sb", [128, 8, 2 * DFF], BF16, eb)
            w_dn = sb("w_dn_sb", [128, NF, D], BF16, eb)
            x1s = sb("x1s", [128, TPB, D], F32, eb)
            xs2 = sb("xs2", [128, TPB, D], BF16, eb)
            hn2 = sb("hn2", [128, 2, 8, BLK], BF16, eb)
            yT = sb("yT", [128, NF, BLK], BF16, eb)
            pbuf = sb("pbuf", [128, 4, BLK + 2], F32, eb)
            acc2 = sb("acc2", [128, 8, BLK], F32, eb)
            sg = sb("sg", [128, 2, BLK], F32, eb)
            x2 = sb("x2", [128, 2, D], F32, eb)
            junk2 = sb("junk2", [128, D], BF16, eb)
            halo2 = sb("halo2", [128, 44, 2], F32, eb)
            fcw = sb("fcw_sb", [128, 44, 3], F32, eb)
            fcb = sb("fcb_sb", [128, 44], F32, eb)
            st2 = sb("st2", [128, 64], F32, eb)
            nrmb2 = sb("nrmb2", [128, 2 * D], F32, eb)
            psT2 = ps("psT2", [128, 1024], BF16, eb)
            ps_u = [ps(f"ps_u{i}", [128, 512], F32, eb) for i in range(4)]
            ps_d = [ps(f"ps_d{i}", [128, 512], F32, eb) for i in range(2)]

            B_wup = Buf("wup"); B_wdn = Buf("wdn"); B_par2 = Buf("par2")
            B_x1s = [Buf(f"x1s{i}") for i in range(TPB)]
            B_xs2 = [Buf(f"xs2{i}") for i in range(TPB)]
            B_hn2 = [[Buf(f"hn2{b}_{i}") for i in range(TPB)] for b in range(2)]
            B_yT = [Buf(f"yT{f}") for f in range(NF)]
            B_pbuf = [Buf(f"pbuf{i}") for i in range(4)]
            B_acc2 = [Buf(f"acc2{i}") for i in range(8)]
            B_sg = [Buf(f"sg{i}") for i in range(2)]
            B_x2 = [Buf(f"x2{i}") for i in range(2)]
            B_junk2 = Buf("junk2")
            B_halo2 = [Buf(f"halo2{i}") for i in range(44)]
            B_st2 = [Buf(f"st2{i}") for i in range(64)]
            B_psT2 = Buf("psT2")
            B_ps_u = [Buf(f"ps_u{i}") for i in range(4)]
            B_ps_d = [Buf(f"ps_d{i}") for i in range(2)]
            stb = {"c": 0, "u": 0, "d": 0}

            def st2_slot():
                i = stb["c"] % 64
                stb["c"] += 1
                return st2[:, i:i + 1], B_st2[i]

            for kc in range(8):
                for q in range(4):
                    c0 = q * 1408
                    P.op("pool", lambda g, kc=kc, c0=c0: g.dma_start(out=w_up[:, kc, c0:c0 + 1408],
                                                                   in_=w_up_d[kc * 128:(kc + 1) * 128, c0:c0 + 1408]),
                         writes=[B_wup], dma=True)
            for kc in range(NF):
                P.op("pool", lambda g, kc=kc: g.dma_start(out=w_dn[:, kc, :], in_=w_down_d[kc * 128:(kc + 1) * 128, :]),
                     writes=[B_wdn], dma=True)
            P.op("sp", lambda s: s.dma_start(out=fcw[:], in_=fcw_d.rearrange("p (f j) -> p f j", j=3)), writes=[B_par2],
                 dma=True)
            P.op("sp", lambda s: s.dma_start(out=fcb[:], in_=fcb_d), writes=[B_par2], dma=True)
            P.op("sp", lambda s: s.dma_start(out=nrmb2[:, 0:D], in_=nrm_d[1, :].partition_broadcast(128)), writes=[B_par2], dma=True)
            P.op("sp", lambda s: s.dma_start(out=nrmb2[:, D:2 * D], in_=nrm_d[2, :].partition_broadcast(128)), writes=[B_par2], dma=True)
            P.op("pool", lambda g: g.memset(halo2[:].rearrange("p a b -> p (a b)"), 0.0), writes=B_halo2)

            def rms_rstd2(src_ap, junk_ap, n, reads, junk_buf):
                ssq, bs = st2_slot()
                rs, br = st2_slot()
                P.op("act", lambda a: a.activation(out=junk_ap, in_=src_ap, func=AF.Square, accum_out=ssq),
                     reads=reads, writes=[junk_buf, bs])
                P.op("pool", lambda g: g.tensor_scalar(out=ssq, in0=ssq, scalar1=1.0 / n, scalar2=EPS, op0=ALU.mult,
                                                       op1=ALU.add), reads=[bs], writes=[bs])
                P.op("pool", lambda g: g.tensor_tensor(out=rs, in0=ssq, in1=NEGH, op=ALU.pow), reads=[bs, B_const],
                     writes=[br])
                return rs, br

            def B1(jb):
                hb = jb % 2
                for ti in range(TPB):
                    gt = jb * TPB + ti
                    P.op("sp", lambda s, ti=ti, gt=gt: s.dma_start(out=x1s[:, ti, :], in_=x1_d[gt * 128:(gt + 1) * 128, :]),
                         writes=[B_x1s[ti]], dma=True)
                    rs, br = rms_rstd2(x1s[:, ti, :], xs2[:, ti, :], D, [B_x1s[ti]], B_xs2[ti])
                    yield
                    P.op("dve", lambda v, ti=ti, rs=rs: v.scalar_tensor_tensor(
                        out=xs2[:, ti, :], in0=x1s[:, ti, :], scalar=rs, in1=nrmb2[:, 0:D], op0=ALU.mult, op1=ALU.mult),
                        reads=[B_x1s[ti], br, B_par2], writes=[B_xs2[ti]])
                    yield
                    for kc in range(8):
                        P.op("pe", lambda t, kc=kc, ti=ti: t.transpose(out=psT2[:, kc * 128:(kc + 1) * 128],
                                                                       in_=xs2[:, ti, kc * 128:(kc + 1) * 128],
                                                                       identity=ident[:]),
                             reads=[B_xs2[ti], B_const], writes=[B_psT2])
                    P.op("act", lambda a, ti=ti: a.copy(out=hn2[:, hb, :, ti * 128:(ti + 1) * 128],
                                                        in_=psT2[:].rearrange("p (k t) -> p k t", k=8)),
                         reads=[B_psT2], writes=[B_hn2[hb][ti]])
                    yield

            def B2f(jb, f):
                hb = jb % 2
                slots = []
                for part in range(2):
                    tidx = part * NF + f
                    col0 = part * DFF + f * 128
                    ui = stb["u"] % 4
                    ai = stb["u"] % 8
                    stb["u"] += 1
                    slots.append((ui, ai, tidx))
                    for kc in range(8):
                        P.op("pe", lambda t, kc=kc, ui=ui, col0=col0: t.matmul(
                            ps_u[ui][:, 0:BLK], lhsT=w_up[:, kc, col0:col0 + 128], rhs=hn2[:, hb, kc, :],
                            start=(kc == 0), stop=(kc == 7)), reads=[B_wup] + B_hn2[hb], writes=[B_ps_u[ui]])
                    P.op("pool", lambda g, ui=ui, tidx=tidx: g.tensor_copy(out=pbuf[:, ui, 0:2], in_=halo2[:, tidx, :]),
                         reads=[B_halo2[tidx]], writes=[B_pbuf[ui]])
                    P.op("act", lambda a, ui=ui: a.copy(out=pbuf[:, ui, 2:BLK + 2], in_=ps_u[ui][:, 0:BLK]),
                         reads=[B_ps_u[ui]], writes=[B_pbuf[ui]])
                    P.op("act", lambda a, ui=ui, ai=ai, tidx=tidx: a.activation(
                        out=acc2[:, ai, :], in_=ps_u[ui][:, 0:BLK], func=AF.Identity, bias=fcb[:, tidx:tidx + 1],
                        scale=fcw[:, tidx, 2:3]), reads=[B_ps_u[ui], B_par2], writes=[B_acc2[ai]])
                    P.op("pool", lambda g, ui=ui, tidx=tidx: g.tensor_copy(out=halo2[:, tidx, :],
                                                                           in_=pbuf[:, ui, BLK:BLK + 2]),
                         reads=[B_pbuf[ui]], writes=[B_halo2[tidx]])
                yield
                for (ui, ai, tidx) in slots:
                    for j in (1, 0):
                        P.op("dve", lambda v, ui=ui, ai=ai, tidx=tidx, j=j: v.scalar_tensor_tensor(
                            out=acc2[:, ai, :], in0=pbuf[:, ui, j:j + BLK], scalar=fcw[:, tidx, j:j + 1],
                            in1=acc2[:, ai, :], op0=ALU.mult, op1=ALU.add),
                            reads=[B_pbuf[ui], B_acc2[ai], B_par2], writes=[B_acc2[ai]])
                yield
                ag, au = slots[0][1], slots[1][1]
                gi = f % 2
                P.op("act", lambda a: a.activation(out=sg[:, gi, :], in_=acc2[:, ag, :], func=AF.Silu),
                     reads=[B_acc2[ag]], writes=[B_sg[gi]])
                yield
                P.op("dve", lambda v: v.tensor_tensor(out=yT[:, f, :], in0=sg[:, gi, :], in1=acc2[:, au, :], op=ALU.mult),
                     reads=[B_sg[gi], B_acc2[au]], writes=[B_yT[f]])

            def B3(jb):
                for ti in range(TPB):
                    gt = jb * TPB + ti
                    xi = gt % 2
                    P.op("sp", lambda s, xi=xi, gt=gt: s.dma_start(out=x2[:, xi, :], in_=x1_d[gt * 128:(gt + 1) * 128, :]),
                         writes=[B_x2[xi]], dma=True)
                    for hf in range(2):
                        di = stb["d"] % 2
                        stb["d"] += 1
                        for kc in range(NF):
                            P.op("pe", lambda t, kc=kc, di=di, ti=ti, hf=hf: t.matmul(
                                ps_d[di][:, :], lhsT=yT[:, kc, ti * 128:(ti + 1) * 128],
                                rhs=w_dn[:, kc, hf * 512:(hf + 1) * 512], start=(kc == 0), stop=(kc == NF - 1)),
                                reads=[B_wdn, B_yT[kc]], writes=[B_ps_d[di]])
                        P.op("dve", lambda v, di=di, xi=xi, hf=hf: v.tensor_tensor(
                            out=x2[:, xi, hf * 512:(hf + 1) * 512], in0=ps_d[di][:, :],
                            in1=x2[:, xi, hf * 512:(hf + 1) * 512], op=ALU.add),
                            reads=[B_ps_d[di], B_x2[xi]], writes=[B_x2[xi]])
                    rs, br = rms_rstd2(x2[:, xi, :], junk2[:], D, [B_x2[xi]], B_junk2)
                    P.op("dve", lambda v, xi=xi, rs=rs: v.scalar_tensor_tensor(
                        out=x2[:, xi, :], in0=x2[:, xi, :], scalar=rs, in1=nrmb2[:, D:2 * D], op0=ALU.mult,
                        op1=ALU.mult), reads=[B_x2[xi], br, B_par2], writes=[B_x2[xi]])
                    P.op("sp", lambda s, xi=xi, gt=gt: s.dma_start(out=out_d[gt * 128:(gt + 1) * 128, :], in_=x2[:, xi, :]),
                         reads=[B_x2[xi]], dma=True)

            def adv2(g):
                try:
                    next(g)
                except StopIteration:
                    pass

            g0 = B1(0)
            for _ in g0:
                pass
            for jb in range(nblk):
                fg = [B2f(jb, f) for f in range(NF)]
                A = []
                NS2 = 4
                for t in range(NF + NS2 - 1):
                    for st_ in reversed(range(NS2)):
                        u = t - st_
                        if 0 <= u < NF:
                            A.append(fg[u])
                M = []
                if jb + 1 < nblk:
                    gn = B1(jb + 1)
                    M = [gn] * (3 * TPB)
                for g in interleave(A, M):
                    adv2(g)
                for g in fg + (M[:1]):
                    for _ in g:
                        pass
                B3(jb)
            P.wait_all("sp", [("d", k, 16 * P.dma_cnt[k]) for k in range(NDMA) if P.dma_cnt[k] > 0])
            P.emit()
    return nc


def _prep_inputs(inputs):
    f = lambda a: np.ascontiguousarray(np.asarray(a, dtype=np.float32))
    x = f(inputs["x"])
    shared = {
        "w_in": f(inputs["w_in"][0]),
        "w_out": f(inputs["w_out"][0]),
        "w_up": f(inputs["w_up"][0]),
        "w_down": f(inputs["w_down"][0]),
        "nrm": f(np.stack([np.asarray(inputs["attn_norm_w"][0]), np.asarray(inputs["ffn_norm_w"][0]),
                           np.asarray(inputs["final_norm_w"])], axis=0)),
        "mlnw": f(np.asarray(inputs["mlstm_norm_w"]).reshape(1, 512)),
        "dnw": f(np.asarray(inputs["diff_norm_w"]).reshape(1, 128)),
        "lamv": f(np.concatenate([np.asarray(inputs[k]).reshape(-1) for k in
                                  ("lambda_q1", "lambda_k1", "lambda_q2", "lambda_k2")]).reshape(1, 256)),
        "mcw": f(np.asarray(inputs["mlstm_conv_w"][0]).reshape(4, 8, 128).transpose(2, 1, 0).reshape(128, 32)),
        "mcb": f(np.asarray(inputs["mlstm_conv_b"][0]).reshape(8, 128).T),
        "gb": f(np.stack([np.asarray(inputs["mlstm_igate_b"][0]), np.asarray(inputs["mlstm_fgate_b"][0])], axis=1)),
        "fcw": f(np.asarray(inputs["ffn_conv_w"][0]).reshape(3, 44, 128).transpose(2, 1, 0).reshape(128, 132)),
        "fcb": f(np.asarray(inputs["ffn_conv_b"][0]).reshape(44, 128).T),
    }
    return x, shared


def kernel(**inputs):
    x, shared = _prep_inputs(inputs)
    nc = build_program()
    in_maps = [dict(shared, x=np.ascontiguousarray(x[b])) for b in range(8)]
    res = run_bass_kernel_spmd(nc, in_maps, core_ids=list(range(8)))
    out = np.stack([np.asarray(r["out"], dtype=np.float32) for r in res.results], axis=0)
    return out
```

```python
import math
from contextlib import ExitStack

import numpy as np
import concourse.bass as bass
import concourse.mybir as mybir
from concourse.bass_utils import run_bass_kernel_spmd

F32 = mybir.dt.float32
BF16 = mybir.dt.bfloat16
I32 = mybir.dt.int32
AF = mybir.ActivationFunctionType
ALU = mybir.AluOpType

S = 4096
D = 1024
NT = S // 128
BLK = 256
TPB = BLK // 128
NB = S // BLK
INC = 3592
DFF = 2816
NF = DFF // 128
EPS = 1e-6
LAM_INIT = 0.8 - 0.6 * math.exp(0.0)
SLOPES = [2.0 ** (-8.0 * (i + 1) / 4) for i in range(4)]
LN_SQRT_D = 0.5 * math.log(128.0)
CAP = 2000
NDMA = 24

DEBUG = {}


class Buf:
    __slots__ = ("name", "w", "r")

    def __init__(self, name):
        self.name = name
        self.w = None
        self.r = {}


class Op:
    __slots__ = ("fn", "deps", "sig", "dma", "sigidx")

    def __init__(self, fn, deps, dma):
        self.fn = fn
        self.deps = deps
        self.sig = False
        self.dma = dma
        self.sigidx = 0


class Prog:
    ENGS = ("pe", "act", "dve", "pool", "sp")

    def __init__(self, nc, es):
        self.nc = nc
        self.es = es
        self.ops = {e: [] for e in self.ENGS}
        self.dma_cnt = [0] * NDMA
        self.dma_rr = {e: 0 for e in self.ENGS}
        self.dsem = [es.enter_context(nc.semaphore(f"dsem{k}")) for k in range(NDMA)]
        self.esem = {e: [] for e in self.ENGS}
        self.sigtot = {e: 0 for e in self.ENGS}
        self.start = {e: 0 for e in self.ENGS}

    def op(self, eng, fn, reads=(), writes=(), dma=False):
        self.nrec = getattr(self, "nrec", 0) + 1
        if DEBUG.get("oplimit") and self.nrec > DEBUG["oplimit"]:
            return None
        deps = set()
        for b in reads:
            if b.w is not None:
                deps.add(b.w)
        for b in writes:
            if b.w is not None:
                deps.add(b.w)
            deps.update(b.r.values())
        deps = {d for d in deps if not (d[0] == "e" and d[2] < self.start[d[1]])}
        if eng == "pe":
            deps = {d for d in deps if not (d[0] == "e" and d[1] == "pe")}
        dmatok = None
        if dma:
            lo, n = (0, 8) if eng == "pool" else (8, NDMA - 8)
            k = lo + self.dma_rr[eng] % n
            self.dma_rr[eng] += 1
            self.dma_cnt[k] += 1
            val = 16 * self.dma_cnt[k]
            if self.dma_cnt[k] > 1:
                deps.add(("d", k, val - 16))
            dmatok = ("d", k, val)
        for d in deps:
            if d[0] == "e":
                self.ops[d[1]][d[2]].sig = True
        o = Op(fn, deps, dmatok)
        seq = len(self.ops[eng])
        self.ops[eng].append(o)
        tok = dmatok if dma else ("e", eng, seq)
        for b in reads:
            key = ("d", tok[1]) if dma else eng
            b.r[key] = tok
        for b in writes:
            b.w = tok
            b.r = {}
        return tok

    def wait_all(self, eng, toks):
        deps = set(toks)
        for d in deps:
            if d[0] == "e":
                self.ops[d[1]][d[2]].sig = True
        self.ops[eng].append(Op(None, deps, None))

    def emit(self):
        nc, es = self.nc, self.es
        for e in self.ENGS:
            c = self.sigtot[e]
            for o in self.ops[e][self.start[e]:]:
                if o.sig:
                    c += 1
                o.sigidx = c if o.sig else 0
            need = (c + CAP - 1) // CAP
            while len(self.esem[e]) < max(need, 1):
                self.esem[e].append(es.enter_context(nc.semaphore(f"es_{e}{len(self.esem[e])}")))
            self.sigtot[e] = c

        def resolve(d):
            if d[0] == "e":
                si = self.ops[d[1]][d[2]].sigidx
                assert si > 0, d
                return ("e", d[1], (si - 1) // CAP), self.esem[d[1]][(si - 1) // CAP], (si - 1) % CAP + 1
            return ("d", d[1]), self.dsem[d[1]], d[2]

        def run(eng, handle):
            waited = {}
            for o in self.ops[eng][self.start[eng]:]:
                for d in sorted(o.deps, key=str):
                    key, sem, val = resolve(d)
                    if waited.get(key, 0) >= val:
                        continue
                    handle.wait_ge(sem, val)
                    waited[key] = val
                if o.fn is None:
                    continue
                ins = o.fn(handle)
                if o.sig:
                    si = o.sigidx
                    ins.then_inc(self.esem[eng][(si - 1) // CAP], 1)
                if o.dma is not None:
                    ins.then_inc(self.dsem[o.dma[1]], 16)

        with nc.Block() as block:
            @block.tensor
            def _(h):
                run("pe", h)

            @block.scalar
            def _(h):
                run("act", h)

            @block.vector
            def _(h):
                run("dve", h)

            @block.gpsimd
            def _(h):
                run("pool", h)

            @block.sync
            def _(h):
                run("sp", h)
        for e in self.ENGS:
            self.start[e] = len(self.ops[e])


def interleave(A, M):
    out = []
    na, nm = len(A), len(M)
    if na == 0:
        return list(M)
    mi = 0
    for i, a in enumerate(A):
        out.append(a)
        tgt = ((i + 1) * nm) // na
        while mi < tgt:
            out.append(M[mi])
            mi += 1
    out.extend(M[mi:])
    return out


def build_program():
    nc = bass.Bass("TRN2", target_bir_lowering=False)
    dt = nc.dram_tensor
    x_d = dt("x", [S, D], F32, kind="ExternalInput").ap()
    w_in_d = dt("w_in", [D, INC], F32, kind="ExternalInput").ap()
    w_out_d = dt("w_out", [D, D], F32, kind="ExternalInput").ap()
    w_up_d = dt("w_up", [D, 2 * DFF], F32, kind="ExternalInput").ap()
    w_down_d = dt("w_down", [DFF, D], F32, kind="ExternalInput").ap()
    nrm_d = dt("nrm", [3, D], F32, kind="ExternalInput").ap()
    mlnw_d = dt("mlnw", [1, 512], F32, kind="ExternalInput").ap()
    dnw_d = dt("dnw", [1, 128], F32, kind="ExternalInput").ap()
    lamv_d = dt("lamv", [1, 256], F32, kind="ExternalInput").ap()
    mcw_d = dt("mcw", [128, 8 * 4], F32, kind="ExternalInput").ap()
    mcb_d = dt("mcb", [128, 8], F32, kind="ExternalInput").ap()
    gb_d = dt("gb", [4, 2], F32, kind="ExternalInput").ap()
    fcw_d = dt("fcw", [128, 44 * 3], F32, kind="ExternalInput").ap()
    fcb_d = dt("fcb", [128, 44], F32, kind="ExternalInput").ap()
    out_d = dt("out", [S, D], F32, kind="ExternalOutput").ap()
    x1_d = out_d

    with ExitStack() as es:
        P = Prog(nc, es)

        def sb(name, shape, dtype, ctx=es):
            return ctx.enter_context(nc.sbuf_tensor(name, shape, dtype))

        def ps(name, shape, dtype, ctx=es):
            return ctx.enter_context(nc.psum_tensor(name, shape, dtype))

        phase_sem = es.enter_context(nc.semaphore("phase"))

        ident = sb("ident", [128, 128], BF16)
        identf = sb("identf", [128, 128], F32)
        maskc = sb("maskc", [128, 128], BF16)
        mask2 = sb("mask2", [128, 128], F32)
        biasT = sb("biasT", [128, 4 * 32], F32)
        biasI = sb("biasI", [128, 32], I32)
        cst = sb("cst", [128, 16], F32)
        nrmb = sb("nrmb", [128, D], F32)
        B_const = Buf("const")

        NEGH = cst[:, 0:1]
        NLAM = cst[:, 2:3]

        def bc_rows(ap_row, n):
            return ap_row.partition_broadcast(128)

        def setup_consts():
            P.op("pool", lambda g: g.memset(cst[:], 0.0), writes=[B_const])
            P.op("pool", lambda g: g.memset(cst[:, 0:1], -0.5), writes=[B_const])
            for t, dtp in ((ident, BF16), (identf, F32)):
                P.op("pool", lambda g, t=t: g.memset(t[:], 1.0), writes=[B_const])
                P.op("pool", lambda g, t=t: g.affine_select(out=t[:], in_=t[:], pattern=[[1, 128]],
                                                            compare_op=ALU.is_ge, fill=0.0, base=0,
                                                            channel_multiplier=-1), reads=[B_const], writes=[B_const])
                P.op("pool", lambda g, t=t: g.affine_select(out=t[:], in_=t[:], pattern=[[-1, 128]],
                                                            compare_op=ALU.is_ge, fill=0.0, base=0,
                                                            channel_multiplier=1), reads=[B_const], writes=[B_const])
            for t in (maskc, mask2):
                P.op("pool", lambda g, t=t: g.memset(t[:], 1.0), writes=[B_const])
                P.op("pool", lambda g, t=t: g.affine_select(out=t[:], in_=t[:], pattern=[[1, 128]],
                                                            compare_op=ALU.is_ge, fill=0.0, base=0,
                                                            channel_multiplier=-1), reads=[B_const], writes=[B_const])
            P.op("pool", lambda g: g.iota(biasI[:], pattern=[[-128, 32]], base=-127, channel_multiplier=1),
                 writes=[B_const])
            for h in range(4):
                P.op("dve", lambda v, h=h: v.tensor_scalar(out=biasT[:, h * 32:(h + 1) * 32], in0=biasI[:],
                                                          scalar1=float(SLOPES[h]), scalar2=None, op0=ALU.mult),
                     reads=[B_const], writes=[B_const])
            P.op("sp", lambda s: s.dma_start(out=nrmb[:], in_=nrm_d[0, :].partition_broadcast(128)),
                 writes=[B_const], dma=True)

        setup_consts()

        with ExitStack() as ea:
            w_in = sb("w_in_sb", [128, 8, INC], BF16, ea)
            w_out = sb("w_out_sb", [128, 8, D], BF16, ea)
            kT = sb("kT", [128, 4, S], BF16, ea)
            vext = sb("vext", [128, NT, 4, 129], BF16, ea)
            xt = sb("xt", [128, TPB, D], F32, ea)
            xs = sb("xs", [128, 1, D], BF16, ea)
            hnT = sb("hnT", [128, 8, BLK], BF16, ea)
            qT0 = sb("qT0", [128, 4, BLK], BF16, ea)
            qT1 = sb("qT1", [128, 4, BLK], BF16, ea)
            mqkT = sb("mqkT", [128, 8, BLK], BF16, ea)
            pre = sb("pre", [128, 2, BLK + 3], F32, ea)
            acc = sb("acc", [128, 2, BLK], F32, ea)
            th = sb("th", [128, 1, BLK], F32, ea)
            mk = sb("mk", [128, 2, 4, 128], BF16, ea)
            ve = sb("ve", [128, 2, 4, 129], BF16, ea)
            gate = sb("gate", [128, 2, 512], F32, ea)
            E = sb("E", [128, 2, 2, BLK], BF16, ea)
            mixed = sb("mixed", [128, TPB, D], BF16, ea)
            x1t = sb("x1t", [128, 2, 512], F32, ea)
            epiA = sb("epiA", [128, 2, 128], F32, ea)
            epiT = sb("epiT", [128, 2, 128], F32, ea)
            mlT = sb("mlT", [128, 4, 128], F32, ea)
            junk = sb("junk", [128, 2, 128], BF16, ea)
            sm = sb("sm", [128, 4, 128], BF16, ea)
            nwb_ml = sb("nwb_ml", [128, 512], F32, ea)
            dnw_b = sb("dnw_b", [128, 128], F32, ea)
            lamv = sb("lamv_sb", [128, 256], F32, ea)
            Cf = sb("Cf", [128, 4, 129], F32, ea)
            CbA = sb("CbA", [128, 4, 129], BF16, ea)
            halo = sb("halo", [128, 8, 3], F32, ea)
            cw = sb("cw", [128, 8, 4], F32, ea)
            cb = sb("cb", [128, 8], F32, ea)
            gbias = sb("gbias", [4, 4], F32, ea)
            g_i = sb("g_i", [4, BLK], F32, ea)
            g_f = sb("g_f", [4, BLK], F32, ea)
            g_B = sb("g_B", [4, BLK], F32, ea)
            g_G = sb("g_G", [4, BLK], F32, ea)
            g_e = sb("g_e", [4, 2, BLK], F32, ea)
            g_c = sb("g_c", [4, 8], F32, ea)
            g_d = sb("g_d", [4, 4], F32, ea)
            ones4 = sb("ones4", [4, 128], F32, ea)
            etok = sb("etok", [128, TPB, 8], F32, ea)
            sbc = sb("sbc", [128, 4], F32, ea)
            st = sb("st", [128, 64], F32, ea)

            psT_f = ps("psT_f", [128, 512], F32, ea)
            psT = psT_f[:].bitcast(BF16)
            ps_s = [ps(f"ps_s{i}", [128, 2, BLK], F32, ea) for i in range(2)]
            ps_o = [ps(f"ps_o{i}", [128, 512], F32, ea) for i in range(2)]
            ps_p = [ps(f"ps_p{i}", [128, 512], F32, ea) for i in range(2)]
            ps_m = ps("ps_m", [128, 512], F32, ea)

            B_win = Buf("w_in"); B_wout = Buf("w_out")
            B_kT = [Buf(f"kT{j}") for j in range(NB)]
            B_v = [Buf(f"v{t}") for t in range(NT)]
            B_xt = [Buf(f"xt{i}") for i in range(TPB)]
            B_xs = [Buf(f"xs{i}") for i in range(TPB)]
            B_hnT = [Buf(f"hnT{i}") for i in range(TPB)]
            B_qT = [Buf(f"qT{h}") for h in range(4)]
            B_mqk = [Buf(f"mqk{f}") for f in range(8)]
            B_pre = [Buf(f"pre{i}") for i in range(2)]
            B_acc = [Buf(f"acc{i}") for i in range(2)]
            B_th = [Buf(f"th{i}") for i in range(2)]
            B_mk = [Buf(f"mk{i}") for i in range(2)]
            B_ve = [Buf(f"ve{i}") for i in range(2)]
            B_gate = [Buf(f"gate{i}") for i in range(2)]
            B_E = [Buf(f"E{i}") for i in range(2)]
            B_veAB = Buf("veAB")
            B_mixed = [[Buf(f"mixed{i}_{c}") for c in range(8)] for i in range(TPB)]
            B_x1t = [Buf(f"x1t{i}") for i in range(2)]
            B_epiA = [Buf(f"epiA{i}") for i in range(2)]
            B_epiT = [Buf(f"epiT{i}") for i in range(2)]
            B_mlT = [Buf(f"mlT{i}") for i in range(4)]
            B_junk = [Buf(f"junk{i}") for i in range(2)]
            B_sm = [Buf(f"sm{i}") for i in range(4)]
            B_preh = [Buf(f"preh{i}") for i in range(2)]
            B_Cf = [Buf(f"Cf{h}") for h in range(4)]
            B_CbA = [Buf(f"CbA{h}") for h in range(4)]
            B_CbB = [Buf(f"CbB{h}") for h in range(4)]
            B_halo = [Buf(f"halo{f}") for f in range(8)]
            B_g = Buf("gates")
            B_etok = Buf("etok")
            B_sbc = Buf("sbc")
            B_st = [Buf(f"st{i}") for i in range(64)]
            B_psT = Buf("psT")
            B_ps_s = [Buf(f"ps_s{i}") for i in range(2)]
            B_ps_o = [Buf(f"ps_o{i}") for i in range(2)]
            B_ps_p = [Buf(f"ps_p{i}") for i in range(2)]
            B_psm_sc = Buf("psm"); B_psm_num = B_psm_sc
            B_psm_dc = [B_psm_sc, B_psm_sc]; B_psm_x = B_psm_sc
            PM_SC = ps_m[:, 0:128]
            PM_NUM = ps_m[:, 128:257]
            PM_DC = [ps_m[:, 257:386], ps_m[:, 386:515] if False else None]

            B_psm_dc[1] = B_psm_dc[0]
            PM_DC = [ps_m[:, 257:386], ps_m[:, 257:386]]
            PM_X = ps_m[:, 386:400]

            state = {"stc": 0, "pp": 0}

            def st_slot():
                i = state["stc"] % 64
                state["stc"] += 1
                return st[:, i:i + 1], B_st[i]

            def pp_next():
                i = state["pp"] % 2
                state["pp"] += 1
                return i

            half = INC // 2
            for kc in range(8):
                for hf in range(2):
                    P.op("pool", lambda g, kc=kc, hf=hf: g.dma_start(
                        out=w_in[:, kc, hf * half:(hf + 1) * half],
                        in_=w_in_d[kc * 128:(kc + 1) * 128, hf * half:(hf + 1) * half]),
                        writes=[B_win], dma=True)
            for kc in range(8):
                P.op("pool", lambda g, kc=kc: g.dma_start(out=w_out[:, kc, :], in_=w_out_d[kc * 128:(kc + 1) * 128, :]),
                     writes=[B_wout], dma=True)
            B_par = Buf("params")
            P.op("sp", lambda s: s.dma_start(out=nwb_ml[:], in_=mlnw_d.rearrange("a d -> (a d)").partition_broadcast(128)),
                 writes=[B_par], dma=True)
            P.op("sp", lambda s: s.dma_start(out=dnw_b[:], in_=dnw_d.rearrange("a d -> (a d)").partition_broadcast(128)),
                 writes=[B_par], dma=True)
            P.op("sp", lambda s: s.dma_start(out=lamv[:], in_=lamv_d.rearrange("a d -> (a d)").partition_broadcast(128)),
                 writes=[B_par], dma=True)
            P.op("sp", lambda s: s.dma_start(out=cw[:], in_=mcw_d.rearrange("p (f j) -> p f j", j=4)), writes=[B_par], dma=True)
            P.op("sp", lambda s: s.dma_start(out=cb[:], in_=mcb_d), writes=[B_par], dma=True)
            P.op("sp", lambda s: s.dma_start(out=gbias[:, 0:2], in_=gb_d), writes=[B_par], dma=True)
            P.op("dve", lambda v: v.tensor_scalar(out=nwb_ml[:], in0=nwb_ml[:], scalar1=0.5, scalar2=None, op0=ALU.mult),
                 reads=[B_par], writes=[B_par])
            P.op("dve", lambda v: v.tensor_scalar(out=dnw_b[:], in0=dnw_b[:], scalar1=float(1.0 - LAM_INIT), scalar2=None,
                                                  op0=ALU.mult), reads=[B_par], writes=[B_par])
            P.op("dve", lambda v: v.tensor_scalar(out=cw[:], in0=cw[:], scalar1=0.5, scalar2=None, op0=ALU.mult),
                 reads=[B_par], writes=[B_par])
            P.op("dve", lambda v: v.tensor_scalar(out=cb[:], in0=cb[:], scalar1=0.5, scalar2=None, op0=ALU.mult),
                 reads=[B_par], writes=[B_par])
            P.op("dve", lambda v: v.tensor_scalar(out=gbias[:, 2:3], in0=gbias[:, 1:2], scalar1=-1.0, scalar2=None,
                                                  op0=ALU.mult), reads=[B_par], writes=[B_par])
            P.op("dve", lambda v: v.tensor_tensor(out=lamv[:, 0:64], in0=lamv[:, 0:64], in1=lamv[:, 64:128], op=ALU.mult),
                 reads=[B_par], writes=[B_par])
            P.op("dve", lambda v: v.tensor_tensor(out=lamv[:, 128:192], in0=lamv[:, 128:192], in1=lamv[:, 192:256],
                                                  op=ALU.mult), reads=[B_par], writes=[B_par])
            P.op("dve", lambda v: v.reduce_sum(out=cst[:, 3:4], in_=lamv[:, 0:64], axis=mybir.AxisListType.X),
                 reads=[B_par, B_const], writes=[B_const])
            P.op("dve", lambda v: v.reduce_sum(out=cst[:, 4:5], in_=lamv[:, 128:192], axis=mybir.AxisListType.X),
                 reads=[B_par, B_const], writes=[B_const])
            P.op("act", lambda a: a.activation(out=cst[:, 5:7], in_=cst[:, 3:5], func=AF.Exp), reads=[B_const],
                 writes=[B_const])
            P.op("dve", lambda v: v.tensor_tensor(out=cst[:, 1:2], in0=cst[:, 5:6], in1=cst[:, 6:7], op=ALU.subtract),
                 reads=[B_const], writes=[B_const])
            P.op("dve", lambda v: v.tensor_scalar(out=cst[:, 1:2], in0=cst[:, 1:2], scalar1=float(LAM_INIT), scalar2=None,
                                                  op0=ALU.add), reads=[B_const], writes=[B_const])
            P.op("dve", lambda v: v.tensor_scalar(out=cst[:, 2:3], in0=cst[:, 1:2], scalar1=-1.0, scalar2=None,
                                                  op0=ALU.mult), reads=[B_const], writes=[B_const])
            P.op("pool", lambda g: g.memset(vext[:].rearrange("p a b c -> p (a b c)"), 1.0), writes=B_v)
            P.op("pool", lambda g: g.memset(Cf[:].rearrange("p a b -> p (a b)"), 0.0), writes=B_Cf)
            P.op("pool", lambda g: g.memset(CbA[:].rearrange("p a b -> p (a b)"), 0.0), writes=B_CbA)
            P.op("pool", lambda g: g.memset(halo[:].rearrange("p a b -> p (a b)"), 0.0), writes=B_halo)
            P.op("pool", lambda g: g.memset(g_c[:], 0.0), writes=[B_g])
            for tz in (qT0, qT1):
                P.op("pool", lambda g, tz=tz: g.memset(tz[:].rearrange("p a b -> p (a b)"), 0.0), writes=B_qT + [B_veAB])
            P.op("pool", lambda g: g.memset(ones4[:], 1.0), writes=[B_g])

            def rms_rstd(src_ap, junk_ap, n, reads, junk_buf):
                ssq, bs = st_slot()
                rs, br = st_slot()
                P.op("act", lambda a: a.activation(out=junk_ap, in_=src_ap, func=AF.Square, accum_out=ssq),
                     reads=reads, writes=[junk_buf, bs])
                P.op("pool", lambda g: g.tensor_scalar(out=ssq, in0=ssq, scalar1=1.0 / n, scalar2=EPS, op0=ALU.mult,
                                                       op1=ALU.add), reads=[bs], writes=[bs])
                P.op("pool", lambda g: g.tensor_tensor(out=rs, in0=ssq, in1=NEGH, op=ALU.pow), reads=[bs, B_const],
                     writes=[br])
                return rs, br

            def transposes_to(dst_ap, dst_bufs, src_fn, src_bufs, evac="act"):
                for kc in range(8):
                    P.op("pe", lambda t, kc=kc: t.transpose(out=psT[:, kc * 128:(kc + 1) * 128], in_=src_fn(kc),
                                                            identity=ident[:]),
                         reads=src_bufs + [B_const], writes=[B_psT])
                src = psT[:].rearrange("p (k t) -> p k t", k=8)
                if evac == "act":
                    P.op("act", lambda a: a.copy(out=dst_ap, in_=src), reads=[B_psT], writes=dst_bufs)
                else:
                    P.op("dve", lambda v: v.tensor_copy(out=dst_ap, in_=src), reads=[B_psT], writes=dst_bufs)

            def mm_group(out_ap, out_buf, pairs, reads):
                n = len(pairs)
                for i, (l, r) in enumerate(pairs):
                    P.op("pe", lambda t, l=l, r=r, i=i: t.matmul(out_ap, lhsT=l, rhs=r, start=(i == 0), stop=(i == n - 1)),
                         reads=reads, writes=[out_buf])

            def A1(jb):
                for ti in range(TPB):
                    gt = jb * TPB + ti
                    P.op("sp", lambda s, ti=ti, gt=gt: s.dma_start(out=xt[:, ti, :], in_=x_d[gt * 128:(gt + 1) * 128, :]),
                         writes=[B_xt[ti]], dma=True)
                    rs, br = rms_rstd(xt[:, ti, :], xs[:, 0, :], D, [B_xt[ti]], B_xs[0])
                    P.op("dve", lambda v, ti=ti, rs=rs: v.scalar_tensor_tensor(
                        out=xs[:, 0, :], in0=xt[:, ti, :], scalar=rs, in1=nrmb[:, 0:D], op0=ALU.mult, op1=ALU.mult),
                        reads=[B_xt[ti], br, B_const], writes=[B_xs[0]])
                    transposes_to(hnT[:, :, ti * 128:(ti + 1) * 128], [B_hnT[ti]],
                                  lambda kc, ti=ti: xs[:, 0, kc * 128:(kc + 1) * 128], [B_xs[0]])

            def proj_fm(col0, evac):
                i = pp_next()
                mm_group(ps_p[i][:, 0:BLK], B_ps_p[i],
                         [(w_in[:, kc, col0:col0 + 128], hnT[:, kc, :]) for kc in range(8)],
                         [B_win] + B_hnT)
                evac(ps_p[i][:, 0:BLK], B_ps_p[i])

            def proj_tm(ti, col0, ncols, evac):
                i = pp_next()
                mm_group(ps_p[i][:, 0:ncols], B_ps_p[i],
                         [(hnT[:, kc, ti * 128:(ti + 1) * 128], w_in[:, kc, col0:col0 + ncols]) for kc in range(8)],
                         [B_win, B_hnT[ti]])
                evac(ps_p[i][:, 0:ncols], B_ps_p[i])

            def A2_qkv(jb):
                for h in range(4):
                    def ev_q(p_ap, p_buf, h=h):
                        P.op("dve", lambda v: v.tensor_scalar(out=qT0[0:64, h, :], in0=p_ap[0:64, :], scalar1=0.125,
                                                              scalar2=None, op0=ALU.mult), reads=[p_buf], writes=[B_qT[h]])
                        P.op("dve", lambda v: v.tensor_scalar(out=qT1[64:128, h, :], in0=p_ap[64:128, :], scalar1=0.125,
                                                              scalar2=None, op0=ALU.mult), reads=[p_buf], writes=[B_qT[h]])
                    proj_fm(h * 128, ev_q)
                    yield

                    def ev_k(p_ap, p_buf, h=h):
                        P.op("act", lambda a: a.copy(out=kT[:, h, jb * BLK:(jb + 1) * BLK], in_=p_ap), reads=[p_buf],
                             writes=[B_kT[jb]])
                    proj_fm(512 + h * 128, ev_k)
                    yield
                for ti in range(TPB):
                    gt = jb * TPB + ti

                    def ev_v(p_ap, p_buf, gt=gt):
                        P.op("dve", lambda v: v.tensor_copy(out=vext[:, gt, :, 0:128],
                                                            in_=p_ap.rearrange("p (h d) -> p h d", h=4)),
                             reads=[p_buf], writes=[B_v[gt]])
                    proj_tm(ti, 1024, 512, ev_v)
                    yield

            def attn_units(jb):
                units = []
                steps = []
                epis = []
                q0 = jb * BLK
                nk = 2 * jb + 2
                for h in range(4):
                    for kt in range(nk):
                        def step(h=h, kt=kt):
                            si = (h * nk + kt) % 2
                            ei = (h * nk + kt) % 2
                            diag = kt - 2 * jb
                            qlo = 128 if diag == 1 else 0
                            for c in range(2):
                                P.op("pe", lambda t, c=c: t.matmul(
                                    ps_s[si][:, c, qlo:BLK],
                                    lhsT=kT[:, h, kt * 128:(kt + 1) * 128],
                                    rhs=(qT0 if c == 0 else qT1)[:, h, qlo:BLK], start=True, stop=True),
                                    reads=[B_kT[kt // TPB], B_qT[h]], writes=[B_ps_s[si]])
                            yield
                            bi = (q0 - kt * 128) // 128 + 1
                            P.op("act", lambda a: a.activation(out=E[:, ei, :, qlo:BLK], in_=ps_s[si][:, :, qlo:BLK],
                                                               func=AF.Exp, bias=biasT[:, h * 32 + bi:h * 32 + bi + 1],
                                                               scale=1.0),
                                 reads=[B_ps_s[si], B_const], writes=[B_E[ei]])
                            if diag >= 0:
                                qs = diag * 128
                                P.op("pool", lambda g: g.tensor_tensor(
                                    out=E[:, ei, :, qs:qs + 128], in0=E[:, ei, :, qs:qs + 128],
                                    in1=maskc[:].unsqueeze(1).broadcast_to([128, 2, 128]), op=ALU.mult),
                                    reads=[B_E[ei], B_const], writes=[B_E[ei]])
                            yield
                            for qi in range(2):
                                if diag == 1 and qi == 0:
                                    continue
                                last = (kt == nk - 1) if qi == 1 else (kt == nk - 2)
                                for c in range(2):
                                    P.op("pe", lambda t, qi=qi, c=c, last=last: t.matmul(
                                        ps_o[qi][:, c * 129:(c + 1) * 129],
                                        lhsT=E[:, ei, c, qi * 128:(qi + 1) * 128],
                                        rhs=vext[:, kt, h, :], start=(kt == 0 and c == 0), stop=last,
                                        skip_group_check=True),
                                        reads=[B_E[ei], B_v[kt]], writes=[B_ps_o[qi]])
                        steps.append((step(), h, kt == nk - 1))

                    def epi(h=h):
                        for qi in range(2):
                            epi_q(h, qi)

                    def epi_q(h, qi):
                        if True:
                            po = ps_o[qi]
                            r2, b2 = st_slot()
                            r2b, b2b = st_slot()
                            nl, bn = st_slot()
                            P.op("dve", lambda v: v.reciprocal(out=r2, in_=po[:, 128:129]), reads=[B_ps_o[qi]], writes=[b2])
                            P.op("dve", lambda v: v.reciprocal(out=r2b, in_=po[:, 257:258]), reads=[B_ps_o[qi]],
                                 writes=[b2b])
                            P.op("dve", lambda v: v.tensor_tensor(out=nl, in0=r2b, in1=NLAM, op=ALU.mult),
                                 reads=[b2b, B_const], writes=[bn])
                            P.op("dve", lambda v: v.tensor_scalar(out=epiT[:, qi, :], in0=po[:, 129:257], scalar1=nl,
                                                                  scalar2=None, op0=ALU.mult),
                                 reads=[B_ps_o[qi], bn], writes=[B_epiT[qi]])
                            P.op("dve", lambda v: v.scalar_tensor_tensor(out=epiA[:, qi, :], in0=po[:, 0:128], scalar=r2,
                                                                         in1=epiT[:, qi, :], op0=ALU.mult, op1=ALU.add),
                                 reads=[B_ps_o[qi], b2, B_epiT[qi]], writes=[B_epiA[qi]])
                            rs, br = rms_rstd(epiA[:, qi, :], junk[:, qi, :], 128, [B_epiA[qi]], B_junk[qi])
                            P.op("dve", lambda v, rs=rs: v.scalar_tensor_tensor(
                                out=mixed[:, qi, h * 128:(h + 1) * 128], in0=epiA[:, qi, :], scalar=rs, in1=dnw_b[:],
                                op0=ALU.mult, op1=ALU.mult),
                                reads=[B_epiA[qi], br, B_par], writes=[B_mixed[qi][h]])
                    epis.append(epi)
                n = len(steps)
                units.append([steps[0][0]])
                for i in range(n):
                    u = []
                    if i + 1 < n:
                        u.append(steps[i + 1][0])
                    u.append(steps[i][0])
                    u.append(steps[i][0])
                    units.append(u)
                    if steps[i][2]:
                        units.append([epis[steps[i][1]]])
                return units

            def ml_units(jb):
                units = []
                m1s = []
                m3s = []
                m4s = [[] for _ in range(TPB)]
                for f in range(8):
                    def m1(f=f):
                        s_ = f % 2

                        def ev(p_ap, p_buf):
                            P.op("act", lambda a: a.copy(out=pre[:, s_, 3:BLK + 3], in_=p_ap), reads=[p_buf],
                                 writes=[B_pre[s_]])
                        P.op("pool", lambda g: g.tensor_copy(out=pre[:, s_, 0:3], in_=halo[:, f, :]), reads=[B_halo[f]],
                             writes=[B_preh[s_]])
                        proj_fm(1536 + f * 128, ev)
                        P.op("pool", lambda g: g.tensor_copy(out=halo[:, f, :], in_=pre[:, s_, BLK:BLK + 3]),
                             reads=[B_pre[s_]], writes=[B_halo[f]])
                        yield
                        P.op("dve", lambda v: v.tensor_scalar(out=acc[:, s_, :], in0=pre[:, s_, 3:BLK + 3],
                                                              scalar1=cw[:, f, 3:4], scalar2=cb[:, f:f + 1],
                                                              op0=ALU.mult, op1=ALU.add),
                             reads=[B_pre[s_], B_par], writes=[B_acc[s_]])
                        for j in (2, 1, 0):
                            P.op("dve", lambda v, j=j: v.scalar_tensor_tensor(
                                out=acc[:, s_, :], in0=pre[:, s_, j:j + BLK], scalar=cw[:, f, j:j + 1], in1=acc[:, s_, :],
                                op0=ALU.mult, op1=ALU.add), reads=[B_pre[s_], B_preh[s_], B_acc[s_], B_par], writes=[B_acc[s_]])
                        yield
                        P.op("act", lambda a: a.activation(out=th[:, 0, 0:BLK], in_=acc[:, s_, :], func=AF.Tanh),
                             reads=[B_acc[s_]], writes=[B_th[0]])
                        yield
                        P.op("dve", lambda v: v.scalar_tensor_tensor(out=mqkT[:, f, :], in0=th[:, 0, 0:BLK], scalar=1.0,
                                                                     in1=acc[:, s_, :], op0=ALU.add, op1=ALU.mult),
                             reads=[B_th[0], B_acc[s_]], writes=[B_mqk[f]])
                    m1s.append(m1())

                def m2():
                    ii = pp_next()
                    mm_group(ps_p[ii][0:4, 0:BLK], B_ps_p[ii],
                             [(w_in[:, kc, 3584:3588], hnT[:, kc, :]) for kc in range(8)], [B_win] + B_hnT)
                    P.op("act", lambda a: a.activation(out=g_i[:], in_=ps_p[ii][0:4, 0:BLK], func=AF.Identity,
                                                       bias=gbias[:, 0:1], scale=1.0),
                         reads=[B_ps_p[ii], B_par], writes=[B_g])
                    fi = pp_next()
                    mm_group(ps_p[fi][0:4, 0:BLK], B_ps_p[fi],
                             [(w_in[:, kc, 3588:3592], hnT[:, kc, :]) for kc in range(8)], [B_win] + B_hnT)
                    P.op("act", lambda a: a.activation(out=g_f[:], in_=ps_p[fi][0:4, 0:BLK], func=AF.Exp,
                                                       bias=gbias[:, 2:3], scale=-1.0),
                         reads=[B_ps_p[fi], B_par, B_g], writes=[B_g])
                    P.op("act", lambda a: a.activation(out=g_f[:], in_=g_f[:], func=AF.Ln, bias=1.0, scale=1.0),
                         reads=[B_g], writes=[B_g])
                    yield
                    P.op("dve", lambda v: v.tensor_tensor_scan(out=g_B[:], data0=g_f[:], data1=g_f[:],
                                                               initial=g_c[:, 0:1], op0=ALU.add, op1=ALU.max),
                         reads=[B_g], writes=[B_g])
                    P.op("dve", lambda v: v.tensor_copy(out=g_c[:, 0:1], in_=g_B[:, BLK - 1:BLK]), reads=[B_g], writes=[B_g])
                    P.op("dve", lambda v: v.tensor_tensor(out=g_i[:], in0=g_i[:], in1=g_B[:], op=ALU.add),
                         reads=[B_g], writes=[B_g])
                    P.op("dve", lambda v: v.tensor_tensor_scan(out=g_G[:], data0=g_i[:], data1=g_i[:],
                                                               initial=g_c[:, 1:2], op0=ALU.max, op1=ALU.max),
                         reads=[B_g], writes=[B_g])
                    yield
                    P.op("dve", lambda v: v.tensor_scalar(out=g_c[:, 2:3], in0=g_G[:, BLK - 1:BLK], scalar1=-1.0,
                                                          scalar2=None, op0=ALU.mult), reads=[B_g], writes=[B_g])
                    P.op("dve", lambda v: v.tensor_scalar(out=g_c[:, 4:5], in0=g_c[:, 2:3], scalar1=float(-LN_SQRT_D),
                                                          scalar2=None, op0=ALU.add), reads=[B_g], writes=[B_g])
                    P.op("act", lambda a: a.activation(out=g_c[:, 3:4], in_=g_c[:, 1:2], func=AF.Exp, bias=g_c[:, 2:3],
                                                       scale=1.0), reads=[B_g], writes=[B_g])
                    P.op("dve", lambda v: v.tensor_copy(out=g_c[:, 1:2], in_=g_G[:, BLK - 1:BLK]), reads=[B_g], writes=[B_g])
                    P.op("act", lambda a: a.activation(out=g_e[:, 0, :], in_=g_i[:], func=AF.Exp, bias=g_c[:, 4:5],
                                                       scale=1.0), reads=[B_g], writes=[B_g])
                    P.op("act", lambda a: a.activation(out=g_e[:, 1, :], in_=g_B[:], func=AF.Exp, bias=g_c[:, 2:3],
                                                       scale=1.0), reads=[B_g], writes=[B_g])
                    yield
                    for ti in range(TPB):
                        for w_ in range(2):
                            P.op("pe", lambda t, ti=ti, w_=w_: t.transpose(
                                out=PM_X[:, w_ * 4:(w_ + 1) * 4], in_=g_e[:, w_, ti * 128:(ti + 1) * 128],
                                identity=identf[0:4, 0:4]), reads=[B_g, B_const], writes=[B_psm_x])
                        P.op("dve", lambda v, ti=ti: v.tensor_copy(out=etok[:, ti, :], in_=PM_X[:, 0:8]),
                             reads=[B_psm_x], writes=[B_etok])
                    yield
                    P.op("dve", lambda v: v.tensor_scalar(out=g_d[:], in0=identf[0:4, 0:4], scalar1=g_c[:, 3:4],
                                                          scalar2=None, op0=ALU.mult), reads=[B_g, B_const], writes=[B_g])
                    P.op("pe", lambda t: t.matmul(PM_X[:, 8:12], lhsT=ones4[:], rhs=g_d[:], start=True, stop=True),
                         reads=[B_g], writes=[B_psm_x])
                    P.op("dve", lambda v: v.tensor_copy(out=sbc[:], in_=PM_X[:, 8:12]), reads=[B_psm_x], writes=[B_sbc])
                    for hh in range(4):
                        P.op("dve", lambda v, hh=hh: v.tensor_scalar(out=Cf[:, hh, :], in0=Cf[:, hh, :],
                                                                    scalar1=sbc[:, hh:hh + 1], scalar2=None,
                                                                    op0=ALU.mult), reads=[B_sbc, B_Cf[hh]], writes=[B_Cf[hh]])
                    P.op("pool", lambda g: g.tensor_copy(out=CbA[:], in_=Cf[:]), reads=B_Cf, writes=B_CbA)
                m2g = m2()

                for ti in range(TPB):
                    def m3(ti=ti):
                        sl = ti % 2

                        def ev_v(p_ap, p_buf):
                            for hh in range(4):
                                P.op("dve", lambda v, hh=hh: v.tensor_scalar(
                                    out=ve[:, sl, hh, 0:128], in0=p_ap[:, hh * 128:(hh + 1) * 128],
                                    scalar1=etok[:, ti, hh:hh + 1], scalar2=None, op0=ALU.mult),
                                    reads=[p_buf, B_etok], writes=[B_ve[sl]])
                                P.op("dve", lambda v, hh=hh: v.tensor_copy(out=ve[:, sl, hh, 128:129],
                                                                          in_=etok[:, ti, hh:hh + 1]),
                                     reads=[B_etok], writes=[B_ve[sl]])
                        proj_tm(ti, 2560, 512, ev_v)

                        yield

                        def ev_o(p_ap, p_buf):
                            P.op("act", lambda a: a.activation(out=gate[:, sl, :], in_=p_ap, func=AF.Tanh, scale=0.5),
                                 reads=[p_buf], writes=[B_gate[sl]])
                            P.op("dve", lambda v: v.scalar_tensor_tensor(out=gate[:, sl, :], in0=gate[:, sl, :], scalar=1.0,
                                                                         in1=nwb_ml[:], op0=ALU.add, op1=ALU.mult),
                                 reads=[B_gate[sl], B_par], writes=[B_gate[sl]])
                        proj_tm(ti, 3072, 512, ev_o)
                        yield
                        for h in range(4):
                            P.op("pe", lambda t, h=h: t.transpose(out=psT[:, h * 128:(h + 1) * 128],
                                                                  in_=mqkT[:, 4 + h, ti * 128:(ti + 1) * 128],
                                                                  identity=ident[:]),
                                 reads=[B_mqk[4 + h], B_const], writes=[B_psT])
                        P.op("act", lambda a: a.copy(out=mk[:, sl, :, :], in_=psT[:, 0:512].rearrange("p (h d) -> p h d", h=4)),
                             reads=[B_psT], writes=[B_mk[sl]])
                    m3s.append(m3())
                    for h in range(4):
                        def m4(ti=ti, h=h):
                            sl = ti % 2
                            si = h
                            ji = h % 2
                            tok = slice(ti * 128, (ti + 1) * 128)
                            bank, B_bank = [(ps_m, B_psm_sc), (psT_f, B_psT), (ps_p[0], B_ps_p[0]), (ps_p[1], B_ps_p[1])][h]
                            SC, NUM, DC = bank[:, 0:128], bank[:, 128:257], bank[:, 257:386]
                            P.op("pe", lambda t: t.matmul(SC, lhsT=mqkT[:, 4 + h, tok], rhs=mqkT[:, h, tok],
                                                          start=True, stop=True),
                                 reads=[B_mqk[4 + h], B_mqk[h]], writes=[B_bank])
                            yield
                            P.op("dve", lambda v: v.tensor_tensor(out=sm[:, si, :], in0=SC, in1=mask2[:], op=ALU.mult),
                                 reads=[B_bank, B_const], writes=[B_sm[si]])
                            yield
                            P.op("pe", lambda t: t.matmul(NUM, lhsT=sm[:, si, :], rhs=ve[:, sl, h, :], start=True,
                                                          stop=False),
                                 reads=[B_sm[si], B_ve[sl]], writes=[B_bank])
                            P.op("pe", lambda t: t.matmul(NUM, lhsT=mqkT[:, h, tok], rhs=CbA[:, h, :], start=False,
                                                          stop=True),
                                 reads=[B_mqk[h], B_CbA[h]], writes=[B_bank])
                            P.op("pe", lambda t: t.matmul(DC, lhsT=mk[:, sl, h, :], rhs=ve[:, sl, h, :], start=True,
                                                          stop=True, skip_group_check=True),
                                 reads=[B_mk[sl], B_ve[sl]], writes=[B_bank])
                            yield
                            P.op("dve", lambda v: v.tensor_tensor(out=Cf[:, h, :], in0=Cf[:, h, :], in1=DC, op=ALU.add),
                                 reads=[B_Cf[h], B_bank], writes=[B_Cf[h]])
                            P.op("pool", lambda g: g.tensor_copy(out=CbA[:, h, :], in_=Cf[:, h, :]), reads=[B_Cf[h]],
                                 writes=[B_CbA[h]])
                            dn, bd = st_slot()
                            r_, brr = st_slot()
                            t3, b3 = st_slot()
                            P.op("dve", lambda v: v.tensor_scalar(out=t3, in0=bank[:, 256:257], scalar1=-1.0, scalar2=None,
                                                                  op0=ALU.mult), reads=[B_bank], writes=[b3])
                            P.op("dve", lambda v: v.scalar_tensor_tensor(out=dn, in0=bank[:, 256:257], scalar=1.0, in1=t3,
                                                                         op0=ALU.mult, op1=ALU.max),
                                 reads=[B_bank, b3], writes=[bd])
                            P.op("dve", lambda v: v.tensor_scalar(out=dn, in0=dn, scalar1=etok[:, ti, 4 + h:5 + h],
                                                                  scalar2=None, op0=ALU.max),
                                 reads=[bd, B_etok], writes=[bd])
                            P.op("dve", lambda v: v.reciprocal(out=r_, in_=dn), reads=[bd], writes=[brr])
                            P.op("dve", lambda v: v.tensor_scalar(out=mlT[:, si, :], in0=bank[:, 128:256], scalar1=r_,
                                                                  scalar2=None, op0=ALU.mult),
                                 reads=[B_bank, brr], writes=[B_mlT[si]])
                            yield
                            rs, br = rms_rstd(mlT[:, si, :], junk[:, ji, :], 128, [B_mlT[si]], B_junk[ji])
                            yield
                            P.op("dve", lambda v: v.scalar_tensor_tensor(
                                out=mixed[:, ti, 512 + h * 128:512 + (h + 1) * 128], in0=mlT[:, si, :], scalar=rs,
                                in1=gate[:, sl, h * 128:(h + 1) * 128], op0=ALU.mult, op1=ALU.mult),
                                reads=[B_mlT[si], br, B_gate[sl]], writes=[B_mixed[ti][4 + h]])
                        m4s[ti].append(m4())
                order = [4, 5, 6, 7, 0, 1, 2, 3]
                NS1 = 4
                extra = []
                for ti in range(TPB):
                    extra += [m3s[ti], m3s[ti]]
                seq1 = []
                for t in range(len(m1s) + NS1 - 1):
                    for st_ in reversed(range(NS1)):
                        u = t - st_
                        if 0 <= u < len(m1s):
                            seq1.append(m1s[order[u]])
                units += interleave(seq1, extra)
                for ti in range(TPB):
                    units.append(m3s[ti])
                for ti in range(TPB):
                    for _ in range(6):
                        units += list(m4s[ti])
                units = [m2g] + units
                return units

            def out_proj(jb):
                for ti in range(TPB):
                    gt = jb * TPB + ti
                    transposes_to(hnT[:, :, ti * 128:(ti + 1) * 128], [B_hnT[ti]],
                                  lambda kc, ti=ti: mixed[:, ti, kc * 128:(kc + 1) * 128], B_mixed[ti], evac="act")
                    for hf in range(2):
                        i = pp_next()
                        xi = (ti * 2 + hf) % 2
                        mm_group(ps_p[i][:, :], B_ps_p[i],
                                 [(hnT[:, kc, ti * 128:(ti + 1) * 128], w_out[:, kc, hf * 512:(hf + 1) * 512])
                                  for kc in range(8)], [B_wout, B_hnT[ti]])
                        P.op("dve", lambda v, i=i, xi=xi, ti=ti, hf=hf: v.tensor_tensor(
                            out=x1t[:, xi, :], in0=ps_p[i][:, :], in1=xt[:, ti, hf * 512:(hf + 1) * 512], op=ALU.add),
                            reads=[B_ps_p[i], B_xt[ti]], writes=[B_x1t[xi]])
                        P.op("sp", lambda s, xi=xi, gt=gt, hf=hf: s.dma_start(
                            out=x1_d[gt * 128:(gt + 1) * 128, hf * 512:(hf + 1) * 512], in_=x1t[:, xi, :]),
                            reads=[B_x1t[xi]], dma=True)

            nblk = DEBUG.get("nblk", NB)
            stage = DEBUG.get("stage", 99)
            for jb in range(nblk):
                if stage >= 1:
                    A1(jb)
                def adv(g):
                    if isinstance(g, list):
                        for x in g:
                            adv(x)
                    elif callable(g):
                        g()
                    else:
                        try:
                            next(g)
                        except StopIteration:
                            pass
                au, mu = attn_units(jb), ml_units(jb)
                m2g_, mu = mu[0], mu[1:]
                for u in interleave([A2_qkv(jb)] * 10, [m2g_] * 5):
                    adv(u)
                if stage >= 5:
                    for u in interleave(au, mu):
                        adv(u)
                    for u in au + mu:
                        for g in (u if isinstance(u, list) else [u]):
                            if not callable(g):
                                for _ in g:
                                    pass
                if stage >= 6:
                    out_proj(jb)

            if "dbg" in DEBUG:
                dbg_d = dt("dbg", [128, TPB * D], BF16, kind="ExternalOutput").ap()
                P.op("sp", lambda s: s.dma_start(out=dbg_d, in_=mixed[:].rearrange("p t d -> p (t d)")),
                     reads=[b for bb in B_mixed for b in bb], dma=True)

            P.wait_all("sp", [("d", k, 16 * P.dma_cnt[k]) for k in range(NDMA) if P.dma_cnt[k] > 0])
            for e in P.ENGS:
                last = {e2: len(P.ops[e2]) - 1 for e2 in P.ENGS if e2 != e and len(P.ops[e2]) > 0}
                toks = []
                for e2, s2 in last.items():
                    while s2 >= 0 and (P.ops[e2][s2].fn is None or P.ops[e2][s2].dma is not None):
                        s2 -= 1
                    if s2 >= 0:
                        toks.append(("e", e2, s2))
                toks += [("d", k, 16 * P.dma_cnt[k]) for k in range(NDMA) if P.dma_cnt[k] > 0]
                P.wait_all(e, toks)
            P.emit()

        if DEBUG.get("phaseA_only"):
            return nc
        with ExitStack() as eb:
            w_up = sb("w_up_sb", [128, 8, 2 * DFF], BF16, eb)
            w_dn = sb("w_dn_sb", [128, NF, D], BF16, eb)
            x1s = sb("x1s", [128, TPB, D], F32, eb)
            xs2 = sb("xs2", [128, TPB, D], BF16, eb)
            hn2 = sb("hn2", [128, 2, 8, BLK], BF16, eb)
            yT = sb("yT", [128, NF, BLK], BF16, eb)
            pbuf = sb("pbuf", [128, 4, BLK + 2], F32, eb)
            acc2 = sb("acc2", [128, 8, BLK], F32, eb)
            sg = sb("sg", [128, 2, BLK], F32, eb)
            x2 = sb("x2", [128, 2, D], F32, eb)
            junk2 = sb("junk2", [128, D], BF16, eb)
            halo2 = sb("halo2", [128, 44, 2], F32, eb)
            fcw = sb("fcw_sb", [128, 44, 3], F32, eb)
            fcb = sb("fcb_sb", [128, 44], F32, eb)
            st2 = sb("st2", [128, 64], F32, eb)
            nrmb2 = sb("nrmb2", [128, 2 * D], F32, eb)
            psT2 = ps("psT2", [128, 1024], BF16, eb)
            ps_u = [ps(f"ps_u{i}", [128, 512], F32, eb) for i in range(4)]
            ps_d = [ps(f"ps_d{i}", [128, 512], F32, eb) for i in range(2)]

            B_wup = Buf("wup"); B_wdn = Buf("wdn"); B_par2 = Buf("par2")
            B_x1s = [Buf(f"x1s{i}") for i in range(TPB)]
            B_xs2 = [Buf(f"xs2{i}") for i in range(TPB)]
            B_hn2 = [[Buf(f"hn2{b}_{i}") for i in range(TPB)] for b in range(2)]
            B_yT = [Buf(f"yT{f}") for f in range(NF)]
            B_pbuf = [Buf(f"pbuf{i}") for i in range(4)]
            B_acc2 = [Buf(f"acc2{i}") for i in range(8)]
            B_sg = [Buf(f"sg{i}") for i in range(2)]
            B_x2 = [Buf(f"x2{i}") for i in range(2)]
            B_junk2 = Buf("junk2")
            B_halo2 = [Buf(f"halo2{i}") for i in range(44)]
            B_st2 = [Buf(f"st2{i}") for i in range(64)]
            B_psT2 = Buf("psT2")
            B_ps_u = [Buf(f"ps_u{i}") for i in range(4)]
            B_ps_d = [Buf(f"ps_d{i}") for i in range(2)]
            stb = {"c": 0, "u": 0, "d": 0}

            def st2_slot():
                i = stb["c"] % 64
                stb["c"] += 1
                return st2[:, i:i + 1], B_st2[i]

            for kc in range(8):
                for q in range(4):
                    c0 = q * 1408
                    P.op("pool", lambda g, kc=kc, c0=c0: g.dma_start(out=w_up[:, kc, c0:c0 + 1408],
                                                                   in_=w_up_d[kc * 128:(kc + 1) * 128, c0:c0 + 1408]),
                         writes=[B_wup], dma=True)
            for kc in range(NF):
                P.op("pool", lambda g, kc=kc: g.dma_start(out=w_dn[:, kc, :], in_=w_down_d[kc * 128:(kc + 1) * 128, :]),
                     writes=[B_wdn], dma=True)
            P.op("sp", lambda s: s.dma_start(out=fcw[:], in_=fcw_d.rearrange("p (f j) -> p f j", j=3)), writes=[B_par2],
                 dma=True)
            P.op("sp", lambda s: s.dma_start(out=fcb[:], in_=fcb_d), writes=[B_par2], dma=True)
            P.op("sp", lambda s: s.dma_start(out=nrmb2[:, 0:D], in_=nrm_d[1, :].partition_broadcast(128)), writes=[B_par2], dma=True)
            P.op("sp", lambda s: s.dma_start(out=nrmb2[:, D:2 * D], in_=nrm_d[2, :].partition_broadcast(128)), writes=[B_par2], dma=True)
            P.op("pool", lambda g: g.memset(halo2[:].rearrange("p a b -> p (a b)"), 0.0), writes=B_halo2)

            def rms_rstd2(src_ap, junk_ap, n, reads, junk_buf):
                ssq, bs = st2_slot()
                rs, br = st2_slot()
                P.op("act", lambda a: a.activation(out=junk_ap, in_=src_ap, func=AF.Square, accum_out=ssq),
                     reads=reads, writes=[junk_buf, bs])
                P.op("pool", lambda g: g.tensor_scalar(out=ssq, in0=ssq, scalar1=1.0 / n, scalar2=EPS, op0=ALU.mult,
                                                       op1=ALU.add), reads=[bs], writes=[bs])
                P.op("pool", lambda g: g.tensor_tensor(out=rs, in0=ssq, in1=NEGH, op=ALU.pow), reads=[bs, B_const],
                     writes=[br])
                return rs, br

            def B1(jb):
                hb = jb % 2
                for ti in range(TPB):
                    gt = jb * TPB + ti
                    P.op("sp", lambda s, ti=ti, gt=gt: s.dma_start(out=x1s[:, ti, :], in_=x1_d[gt * 128:(gt + 1) * 128, :]),
                         writes=[B_x1s[ti]], dma=True)
                    rs, br = rms_rstd2(x1s[:, ti, :], xs2[:, ti, :], D, [B_x1s[ti]], B_xs2[ti])
                    yield
                    P.op("dve", lambda v, ti=ti, rs=rs: v.scalar_tensor_tensor(
                        out=xs2[:, ti, :], in0=x1s[:, ti, :], scalar=rs, in1=nrmb2[:, 0:D], op0=ALU.mult, op1=ALU.mult),
                        reads=[B_x1s[ti], br, B_par2], writes=[B_xs2[ti]])
                    yield
                    for kc in range(8):
                        P.op("pe", lambda t, kc=kc, ti=ti: t.transpose(out=psT2[:, kc * 128:(kc + 1) * 128],
                                                                       in_=xs2[:, ti, kc * 128:(kc + 1) * 128],
                                                                       identity=ident[:]),
                             reads=[B_xs2[ti], B_const], writes=[B_psT2])
                    P.op("act", lambda a, ti=ti: a.copy(out=hn2[:, hb, :, ti * 128:(ti + 1) * 128],
                                                        in_=psT2[:].rearrange("p (k t) -> p k t", k=8)),
                         reads=[B_psT2], writes=[B_hn2[hb][ti]])
                    yield

            def B2f(jb, f):
                hb = jb % 2
                slots = []
                for part in range(2):
                    tidx = part * NF + f
                    col0 = part * DFF + f * 128
                    ui = stb["u"] % 4
                    ai = stb["u"] % 8
                    stb["u"] += 1
                    slots.append((ui, ai, tidx))
                    for kc in range(8):
                        P.op("pe", lambda t, kc=kc, ui=ui, col0=col0: t.matmul(
                            ps_u[ui][:, 0:BLK], lhsT=w_up[:, kc, col0:col0 + 128], rhs=hn2[:, hb, kc, :],
                            start=(kc == 0), stop=(kc == 7)), reads=[B_wup] + B_hn2[hb], writes=[B_ps_u[ui]])
                    P.op("pool", lambda g, ui=ui, tidx=tidx: g.tensor_copy(out=pbuf[:, ui, 0:2], in_=halo2[:, tidx, :]),
                         reads=[B_halo2[tidx]], writes=[B_pbuf[ui]])
                    P.op("act", lambda a, ui=ui: a.copy(out=pbuf[:, ui, 2:BLK + 2], in_=ps_u[ui][:, 0:BLK]),
                         reads=[B_ps_u[ui]], writes=[B_pbuf[ui]])
                    P.op("act", lambda a, ui=ui, ai=ai, tidx=tidx: a.activation(
                        out=acc2[:, ai, :], in_=ps_u[ui][:, 0:BLK], func=AF.Identity, bias=fcb[:, tidx:tidx + 1],
                        scale=fcw[:, tidx, 2:3]), reads=[B_ps_u[ui], B_par2], writes=[B_acc2[ai]])
                    P.op("pool", lambda g, ui=ui, tidx=tidx: g.tensor_copy(out=halo2[:, tidx, :],
                                                                           in_=pbuf[:, ui, BLK:BLK + 2]),
                         reads=[B_pbuf[ui]], writes=[B_halo2[tidx]])
                yield
                for (ui, ai, tidx) in slots:
                    for j in (1, 0):
                        P.op("dve", lambda v, ui=ui, ai=ai, tidx=tidx, j=j: v.scalar_tensor_tensor(
                            out=acc2[:, ai, :], in0=pbuf[:, ui, j:j + BLK], scalar=fcw[:, tidx, j:j + 1],
                            in1=acc2[:, ai, :], op0=ALU.mult, op1=ALU.add),
                            reads=[B_pbuf[ui], B_acc2[ai], B_par2], writes=[B_acc2[ai]])
                yield
                ag, au = slots[0][1], slots[1][1]
                gi = f % 2
                P.op("act", lambda a: a.activation(out=sg[:, gi, :], in_=acc2[:, ag, :], func=AF.Silu),
                     reads=[B_acc2[ag]], writes=[B_sg[gi]])
                yield
                P.op("dve", lambda v: v.tensor_tensor(out=yT[:, f, :], in0=sg[:, gi, :], in1=acc2[:, au, :], op=ALU.mult),
                     reads=[B_sg[gi], B_acc2[au]], writes=[B_yT[f]])

            def B3(jb):
                for ti in range(TPB):
                    gt = jb * TPB + ti
                    xi = gt % 2
                    P.op("sp", lambda s, xi=xi, gt=gt: s.dma_start(out=x2[:, xi, :], in_=x1_d[gt * 128:(gt + 1) * 128, :]),
                         writes=[B_x2[xi]], dma=True)
                    for hf in range(2):
                        di = stb["d"] % 2
                        stb["d"] += 1
                        for kc in range(NF):
                            P.op("pe", lambda t, kc=kc, di=di, ti=ti, hf=hf: t.matmul(
                                ps_d[di][:, :], lhsT=yT[:, kc, ti * 128:(ti + 1) * 128],
                                rhs=w_dn[:, kc, hf * 512:(hf + 1) * 512], start=(kc == 0), stop=(kc == NF - 1)),
                                reads=[B_wdn, B_yT[kc]], writes=[B_ps_d[di]])
                        P.op("dve", lambda v, di=di, xi=xi, hf=hf: v.tensor_tensor(
                            out=x2[:, xi, hf * 512:(hf + 1) * 512], in0=ps_d[di][:, :],
                            in1=x2[:, xi, hf * 512:(hf + 1) * 512], op=ALU.add),
                            reads=[B_ps_d[di], B_x2[xi]], writes=[B_x2[xi]])
                    rs, br = rms_rstd2(x2[:, xi, :], junk2[:], D, [B_x2[xi]], B_junk2)
                    P.op("dve", lambda v, xi=xi, rs=rs: v.scalar_tensor_tensor(
                        out=x2[:, xi, :], in0=x2[:, xi, :], scalar=rs, in1=nrmb2[:, D:2 * D], op0=ALU.mult,
                        op1=ALU.mult), reads=[B_x2[xi], br, B_par2], writes=[B_x2[xi]])
                    P.op("sp", lambda s, xi=xi, gt=gt: s.dma_start(out=out_d[gt * 128:(gt + 1) * 128, :], in_=x2[:, xi, :]),
                         reads=[B_x2[xi]], dma=True)

            def adv2(g):
                try:
                    next(g)
                except StopIteration:
                    pass

            g0 = B1(0)
            for _ in g0:
                pass
            for jb in range(nblk):
                fg = [B2f(jb, f) for f in range(NF)]
                A = []
                NS2 = 4
                for t in range(NF + NS2 - 1):
                    for st_ in reversed(range(NS2)):
                        u = t - st_
                        if 0 <= u < NF:
                            A.append(fg[u])
                M = []
                if jb + 1 < nblk:
                    gn = B1(jb + 1)
                    M = [gn] * (3 * TPB)
                for g in interleave(A, M):
                    adv2(g)
                for g in fg + (M[:1]):
                    for _ in g:
                        pass
                B3(jb)
            P.wait_all("sp", [("d", k, 16 * P.dma_cnt[k]) for k in range(NDMA) if P.dma_cnt[k] > 0])
            P.emit()
    return nc


def _prep_inputs(inputs):
    f = lambda a: np.ascontiguousarray(np.asarray(a, dtype=np.float32))
    x = f(inputs["x"])
    shared = {
        "w_in": f(inputs["w_in"][0]),
        "w_out": f(inputs["w_out"][0]),
        "w_up": f(inputs["w_up"][0]),
        "w_down": f(inputs["w_down"][0]),
        "nrm": f(np.stack([np.asarray(inputs["attn_norm_w"][0]), np.asarray(inputs["ffn_norm_w"][0]),
                           np.asarray(inputs["final_norm_w"])], axis=0)),
        "mlnw": f(np.asarray(inputs["mlstm_norm_w"]).reshape(1, 512)),
        "dnw": f(np.asarray(inputs["diff_norm_w"]).reshape(1, 128)),
        "lamv": f(np.concatenate([np.asarray(inputs[k]).reshape(-1) for k in
                                  ("lambda_q1", "lambda_k1", "lambda_q2", "lambda_k2")]).reshape(1, 256)),
        "mcw": f(np.asarray(inputs["mlstm_conv_w"][0]).reshape(4, 8, 128).transpose(2, 1, 0).reshape(128, 32)),
        "mcb": f(np.asarray(inputs["mlstm_conv_b"][0]).reshape(8, 128).T),
        "gb": f(np.stack([np.asarray(inputs["mlstm_igate_b"][0]), np.asarray(inputs["mlstm_fgate_b"][0])], axis=1)),
        "fcw": f(np.asarray(inputs["ffn_conv_w"][0]).reshape(3, 44, 128).transpose(2, 1, 0).reshape(128, 132)),
        "fcb": f(np.asarray(inputs["ffn_conv_b"][0]).reshape(44, 128).T),
    }
    return x, shared


def kernel(**inputs):
    x, shared = _prep_inputs(inputs)
    nc = build_program()
    in_maps = [dict(shared, x=np.ascontiguousarray(x[b])) for b in range(8)]
    res = run_bass_kernel_spmd(nc, in_maps, core_ids=list(range(8)))
    out = np.stack([np.asarray(r["out"], dtype=np.float32) for r in res.results], axis=0)
    return out
```
